# Optimizing a Trainium2 kernel written in Bass

```python
import math
import jax, jax.numpy as jnp
from jax import lax
import numpy as np

D_MODEL = 2048
BATCH = 4
SEQ = 2048
DEPTH = 4

GRID_W = 64
CTX_LEN = 256
N_MIXERS = 4
CHUNK = 64
CONV_W = 3
D_FF = 4 * D_MODEL
NORM_EPS = 1e-6

SSD_D_INNER = 2 * D_MODEL
SSD_HEAD_DIM = 64
SSD_HEADS = SSD_D_INNER // SSD_HEAD_DIM
SSD_GROUPS = 8
SSD_STATE = 128
SSD_CONV_CH = SSD_D_INNER + 2 * SSD_GROUPS * SSD_STATE
SSD_IN = SSD_D_INNER + SSD_CONV_CH + 2 * SSD_HEADS

RET_HEADS = 8
RET_QK_DIM = D_MODEL // RET_HEADS
RET_V_DIM = 2 * RET_QK_DIM
RET_DV = RET_HEADS * RET_V_DIM
RET_IN = 2 * D_MODEL + 2 * RET_DV
ROPE_BASE = 10000.0

HGRN_EXPAND = 128
HGRN_HEADS = D_MODEL // HGRN_EXPAND
HGRN_IN = 5 * D_MODEL

GDN_HEAD_DIM = 128
GDN_K_HEADS = D_MODEL // GDN_HEAD_DIM
GDN_V_HEADS = 2 * GDN_K_HEADS
GDN_DK = GDN_K_HEADS * GDN_HEAD_DIM
GDN_DV = GDN_V_HEADS * GDN_HEAD_DIM
GDN_CONV_CH = 2 * GDN_DK + GDN_DV
GDN_IN = GDN_CONV_CH + GDN_DV + 4 * GDN_V_HEADS

kernel_name = 'hybrid_bidir_recurrent_dit_block'


def _rmsnorm(x, g, eps=NORM_EPS):
    x32 = x.astype(jnp.float32)
    y = x32 * lax.rsqrt(jnp.mean(x32 * x32, axis=-1, keepdims=True) + eps)
    return y.astype(x.dtype) * g


def _adaln(x, g, shift, scale):
    return _rmsnorm(x, g) * (1 + scale) + shift


def _l2norm(x, eps=1e-6):
    x32 = x.astype(jnp.float32)
    return x32 * lax.rsqrt(jnp.sum(x32 * x32, axis=-1, keepdims=True) + eps)


def _sq_relu_mlp(h, w1, w2):
    return jnp.square(jax.nn.relu(h @ w1)) @ w2


def _dwconv(u, w):
    return lax.conv_general_dilated(u, w[:, None, :], window_strides=(1,),
                                    padding=[(CONV_W // 2, CONV_W // 2)],
                                    dimension_numbers=('NWC', 'WIO', 'NWC'),
                                    feature_group_count=u.shape[-1])


def _conv_split(u, w, lc):
    return jnp.concatenate([_dwconv(u[:, :lc], w), _dwconv(u[:, lc:], w)], axis=1)


def _rev(t, lc):
    return jnp.concatenate([jnp.flip(t[:, :lc], 1), jnp.flip(t[:, lc:], 1)], axis=1)


def _split_out(out, lc, keep_ctx):
    if keep_ctx:
        return out[:, :lc], out[:, lc:]
    return None, out


def _to_chunks(t):
    b, n = t.shape[:2]
    return jnp.moveaxis(t.reshape((b, n // CHUNK, CHUNK) + t.shape[2:]), 1, 0)


def _from_chunks(t):
    nc, b, q = t.shape[:3]
    return jnp.moveaxis(t, 0, 1).reshape((b, nc * q) + t.shape[3:])


def _chunk_masks():
    idx = jnp.arange(CHUNK)
    return idx[:, None] >= idx[None, :], idx[:, None] > idx[None, :]


def _scalar_decay_scan(q, k, v, log_a):
    f32 = jnp.float32
    q, k, v, log_a = (t.astype(f32) for t in (q, k, v, log_a))
    bsz, _, g, n = q.shape
    r, p = v.shape[-2:]
    incl, _ = _chunk_masks()

    def body(s, xs):
        qc, kc, vc, la = xs
        cum = jnp.cumsum(la, axis=1)
        cum_t = jnp.moveaxis(cum, 1, -1)
        seg = cum_t[..., :, None] - cum_t[..., None, :]
        scores = jnp.einsum('btgn,bsgn->bgts', qc, kc)
        attn = scores[:, :, None] * jnp.exp(jnp.where(incl, seg, -jnp.inf))
        y = jnp.einsum('bgrts,bsgrp->btgrp', attn, vc)
        y = y + jnp.einsum('btgn,bgrnp->btgrp', qc, s) * jnp.exp(cum)[..., None]
        to_end = jnp.exp(cum[:, -1:] - cum)
        s = jnp.exp(cum[:, -1])[..., None, None] * s + jnp.einsum('bsgn,bsgr,bsgrp->bgrnp', kc, to_end, vc)
        return s, y

    s0 = jnp.zeros((bsz, g, r, n, p), f32)
    _, y = lax.scan(body, s0, tuple(_to_chunks(t) for t in (q, k, v, log_a)))
    return _from_chunks(y)


def _vector_decay_scan(q, k, v, log_f):
    f32 = jnp.float32
    q, k, v, log_f = (t.astype(f32) for t in (q, k, v, log_f))
    bsz, _, h, kd = q.shape
    vd = v.shape[-1]
    incl, _ = _chunk_masks()

    def body(s, xs):
        qc, kc, vc, lf = xs
        cum = jnp.cumsum(lf, axis=1)
        seg = cum[:, :, None] - cum[:, None, :]
        decay = jnp.exp(jnp.where(incl[:, :, None, None], seg, -jnp.inf))
        attn = jnp.einsum('bthk,bshk,btshk->bhts', qc, kc, decay)
        y = jnp.einsum('bhts,bshv->bthv', attn, vc)
        y = y + jnp.einsum('bthk,bhkv->bthv', qc * jnp.exp(cum), s)
        s = jnp.exp(cum[:, -1])[..., None] * s + jnp.einsum('bshk,bshv->bhkv', kc * jnp.exp(cum[:, -1:] - cum), vc)
        return s, y

    s0 = jnp.zeros((bsz, h, kd, vd), f32)
    _, y = lax.scan(body, s0, tuple(_to_chunks(t) for t in (q, k, v, log_f)))
    return _from_chunks(y)


def _delta_scan(q, k, v, beta, log_a):
    f32 = jnp.float32
    q, k, v, beta, log_a = (t.astype(f32) for t in (q, k, v, beta, log_a))
    bsz, _, g, kd = q.shape
    r, vd = v.shape[-2:]
    incl, strict = _chunk_masks()

    def body(s, xs):
        qc, kc, vc, bc, la = xs
        cum = jnp.cumsum(la, axis=1)
        cum_t = jnp.moveaxis(cum, 1, -1)
        seg = cum_t[..., :, None] - cum_t[..., None, :]
        beta_t = jnp.moveaxis(bc, 1, -1)
        kk = jnp.einsum('btgk,bsgk->bgts', kc, kc)
        lower = beta_t[..., :, None] * kk[:, :, None] * jnp.exp(jnp.where(strict, seg, -jnp.inf))
        rhs_v = jnp.moveaxis(vc * bc[..., None], 1, 3)
        rhs_k = jnp.moveaxis(kc[:, :, :, None, :] * (bc * jnp.exp(cum))[..., None], 1, 3)
        sol = lax.linalg.triangular_solve(lower, jnp.concatenate([rhs_v, rhs_k], axis=-1),
                                          left_side=True, lower=True, unit_diagonal=True)
        u, w = sol[..., :vd], sol[..., vd:]
        v_new = u - jnp.einsum('bgrtk,bgrkv->bgrtv', w, s)
        qk = jnp.einsum('btgk,bsgk->bgts', qc, kc)
        attn = qk[:, :, None] * jnp.exp(jnp.where(incl, seg, -jnp.inf))
        y = jnp.einsum('bgrts,bgrsv->btgrv', attn, v_new)
        y = y + jnp.einsum('btgk,bgrkv->btgrv', qc, s) * jnp.exp(cum)[..., None]
        to_end = jnp.exp(cum_t[..., -1:] - cum_t)
        s = jnp.exp(cum_t[..., -1])[..., None, None] * s + jnp.einsum('bsgk,bgrs,bgrsv->bgrkv', kc, to_end, v_new)
        return s, y

    s0 = jnp.zeros((bsz, g, r, kd, vd), f32)
    _, y = lax.scan(body, s0, tuple(_to_chunks(t) for t in (q, k, v, beta, log_a)))
    return _from_chunks(y)


def _rope_2d(t, rows):
    f32 = jnp.float32
    pos = jnp.arange(rows * GRID_W)
    row = (pos // GRID_W).astype(f32)
    col = (pos % GRID_W).astype(f32)
    half = t.shape[-1] // 2
    inv_freq = ROPE_BASE ** (-jnp.arange(0, half, 2, dtype=f32) / half)

    def rot(u, p):
        ang = p[:, None] * inv_freq
        cos, sin = jnp.cos(ang)[:, None, :], jnp.sin(ang)[:, None, :]
        u1, u2 = jnp.split(u, 2, axis=-1)
        return jnp.concatenate([u1 * cos - u2 * sin, u2 * cos + u1 * sin], axis=-1)

    t32 = t.astype(f32)
    return jnp.concatenate([rot(t32[..., :half], row), rot(t32[..., half:], col)], axis=-1).astype(t.dtype)


def _ssd_mixer(h_ctx, h_lat, w_in, conv_w, conv_b, dt_bias, a_log, d_skip, norm_g, w_out, keep_ctx):
    f32 = jnp.float32
    lc = h_ctx.shape[1]
    u = jnp.concatenate([h_ctx, h_lat], axis=1) @ w_in
    bsz, t = u.shape[:2]
    z, xbc, dt = jnp.split(u, [SSD_D_INNER, SSD_D_INNER + SSD_CONV_CH], axis=-1)
    xbc = jax.nn.silu(_conv_split(xbc, conv_w, lc) + conv_b)
    xs, bm, cm = jnp.split(xbc, [SSD_D_INNER, SSD_D_INNER + SSD_GROUPS * SSD_STATE], axis=-1)
    r = SSD_HEADS // SSD_GROUPS
    xs = xs.reshape(bsz, t, SSD_GROUPS, r, SSD_HEAD_DIM)
    bm = bm.reshape(bsz, t, SSD_GROUPS, SSD_STATE)
    cm = cm.reshape(bsz, t, SSD_GROUPS, SSD_STATE)
    dt = jax.nn.softplus(dt.reshape(bsz, t, 2, SSD_HEADS).astype(f32) + dt_bias.astype(f32))
    log_a = -jnp.exp(a_log.astype(f32)) * dt
    grp = lambda a: a.reshape(bsz, t, SSD_GROUPS, r)
    y = _scalar_decay_scan(cm, bm, xs * grp(dt[:, :, 0])[..., None], grp(log_a[:, :, 0]))
    y = y + _rev(_scalar_decay_scan(_rev(cm, lc), _rev(bm, lc), _rev(xs * grp(dt[:, :, 1])[..., None], lc),
                                    _rev(grp(log_a[:, :, 1]), lc)), lc)
    y = y + d_skip.reshape(SSD_GROUPS, r)[..., None] * xs
    start = 0 if keep_ctx else lc
    n = t - start
    y = y.reshape(bsz, t, SSD_D_INNER)[:, start:].astype(h_lat.dtype) * jax.nn.silu(z[:, start:])
    y = _rmsnorm(y.reshape(bsz, n, SSD_GROUPS, -1), norm_g.reshape(SSD_GROUPS, -1)).reshape(bsz, n, SSD_D_INNER)
    return _split_out(y @ w_out, lc, keep_ctx)


def _retention_mixer(h_ctx, h_lat, w_in, log_decay, w_out, rows, keep_ctx):
    f32 = jnp.float32
    lc = h_ctx.shape[1]
    u = jnp.concatenate([h_ctx, h_lat], axis=1) @ w_in
    bsz, t = u.shape[:2]
    q, k, v, g = jnp.split(u, [D_MODEL, 2 * D_MODEL, 2 * D_MODEL + RET_DV], axis=-1)
    q = q.reshape(bsz, t, RET_HEADS, RET_QK_DIM)
    k = k.reshape(bsz, t, RET_HEADS, RET_QK_DIM) * RET_QK_DIM ** -0.5
    q = jnp.concatenate([q[:, :lc], _rope_2d(q[:, lc:], rows)], axis=1)
    k = jnp.concatenate([k[:, :lc], _rope_2d(k[:, lc:], rows)], axis=1)
    v = v.reshape(bsz, t, RET_HEADS, 1, RET_V_DIM)
    ld_f = jnp.broadcast_to(log_decay[0].astype(f32)[:, None], (bsz, t, RET_HEADS, 1))
    ld_b = jnp.broadcast_to(log_decay[1].astype(f32)[:, None], (bsz, t, RET_HEADS, 1))
    y = _scalar_decay_scan(q, k, v, ld_f)
    y = y + _rev(_scalar_decay_scan(_rev(q, lc), _rev(k, lc), _rev(v, lc), ld_b), lc)
    start = 0 if keep_ctx else lc
    n = t - start
    y = y[:, start:].reshape(bsz, n, RET_HEADS, RET_V_DIM)
    mu = jnp.mean(y, axis=-1, keepdims=True)
    var = jnp.mean(jnp.square(y - mu), axis=-1, keepdims=True)
    y = ((y - mu) * lax.rsqrt(var + NORM_EPS)).reshape(bsz, n, RET_DV).astype(h_lat.dtype)
    y = y * jax.nn.silu(g[:, start:])
    return _split_out(y @ w_out, lc, keep_ctx)


def _lower_bound(lb_logits, layer):
    p = jax.nn.softmax(lb_logits.astype(jnp.float32), axis=0)
    return jnp.cumsum(p, axis=0)[layer] - p[0]


def _hgrn2_mixer(h_ctx, h_lat, w_in, lb, norm_g, w_out, keep_ctx):
    f32 = jnp.float32
    lc = h_ctx.shape[1]
    u = jnp.concatenate([h_ctx, h_lat], axis=1) @ w_in
    bsz, t = u.shape[:2]
    q, f_f, f_b, i, g = jnp.split(u, 5, axis=-1)
    shp = (bsz, t, HGRN_HEADS, HGRN_EXPAND)
    q, i = q.reshape(shp), i.reshape(shp)
    lb = lb.reshape(HGRN_HEADS, HGRN_EXPAND)

    def gates(f):
        f = f.reshape(shp).astype(f32)
        log_f = jnp.logaddexp(jnp.log(lb), jnp.log1p(-lb) + jax.nn.log_sigmoid(f))
        return log_f, (1 - lb) * jax.nn.sigmoid(-f)

    lf_f, k_f = gates(f_f)
    lf_b, k_b = gates(f_b)
    y = _vector_decay_scan(q, k_f, i, lf_f)
    y = y + _rev(_vector_decay_scan(_rev(q, lc), _rev(k_b, lc), _rev(i, lc), _rev(lf_b, lc)), lc)
    start = 0 if keep_ctx else lc
    n = t - start
    y = _rmsnorm(y[:, start:].astype(h_lat.dtype), norm_g.reshape(HGRN_HEADS, HGRN_EXPAND))
    y = (y * jax.nn.silu(g[:, start:].reshape(bsz, n, HGRN_HEADS, HGRN_EXPAND))).reshape(bsz, n, D_MODEL)
    return _split_out(y @ w_out, lc, keep_ctx)


def _gdn_mixer(h_ctx, h_lat, w_in, conv_w, dt_bias, a_log, norm_g, w_out, keep_ctx):
    f32 = jnp.float32
    lc = h_ctx.shape[1]
    u = jnp.concatenate([h_ctx, h_lat], axis=1) @ w_in
    bsz, t = u.shape[:2]
    qkv, z, bt, a = jnp.split(u, [GDN_CONV_CH, GDN_CONV_CH + GDN_DV, GDN_CONV_CH + GDN_DV + 2 * GDN_V_HEADS], axis=-1)
    qkv = jax.nn.silu(_conv_split(qkv, conv_w, lc))
    q, k, v = jnp.split(qkv, [GDN_DK, 2 * GDN_DK], axis=-1)
    r = GDN_V_HEADS // GDN_K_HEADS
    q = _l2norm(q.reshape(bsz, t, GDN_K_HEADS, GDN_HEAD_DIM)) * GDN_HEAD_DIM ** -0.5
    k = _l2norm(k.reshape(bsz, t, GDN_K_HEADS, GDN_HEAD_DIM))
    v = v.reshape(bsz, t, GDN_K_HEADS, r, GDN_HEAD_DIM)
    beta = jax.nn.sigmoid(bt.reshape(bsz, t, 2, GDN_K_HEADS, r).astype(f32))
    log_a = -jnp.exp(a_log.astype(f32)).reshape(2, GDN_K_HEADS, r) * jax.nn.softplus(
        a.reshape(bsz, t, 2, GDN_K_HEADS, r).astype(f32) + dt_bias.astype(f32).reshape(2, GDN_K_HEADS, r))
    y = _delta_scan(q, k, v, beta[:, :, 0], log_a[:, :, 0])
    y = y + _rev(_delta_scan(_rev(q, lc), _rev(k, lc), _rev(v, lc), _rev(beta[:, :, 1], lc),
                             _rev(log_a[:, :, 1], lc)), lc)
    start = 0 if keep_ctx else lc
    n = t - start
    y = y[:, start:].reshape(bsz, n, GDN_V_HEADS, GDN_HEAD_DIM).astype(h_lat.dtype)
    y = _rmsnorm(y, norm_g) * jax.nn.silu(z[:, start:].reshape(bsz, n, GDN_V_HEADS, GDN_HEAD_DIM))
    return _split_out(y.reshape(bsz, n, GDN_DV) @ w_out, lc, keep_ctx)


def _n_occ(m):
    return len(range(m, DEPTH, N_MIXERS))


def setup_inputs(seed: int = 0) -> dict:
    key = jax.random.key(seed)
    ks = iter(jax.random.split(key, 48))
    f32 = jnp.float32
    d = D_MODEL

    def nrm(shape, std):
        return std * jax.random.normal(next(ks), shape, f32)

    def gain(shape):
        return 1.0 + nrm(shape, 0.02)

    def dt_bias(shape):
        dt = jnp.exp(jax.random.uniform(next(ks), shape, f32, math.log(1e-3), math.log(1e-1)))
        return dt + jnp.log(-jnp.expm1(-dt))

    def a_log(shape):
        return jnp.log(jax.random.uniform(next(ks), shape, f32, 1.0, 16.0))

    n_a, n_b, n_c, n_d = (_n_occ(m) for m in range(N_MIXERS))
    ret_base = jnp.log1p(-(2.0 ** (-5.0 - jnp.arange(RET_HEADS, dtype=f32))))
    return {
        'x': nrm((BATCH, SEQ, d), 1.0),
        'c': nrm((BATCH, d), 1.0),
        'ctx': nrm((BATCH, CTX_LEN, d), 1.0),
        'c_ctx': nrm((d,), 1.0),
        'ada_w': nrm((DEPTH, d, 6 * d), 0.5 * d ** -0.5),
        'ada_b': nrm((DEPTH, 6 * d), 0.02),
        'norm_g': gain((DEPTH, 2, d)),
        'mlp_w1': nrm((DEPTH, d, D_FF), d ** -0.5),
        'mlp_w2': nrm((DEPTH, D_FF, d), D_FF ** -0.5),
        'final_g': gain((d,)),
        'ssd_w_in': nrm((n_a, d, SSD_IN), d ** -0.5),
        'ssd_conv_w': nrm((n_a, CONV_W, SSD_CONV_CH), CONV_W ** -0.5),
        'ssd_conv_b': nrm((n_a, SSD_CONV_CH), 0.02),
        'ssd_dt_bias': dt_bias((n_a, 2, SSD_HEADS)),
        'ssd_a_log': a_log((n_a, 2, SSD_HEADS)),
        'ssd_d': gain((n_a, SSD_HEADS)),
        'ssd_norm_g': gain((n_a, SSD_D_INNER)),
        'ssd_w_out': nrm((n_a, SSD_D_INNER, d), SSD_D_INNER ** -0.5),
        'ret_w_in': nrm((n_b, d, RET_IN), d ** -0.5),
        'ret_log_decay': ret_base * jnp.exp(nrm((n_b, 2, RET_HEADS), 0.1)),
        'ret_w_out': nrm((n_b, RET_DV, d), RET_DV ** -0.5),
        'hgrn_w_in': nrm((n_c, d, HGRN_IN), d ** -0.5),
        'hgrn_lb_logits': nrm((DEPTH, d), 0.1),
        'hgrn_norm_g': gain((n_c, d)),
        'hgrn_w_out': nrm((n_c, d, d), d ** -0.5),
        'gdn_w_in': nrm((n_d, d, GDN_IN), d ** -0.5),
        'gdn_conv_w': nrm((n_d, CONV_W, GDN_CONV_CH), CONV_W ** -0.5),
        'gdn_dt_bias': dt_bias((n_d, 2, GDN_V_HEADS)),
        'gdn_a_log': a_log((n_d, 2, GDN_V_HEADS)),
        'gdn_norm_g': gain((n_d, GDN_HEAD_DIM)),
        'gdn_w_out': nrm((n_d, GDN_DV, d), GDN_DV ** -0.5),
    }


def reference(x, c, ctx, c_ctx, ada_w, ada_b, norm_g, mlp_w1, mlp_w2, final_g,
              ssd_w_in, ssd_conv_w, ssd_conv_b, ssd_dt_bias, ssd_a_log, ssd_d, ssd_norm_g, ssd_w_out,
              ret_w_in, ret_log_decay, ret_w_out,
              hgrn_w_in, hgrn_lb_logits, hgrn_norm_g, hgrn_w_out,
              gdn_w_in, gdn_conv_w, gdn_dt_bias, gdn_a_log, gdn_norm_g, gdn_w_out):
    bsz = x.shape[0]
    rows = x.shape[1] // GRID_W
    x_lat, x_ctx = x, ctx
    cond = jax.nn.silu(jnp.concatenate([c, c_ctx[None]], axis=0))
    for i in range(DEPTH):
        mixer, occ = i % N_MIXERS, i // N_MIXERS
        keep_ctx = i < DEPTH - 1
        mod = cond @ ada_w[i] + ada_b[i]
        sh1, sc1, g1, sh2, sc2, g2 = jnp.split(mod, 6, axis=-1)
        h_lat = _adaln(x_lat, norm_g[i, 0], sh1[:bsz, None], sc1[:bsz, None])
        h_ctx = _adaln(x_ctx, norm_g[i, 0], sh1[bsz], sc1[bsz])
        if mixer == 0:
            y_ctx, y_lat = _ssd_mixer(h_ctx, h_lat, ssd_w_in[occ], ssd_conv_w[occ], ssd_conv_b[occ], ssd_dt_bias[occ],
                                      ssd_a_log[occ], ssd_d[occ], ssd_norm_g[occ], ssd_w_out[occ], keep_ctx)
        elif mixer == 1:
            y_ctx, y_lat = _retention_mixer(h_ctx, h_lat, ret_w_in[occ], ret_log_decay[occ], ret_w_out[occ],
                                            rows, keep_ctx)
        elif mixer == 2:
            y_ctx, y_lat = _hgrn2_mixer(h_ctx, h_lat, hgrn_w_in[occ], _lower_bound(hgrn_lb_logits, i),
                                        hgrn_norm_g[occ], hgrn_w_out[occ], keep_ctx)
        else:
            y_ctx, y_lat = _gdn_mixer(h_ctx, h_lat, gdn_w_in[occ], gdn_conv_w[occ], gdn_dt_bias[occ],
                                      gdn_a_log[occ], gdn_norm_g[occ], gdn_w_out[occ], keep_ctx)
        x_lat = x_lat + g1[:bsz, None] * y_lat
        h_lat = _adaln(x_lat, norm_g[i, 1], sh2[:bsz, None], sc2[:bsz, None])
        x_lat = x_lat + g2[:bsz, None] * _sq_relu_mlp(h_lat, mlp_w1[i], mlp_w2[i])
        if keep_ctx:
            x_ctx = x_ctx + g1[bsz] * y_ctx
            h_ctx = _adaln(x_ctx, norm_g[i, 1], sh2[bsz], sc2[bsz])
            x_ctx = x_ctx + g2[bsz] * _sq_relu_mlp(h_ctx, mlp_w1[i], mlp_w2[i])
    return _rmsnorm(x_lat, final_g)
```

```python
import numpy as np
from concourse.bass_utils import run_bass_kernel_spmd
from contextlib import ExitStack
import numpy as np
import concourse.bass as bass
import concourse.mybir as mybir

F32 = mybir.dt.float32
BF16 = mybir.dt.bfloat16
AF = mybir.ActivationFunctionType
ALU = mybir.AluOpType
AX = mybir.AxisListType

COMPUTE = ("tensor", "vector", "scalar", "gpsimd")
ISSUERS = ("sync", "gpsimd", "scalar")
KDMA = 8
EPOCH = 20000


class Prog:
    def __init__(self):
        self.nc = bass.Bass("TRN2", target_bir_lowering=False)
        self.ops = []
        self.stack = ExitStack()
        self.last_w = {}
        self.readers = {}
        self.n_sb = 0

    def sbuf(self, shape, dtype, name=None):
        self.n_sb += 1
        name = name or f"sb{self.n_sb}"
        return self.stack.enter_context(self.nc.sbuf_tensor(name, list(shape), dtype))

    def psum(self, shape, dtype, name=None):
        self.n_sb += 1
        name = name or f"ps{self.n_sb}"
        return self.stack.enter_context(self.nc.psum_tensor(name, list(shape), dtype))

    def dram(self, name, shape, dtype, kind="Internal"):
        return self.nc.dram_tensor(name, list(shape), dtype, kind=kind).ap()

    def _deps(self, reads, writes):
        deps = set()
        for k in reads:
            if k in self.last_w:
                deps.add(self.last_w[k])
        for k in writes:
            if k in self.last_w:
                deps.add(self.last_w[k])
            for r in self.readers.get(k, ()):
                deps.add(r)
        return deps

    def _commit(self, idx, reads, writes):
        for k in reads:
            self.readers.setdefault(k, []).append(idx)
        for k in writes:
            self.last_w[k] = idx
            self.readers[k] = []

    def op(self, eng, fn, reads=(), writes=(), floor=None, extra=()):
        idx = len(self.ops)
        deps = self._deps(reads, writes)
        deps.update(extra)
        if floor is not None:
            deps.add(floor)
        self.ops.append(dict(eng=eng, fn=fn, deps=deps, dma=False))
        self._commit(idx, reads, writes)
        return idx

    def dma(self, issuer, out, in_, reads=(), writes=(), floor=None, **kw):
        idx = len(self.ops)
        deps = self._deps(reads, writes)
        if floor is not None:
            deps.add(floor)
        self.ops.append(dict(eng=issuer, fn=lambda e: e.dma_start(out=out, in_=in_, **kw),
                             deps=deps, dma=True))
        self._commit(idx, reads, writes)
        return idx

    def mm(self, out, lhsT, rhs, start, stop, reads, writes, **kw):
        return self.op("tensor", lambda e: e.matmul(out, lhsT, rhs, start=start, stop=stop, **kw),
                       reads, writes)

    def emit(self, final_keys=()):
        nc = self.nc
        ops = self.ops
        has_dep = [False] * len(ops)
        for o in ops:
            for d in o["deps"]:
                has_dep[d] = True
        final_deps = set()
        for k in final_keys:
            if k in self.last_w:
                final_deps.add(self.last_w[k])
        for d in final_deps:
            has_dep[d] = True
        cnt = {e: 0 for e in COMPUTE}
        dcnt = {e: 0 for e in ISSUERS}
        for i, o in enumerate(ops):
            if o["dma"]:
                n = dcnt[o["eng"]]
                dcnt[o["eng"]] += 1
                o["dn"] = n
                o["done"] = (("d", o["eng"], n % KDMA), 16 * (n // KDMA + 1))
            else:
                if has_dep[i]:
                    c = cnt[o["eng"]]
                    cnt[o["eng"]] += 1
                    o["done"] = (("c", o["eng"], c // EPOCH), c % EPOCH + 1)
                    o["inc"] = True
                else:
                    o["done"] = None
                    o["inc"] = False
        semkeys = set()
        for o in ops:
            if o.get("done"):
                semkeys.add(o["done"][0])
        sems = {}
        for k in sorted(semkeys):
            sems[k] = self.stack.enter_context(nc.semaphore("s_" + "_".join(map(str, k))))
        streams = {}
        for i, o in enumerate(ops):
            streams.setdefault(o["eng"], []).append(i)
        stats = dict(waits=0)

        def run_engine(engname, e):
            waited = {}
            dma_hist = []
            for i in streams.get(engname, []):
                o = ops[i]
                need = {}
                for d in o["deps"]:
                    od = ops[d]
                    if od["done"] is None:
                        continue
                    if (not od["dma"]) and od["eng"] == "tensor" and engname == "tensor" and not o["dma"]:
                        continue
                    sk, v = od["done"]
                    if need.get(sk, 0) < v:
                        need[sk] = v
                if o["dma"]:
                    n = o["dn"]
                    if n >= KDMA:
                        sk = ("d", engname, n % KDMA)
                        v = 16 * (n // KDMA)
                        if need.get(sk, 0) < v:
                            need[sk] = v
                for sk, v in need.items():
                    if waited.get(sk, 0) >= v:
                        continue
                    e.wait_ge(sems[sk], v)
                    waited[sk] = v
                    stats["waits"] += 1
                ins = o["fn"](e)
                if o["dma"]:
                    ins.then_inc(sems[o["done"][0]], 16)
                elif o["inc"]:
                    ins.then_inc(sems[o["done"][0]], 1)
            if engname == "sync":
                need = {}
                for d in final_deps:
                    sk, v = ops[d]["done"]
                    if need.get(sk, 0) < v:
                        need[sk] = v
                for sk, v in need.items():
                    if waited.get(sk, 0) < v:
                        e.wait_ge(sems[sk], v)

        with nc.Block() as block:
            @block.sync
            def _(e):
                run_engine("sync", e)

            @block.tensor
            def _(e):
                run_engine("tensor", e)

            @block.vector
            def _(e):
                run_engine("vector", e)

            @block.scalar
            def _(e):
                run_engine("scalar", e)

            @block.gpsimd
            def _(e):
                run_engine("gpsimd", e)
        self.stats = stats
        self.stack.close()
        return nc


import math

T = 2304
LC = 256
D = 2048
KC = 16
NT = 18
TG6 = [(0, 256), (256, 768), (768, 1280), (1280, 1536), (1536, 2048), (2048, 2304)]
SETS = [(0, 1), (2, 3), (4, 5)]
ARENA_BYTES = 184 * 1024


def dsize(dt):
    return 4 if dt == F32 else 2


class Tl:
    def __init__(self, ap, key):
        self.ap = ap
        self.key = key

    def __getitem__(self, idx):
        return self.ap[idx]


class Pool:
    def __init__(self, B, n, shape, dtype, name):
        self.tiles = [B.alloc(shape, dtype, f"{name}{i}") for i in range(n)]
        self.i = 0

    def next(self):
        t = self.tiles[self.i % len(self.tiles)]
        self.i += 1
        return t


class Builder:
    def __init__(self):
        self.P = Prog()
        P = self.P
        self.arena = P.sbuf([128, ARENA_BYTES // 4], F32, name="arena")
        self.aoff = 0
        self.nkey = 0
        self.banks = [Tl(P.psum([128, 512], F32, name=f"bank{i}"), ("bank", i)) for i in range(8)]
        self.bi = 0
        self.bar_tile = P.sbuf([128, 8], F32, name="bar")
        self.floor = None

    def mark(self):
        return self.aoff

    def release(self, m):
        if self.aoff > m:
            self.barrier()
        self.aoff = m

    def alloc(self, shape, dtype, name="t"):
        n = 1
        for s in shape[1:]:
            n *= s
        nb = n * dsize(dtype)
        nb = (nb + 63) // 64 * 64
        assert self.aoff + nb <= ARENA_BYTES, f"arena overflow {name} {self.aoff + nb}"
        v = self.arena[:, self.aoff // 4:(self.aoff + nb) // 4]
        self.aoff += nb
        if dtype != F32:
            v = v.bitcast(dtype)
        v = v[:, 0:n]
        if len(shape) == 3:
            v = v.rearrange("p (a b) -> p a b", a=shape[1])
        elif len(shape) == 4:
            v = v.rearrange("p (a b c) -> p a b c", a=shape[1], b=shape[2])
        if shape[0] != 128:
            v = v[0:shape[0]]
        self.nkey += 1
        return Tl(v, (name, self.nkey))

    def bank(self):
        b = self.banks[self.bi % 8]
        self.bi += 1
        return b

    def op(self, eng, fn, reads=(), writes=()):
        return self.P.op(eng, fn, self._k(reads), self._k(writes), floor=self.floor)

    def I(self, eng, meth, *args, reads=(), writes=(), **kw):
        r = self._k(reads); w = list(self._k(writes))
        bk = [k for k in r if isinstance(k, tuple) and k and k[0] == "bank"]
        r = [k for k in r if k not in bk]
        for k in bk:
            if k not in w:
                w.append(k)
        return self.P.op(eng, lambda e: getattr(e, meth)(*args, **kw), r, w, floor=self.floor)

    def dma(self, out, in_, reads=(), writes=(), issuer="sync", **kw):
        return self.P.dma(issuer, out, in_, self._k(reads), self._k(writes), floor=self.floor, **kw)

    def mm(self, out, lhsT, rhs, start, stop, reads, writes):
        return self.I("tensor", "matmul", out, lhsT, rhs, start=start, stop=stop, reads=reads, writes=writes)

    @staticmethod
    def _k(lst):
        return [x.key if isinstance(x, Tl) else x for x in lst]

    def barrier(self):
        P = self.P
        last = {}
        dmas = {}
        for i, o in enumerate(P.ops):
            if o["dma"]:
                dmas.setdefault(o["eng"], []).append(i)
            else:
                last[o["eng"]] = i
        deps = set(last.values())
        for q, l in dmas.items():
            deps.update(l[-KDMA:])
        bt = self.bar_tile
        idx = P.op("vector", lambda e: e.memset(bt[:], 0.0), [], [("bar", len(P.ops))], floor=self.floor, extra=deps)
        self.floor = idx


def _tok_init(self, nl=4):
    P = self.P
    B = self
    self.d = {}

    def inp(name, shape, dt=F32):
        self.d[name] = P.dram(name, shape, dt, kind="ExternalInput")
        return self.d[name]
    self.inp = inp
    inp("xT0", [D, T]); inp("cvec", [128, 16, 2])
    inp("ada_w", [nl, D, 6 * D]); inp("ada_bT", [nl, 128, 96]); inp("norm_gT", [nl, 2, 128, 16])
    inp("mlp_w1", [nl, D, 4 * D]); inp("mlp_w2", [nl, 4 * D, D]); inp("final_gT", [128, 16])
    inp("ident_f", [128, 128]); inp("ones_f", [128, 128])
    self.XT = P.dram("XT", [D, T], F32)
    self.outT = P.dram("outT", [D, T - LC], F32, kind="ExternalOutput")
    self.ident_f = B.alloc([128, 128], F32, "ident_f")
    self.ones_f = B.alloc([128, 128], F32, "ones_f")
    self.ident_b = B.alloc([128, 128], BF16, "ident_b")
    self.cond = B.alloc([128, 16, 2], BF16, "cond")
    self.modT = B.alloc([128, 96, 2], F32, "modT")
    self.gA = B.alloc([128, 16, 2], F32, "gA")
    self.adab = B.alloc([128, 96], F32, "adab")
    self.ng = B.alloc([128, 2, 16], F32, "ng")
    self.eps_t = B.alloc([128, 1], F32, "eps")
    self.zero_t = B.alloc([128, 1], F32, "zero")
    B.dma(self.ident_f.ap, self.d["ident_f"], [], [self.ident_f])
    B.dma(self.ones_f.ap, self.d["ones_f"], [], [self.ones_f])
    B.I("vector", "tensor_copy", self.ident_b.ap, self.ident_f.ap, reads=[self.ident_f], writes=[self.ident_b])
    B.I("vector", "memset", self.eps_t.ap, 1e-6, writes=[self.eps_t])
    B.I("vector", "memset", self.zero_t.ap, 0.0, writes=[self.zero_t])
    m = B.mark()
    cv = B.alloc([128, 16, 2], F32, "cv")
    B.dma(cv.ap, self.d["cvec"], [], [cv])
    B.I("scalar", "activation", self.cond.ap, cv.ap, AF.Silu, reads=[cv], writes=[self.cond])
    B.barrier()
    B.release(m)
    for k in range(4):
        B.dma(self.XT[k * 512:(k + 1) * 512, :], self.d["xT0"][k * 512:(k + 1) * 512, :], [], [])
    B.barrier()


def _seg_tok(lo):
    return 1 if lo < LC else 0


def _gemm(self, W, nkq, col_blocks, act, groups, evac, kc=16):
    B = self
    m = B.mark()
    stage = Pool(B, 2, [128, kc, 128], F32, "wst")
    wbf = Pool(B, 3, [128, kc, 128], BF16, "wbf")
    for j in col_blocks:
        bk = {}
        for kq in range(nkq):
            st = stage.next()
            src = W[kq * kc * 128:(kq + 1) * kc * 128, j * 128:(j + 1) * 128].rearrange("(k p) c -> p k c", p=128)
            B.dma(st.ap, src, [], [st])
            wb = wbf.next()
            B.I("gpsimd", "tensor_copy", wb.ap, st.ap, reads=[st], writes=[wb])
            for gi, (lo, hi) in enumerate(groups):
                if kq == 0:
                    bk[gi] = B.bank()
                ps = bk[gi]
                for k in range(kc):
                    a_ap, a_keys = act(kq * kc + k, lo, hi)
                    B.mm(ps.ap[:, 0:hi - lo], wb.ap[:, k, :], a_ap, kq == 0 and k == 0,
                         kq == nkq - 1 and k == kc - 1, [wb] + list(a_keys), [ps])
        for gi, (lo, hi) in enumerate(groups):
            evac(j, gi, lo, hi, bk[gi])
    B.release(m)


def _mod(self, li):
    B = self
    m = B.mark()
    B.dma(self.adab.ap, self.d["ada_bT"][li], [], [self.adab])
    B.dma(self.ng.ap, self.d["norm_gT"][li].rearrange("a p k -> p a k"), [], [self.ng])
    cond = self.cond

    def act(k, lo, hi):
        return cond.ap[:, k, :], [cond]

    def evac(j, gi, lo, hi, ps):
        B.I("vector", "tensor_scalar", self.modT.ap[:, j, :], ps.ap[:, 0:2], self.adab.ap[:, j:j + 1], None, ALU.add,
            reads=[ps, self.adab, self.modT], writes=[self.modT])
    _gemm(self, self.d["ada_w"][li], 1, range(96), act, [(0, 2)], evac)
    B.release(m)


def _set_gain(self, which):
    B = self
    sc = self.modT.ap[:, (16 + 48 * which):(32 + 48 * which), :]
    B.I("vector", "tensor_scalar", self.gA.ap, sc, 1.0, None, ALU.add, reads=[self.modT, self.gA], writes=[self.gA])
    B.I("vector", "tensor_tensor", self.gA.ap, self.gA.ap,
        self.ng.ap[:, which, :].unsqueeze(2).to_broadcast([128, 16, 2]), ALU.mult,
        reads=[self.gA, self.ng], writes=[self.gA])


def _adaln(self, src, groups, segs, gain_ap, shift_ap, out_fn, post=None):
    B = self
    m = B.mark()
    xp = Pool(B, 2, [128, 16, 512], F32, "xg")
    sqp = Pool(B, 3, [128, 512], F32, "sq")
    rsp = Pool(B, 2, [128, 512], F32, "rs")
    tmp = Pool(B, 3, [128, 512], F32, "tmp")
    for gi, (lo, hi) in enumerate(groups):
        n = hi - lo
        xg = xp.next()
        B.dma(xg.ap[:, :, 0:n], src[:, lo:hi].rearrange("(k p) t -> p k t", p=128), [], [xg])
        ps = B.bank()
        for k in range(16):
            sq = sqp.next()
            B.I("scalar", "activation", sq.ap[:, 0:n], xg.ap[:, k, 0:n], AF.Square, reads=[xg], writes=[sq])
            B.mm(ps.ap[:, 0:n], self.ones_f.ap, sq.ap[:, 0:n], k == 0, k == 15, [self.ones_f, sq], [ps])
        rs = rsp.next()
        B.I("scalar", "activation", rs.ap[:, 0:n], ps.ap[:, 0:n], AF.Ln, bias=self.eps_t.ap, scale=1.0 / D,
            reads=[ps, self.eps_t], writes=[rs])
        B.I("scalar", "activation", rs.ap[:, 0:n], rs.ap[:, 0:n], AF.Exp, scale=-0.5, reads=[rs], writes=[rs])
        s = segs[gi]
        for k in range(16):
            t = tmp.next()
            B.I("vector", "tensor_tensor", t.ap[:, 0:n], xg.ap[:, k, 0:n], rs.ap[:, 0:n], ALU.mult, reads=[xg, rs], writes=[t])
            o_ap, o_keys = out_fn(k, gi, lo, hi)
            g_ap, g_keys = gain_ap(k, s)
            b_ap, b_keys = shift_ap(k, s)
            B.I("scalar", "activation", o_ap, t.ap[:, 0:n], AF.Identity, bias=b_ap, scale=g_ap,
                reads=[t] + list(g_keys) + list(b_keys), writes=list(o_keys))
            if post is not None:
                post(k, gi, lo, hi)
    B.release(m)


def _residual_evac(self, gate_blk0, xpool):
    B = self

    def evac(j, gi, lo, hi, ps):
        n = hi - lo
        xp = xpool.next()
        B.dma(xp.ap[:, 0:n], self.XT[j * 128:(j + 1) * 128, lo:hi], [], [xp])
        s = _seg_tok(lo)
        B.I("vector", "scalar_tensor_tensor", xp.ap[:, 0:n], ps.ap[:, 0:n], self.modT.ap[:, gate_blk0 + j, s:s + 1],
            xp.ap[:, 0:n], ALU.mult, ALU.add, reads=[ps, self.modT, xp], writes=[xp])
        B.dma(self.XT[j * 128:(j + 1) * 128, lo:hi], xp.ap[:, 0:n], [xp], [])
    return evac


def _mlp(self, li, last):
    B = self
    _set_gain(self, 1)
    B.barrier()
    for si, (ga, gb) in enumerate(SETS):
        groups = [TG6[ga], TG6[gb]]
        if last and si == 0:
            groups = [TG6[gb]]
        base = groups[0][0]
        m = B.mark()
        h2 = B.alloc([128, 16, 768], BF16, "h2")
        _adaln(self, self.XT, groups, [_seg_tok(g[0]) for g in groups],
               lambda k, s: (self.gA.ap[:, k, s:s + 1], [self.gA]),
               lambda k, s: (self.modT.ap[:, 48 + k, s:s + 1], [self.modT]),
               lambda k, gi, lo, hi, h2=h2, base=base: (h2.ap[:, k, lo - base:hi - base], [h2]))
        hid = B.alloc([128, 64, 768], BF16, "hid")
        rp = Pool(B, 3, [128, 512], F32, "relu")

        def act1(k, lo, hi, h2=h2, base=base):
            return h2.ap[:, k, lo - base:hi - base], [h2]

        def evac1(j, gi, lo, hi, ps, hid=hid, base=base, rp=rp):
            n = hi - lo
            r = rp.next()
            B.I("scalar", "activation", r.ap[:, 0:n], ps.ap[:, 0:n], AF.Relu, reads=[ps], writes=[r])
            B.I("gpsimd", "tensor_tensor", hid.ap[:, j, lo - base:hi - base], r.ap[:, 0:n], r.ap[:, 0:n], ALU.mult,
                reads=[r, hid], writes=[hid])
        _gemm(self, self.d["mlp_w1"][li], 1, range(64), act1, groups, evac1)

        if getattr(self, "dbg_mlp", False) and si == 0 and li == 0:
            d1 = B.P.dram("dbg_h2", [128, 16, 768], BF16, kind="ExternalOutput")
            d2 = B.P.dram("dbg_hid", [128, 64, 768], BF16, kind="ExternalOutput")
            B.dma(d1, h2.ap, [h2], [("dbgh2",)])
            B.dma(d2, hid.ap, [hid], [("dbghid",)])

        def act2(k, lo, hi, hid=hid, base=base):
            return hid.ap[:, k, lo - base:hi - base], [hid]
        xpool = Pool(B, 3, [128, 512], F32, "xres")
        _gemm(self, self.d["mlp_w2"][li], 4, range(16), act2, groups, _residual_evac(self, 80, xpool))
        B.barrier()
        B.release(m)


def _outproj(self, li, W, nkq, YT, last):
    B = self
    halves = [[TG6[0], TG6[1], TG6[2]], [TG6[3], TG6[4], TG6[5]]]
    if last:
        halves[0] = [TG6[1], TG6[2]]
    for groups in halves:
        base = groups[0][0]
        ntok = groups[-1][1] - base
        m = B.mark()
        y = B.alloc([128, nkq * 16, 1280], BF16, "yT")
        for q in range(nkq * 2):
            B.dma(y.ap[:, q * 8:(q + 1) * 8, 0:ntok],
                  YT[q * 1024:(q + 1) * 1024, base:base + ntok].rearrange("(k p) t -> p k t", p=128), [], [y])

        def act(k, lo, hi, y=y, base=base):
            return y.ap[:, k, lo - base:hi - base], [y]
        xpool = Pool(B, 3, [128, 512], F32, "xres")
        _gemm(self, W, nkq, range(16), act, groups, _residual_evac(self, 32, xpool))
        B.barrier()
        B.release(m)


def _final(self):
    B = self
    m = B.mark()
    fg = B.alloc([128, 16], F32, "fg")
    B.dma(fg.ap, self.d["final_gT"], [], [fg])
    op = Pool(B, 3, [128, 512], F32, "fo")
    groups = TG6[1:]
    cur = {}

    def out_fn(k, gi, lo, hi):
        t = op.next()
        cur[0] = t
        return t.ap[:, 0:hi - lo], [t]

    def post(k, gi, lo, hi):
        t = cur[0]
        B.dma(self.outT[k * 128:(k + 1) * 128, lo - LC:hi - LC], t.ap[:, 0:hi - lo], [t], [("out", k, gi)])
        self.out_keys.append(("out", k, gi))
    self.out_keys = []
    _adaln(self, self.XT, groups, [0] * len(groups), lambda k, s: (fg.ap[:, k:k + 1], [fg]),
           lambda k, s: (self.zero_t.ap, [self.zero_t]), out_fn, post)
    B.release(m)


NEGV = -30000.0


def _mix_init(self, layers=(0, 1, 2, 3)):
    B = self
    P = self.P
    inp = self.inp
    inp("tri", [2, 128, 128]); inp("neg", [2, 128, 128])
    self.tri = [B.alloc([128, 128], F32, "tri0"), B.alloc([128, 128], F32, "tri1")]
    self.neg = [B.alloc([128, 128], F32, "neg0"), B.alloc([128, 128], F32, "neg1")]
    for i in range(2):
        B.dma(self.tri[i].ap, self.d["tri"][i], [], [self.tri[i]])
        B.dma(self.neg[i].ap, self.d["neg"][i], [], [self.neg[i]])
    self.U = P.dram("U", [97, 128, T], F32)
    self.QT = P.dram("QT", [16, 128, T], BF16)
    self.KT = P.dram("KT", [16, 128, T], BF16)
    self.KK = P.dram("KK", [T, 2048], BF16)
    self.VV = P.dram("VV", [T, 4096], BF16)
    self.LA = P.dram("LA", [2, T, 64], F32)
    self.WT = P.dram("WT", [2, T, 64], F32)
    self.Y = P.dram("Y", [T, 4096], F32)
    self.YT = P.dram("YT", [4096, T], BF16)
    if 1 in layers:
        inp("ret_w_in", [D, 12288]); inp("ret_w_out", [4096, D]); inp("ret_la", [2, T, 8])
        inp("rope_cos", [128, T]); inp("rope_sin", [128, T])
    if 3 in layers:
        inp("gdn_w_in", [D, 12416]); inp("gdn_w_out", [4096, D]); inp("gdn_cw", [128, 64, 3]); inp("gdn_par", [128, 3])
        inp("gdn_ng", [128, 32]); inp("pos", [2, 128, 128]); inp("bmask", [4, 128, 128])
    if 2 in layers:
        inp("hgrn_w_in", [D, 10240]); inp("hgrn_w_out", [D, D]); inp("hgrn_lbl", [128, 16, 4]); inp("hgrn_ng", [128, 16])
    if 0 in layers:
        inp("ssd_w_in", [D, 10368]); inp("ssd_w_out", [4096, D]); inp("ssd_cw", [128, 48, 3]); inp("ssd_cb", [128, 48])
        inp("ssd_par", [128, 2]); inp("ssd_dbc", [128, 64]); inp("ssd_ng", [128, 32])
    B.barrier()


def _inproj_dump(self, W, nblk):
    B = self
    m = B.mark()
    hT = B.alloc([128, 16, T], BF16, "hT")
    _adaln(self, self.XT, TG6, [_seg_tok(g[0]) for g in TG6],
           lambda k, s: (self.gA.ap[:, k, s:s + 1], [self.gA]),
           lambda k, s: (self.modT.ap[:, k, s:s + 1], [self.modT]),
           lambda k, gi, lo, hi: (hT.ap[:, k, lo:hi], [hT]))
    up = Pool(B, 3, [128, 512], F32, "uo")

    def act(k, lo, hi):
        return hT.ap[:, k, lo:hi], [hT]

    def evac(j, gi, lo, hi, ps):
        n = hi - lo
        u = up.next()
        B.I("scalar", "copy", u.ap[:, 0:n], ps.ap[:, 0:n], reads=[ps], writes=[u])
        B.dma(self.U[j, :, lo:hi], u.ap[:, 0:n], [u], [])
    _gemm(self, W, 1, range(nblk), act, TG6, evac)
    B.release(m)


def _transpose_blocks(self, src_fn, nblk, dst, dt_in, chunkcols=None):
    B = self
    m = B.mark()
    sp = Pool(B, 3, [128, T], dt_in, "tsrc")
    op = Pool(B, 3, [128, 1024], BF16, "tdst")
    ident = self.ident_b if dt_in == BF16 else self.ident_f
    for b in range(nblk):
        s = sp.next()
        B.dma(s.ap, src_fn(b), [], [s])
        for c0 in range(0, NT, 8):
            nt = min(8, NT - c0)
            ps = B.bank()
            pv = ps.ap.bitcast(BF16) if dt_in == BF16 else ps.ap
            for i in range(nt if dt_in == BF16 else 0):
                B.I("tensor", "transpose", pv[:, i * 128:(i + 1) * 128], s.ap[:, (c0 + i) * 128:(c0 + i + 1) * 128], ident.ap,
                    reads=[s, ident], writes=[ps])
            o = op.next()
            B.I("vector", "tensor_copy", o.ap[:, 0:nt * 128], pv[:, 0:nt * 128], reads=[ps], writes=[o])
            B.dma(dst[c0 * 128:(c0 + nt) * 128, b * 128:(b + 1) * 128].rearrange("(c t) f -> t c f", t=128),
                  o.ap[:, 0:nt * 128].rearrange("t (c f) -> t c f", f=128), [o], [])
    B.release(m)


def _scan(self, H, kq, qmap, Vd, decay, has_wt, Q=128, EE=None, LA=None, WT=None, QTs=None, KTs=None, KKs=None, aux=None, vhook=None):
    B = self
    LA = self.LA if LA is None else LA
    WT = self.WT if WT is None else WT
    QTs = QTs or [self.QT, self.QT]; KTs = KTs or [self.KT, self.KT]; KKs = KKs or [self.KK, self.KK]
    VT = H * Vd
    NQB = max(max(qmap(h)) for h in range(H)) + 1
    nch = T // Q
    nctx = LC // Q
    m = B.mark()
    S = B.alloc([128, kq, VT], F32, "S")
    Sb = B.alloc([128, kq, VT], BF16, "Sb")
    qp = Pool(B, 2, [128, NQB, Q], BF16, "qc")
    kp = Pool(B, 2, [128, NQB, Q], BF16, "kc")
    kkp = Pool(B, 2, [128, NQB * 128], BF16, "kkc")
    vp = Pool(B, 2, [128, VT], BF16, "vc")
    yp = Pool(B, 2, [128, VT], F32, "yc")
    lap = Pool(B, 2, [128, 2, H], F32, "lac")
    smallp = Pool(B, 2, [128, 6, H], F32, "small")
    rhp = Pool(B, 3, [128, 128], F32, "rh")
    dtp = Pool(B, 3, [128, 128], F32, "dt")
    scp = Pool(B, 2, [128, 128], F32, "sc")
    atp = Pool(B, 3, [128, 128], BF16, "at")
    kwp = Pool(B, 3, [128, kq * 128], BF16, "kw")
    for dr in (0, 1):
        order = list(range(nch)) if dr == 0 else list(range(nctx - 1, -1, -1)) + list(range(nch - 1, nctx - 1, -1))
        for h in range(H):
            B.I("vector", "memset", S.ap[:, :, h * Vd:(h + 1) * Vd], 0.0, writes=[("S", h)])
            B.I("gpsimd", "memset", Sb.ap[:, :, h * Vd:(h + 1) * Vd], 0.0, writes=[("Sb", h)])
        tri = self.tri[dr]
        neg = self.neg[dr]
        for c in order:
            t0 = c * Q
            qc = qp.next(); kc = kp.next(); kk = kkp.next(); vc = vp.next(); yc = yp.next()
            B.dma(qc.ap, QTs[dr][0:NQB, :, t0:t0 + Q].rearrange("b p t -> p b t"), [], [qc])
            B.dma(kc.ap, KTs[dr][0:NQB, :, t0:t0 + Q].rearrange("b p t -> p b t"), [], [kc])
            B.dma(kk.ap[0:Q], KKs[dr][t0:t0 + Q, 0:NQB * 128], [], [kk])
            B.dma(vc.ap[0:Q], self.VV[t0:t0 + Q, 0:VT], [], [vc])
            if decay:
                la = lap.next(); sm = smallp.next()
                B.dma(la.ap[0:Q, 0, :], LA[dr, t0:t0 + Q, 0:H], [], [la])
                if has_wt:
                    B.dma(la.ap[0:Q, 1, :], WT[dr, t0:t0 + Q, 0:H], [], [la])
                if aux is not None:
                    B.dma(la.ap[0:Q, 1, :], aux[dr, t0:t0 + Q, 0:H], [], [la])
                psc = B.bank()
                B.mm(psc.ap[0:Q, 0:H], tri.ap[0:Q, 0:Q], la.ap[0:Q, 0, :], True, True, [tri, la], [psc])
                B.mm(psc.ap[0:Q, 128:128 + H], self.ones_f.ap[0:Q, 0:Q], la.ap[0:Q, 0, :], True, True, [self.ones_f, la], [psc])
                cs, negb, ec, w, eend, tmp = [sm.ap[0:Q, i, :] for i in range(6)]
                B.I("vector", "tensor_copy", cs, psc.ap[0:Q, 0:H], reads=[psc], writes=[sm])
                if has_wt:
                    B.I("vector", "tensor_tensor", negb, la.ap[0:Q, 1, :], cs, ALU.subtract, reads=[sm, la], writes=[sm])
                else:
                    B.I("vector", "tensor_scalar", negb, cs, -1.0, None, ALU.mult, reads=[sm], writes=[sm])
                B.I("scalar", "activation", ec, cs, AF.Exp, reads=[sm], writes=[sm])
                B.I("vector", "tensor_tensor", tmp, psc.ap[0:Q, 128:128 + H], negb, ALU.add, reads=[psc, sm], writes=[sm])
                B.I("scalar", "activation", w, tmp, AF.Exp, reads=[sm], writes=[sm])
                B.I("scalar", "activation", eend, psc.ap[0:Q, 128:128 + H], AF.Exp, reads=[psc, sm], writes=[sm])
            sc = None
            lastg = None
            for h in range(H):
                qbs = qmap(h)
                hs = slice(h * Vd, (h + 1) * Vd)
                if decay:
                    rh = rhp.next()
                    B.I("vector", "tensor_scalar", rh.ap[0:Q, 0:Q], tri.ap[0:Q, 0:Q], la.ap[0:Q, 0, h:h + 1], None, ALU.mult,
                        reads=[tri, la], writes=[rh])
                    psd = B.bank()
                    B.mm(psd.ap[0:Q, 0:Q], self.ones_f.ap[0:Q, 0:Q], rh.ap[0:Q, 0:Q], True, False, [self.ones_f, rh], [psd])
                    B.mm(psd.ap[0:Q, 0:Q], self.ident_f.ap[0:Q, 0:Q], neg.ap[0:Q, 0:Q], False, True, [self.ident_f, neg], [psd])
                    dtm = dtp.next()
                    B.I("scalar", "activation", dtm.ap[0:Q, 0:Q], psd.ap[0:Q, 0:Q], AF.Exp, bias=negb[:, h:h + 1],
                        reads=[psd, sm], writes=[dtm])
                    dmat = dtm
                else:
                    dmat = tri
                if tuple(qbs) != lastg:
                    lastg = tuple(qbs)
                    pss = B.bank()
                    for i, qb in enumerate(qbs):
                        B.mm(pss.ap[0:Q, 0:Q], kc.ap[:, qb, :], qc.ap[:, qb, :], i == 0, i == len(qbs) - 1, [kc, qc], [pss])
                    sc = scp.next()
                    B.I("scalar", "copy", sc.ap[0:Q, 0:Q], pss.ap[0:Q, 0:Q], reads=[pss], writes=[sc])
                at = atp.next()
                B.I("vector", "tensor_tensor", at.ap[0:Q, 0:Q], dmat.ap[0:Q, 0:Q], sc.ap[0:Q, 0:Q], ALU.mult,
                    reads=[dmat, sc], writes=[at])
                v_ap, v_keys = vc.ap[0:Q, hs], [vc]
                if vhook is not None:
                    v_ap, v_keys = vhook(h, dr, qbs, kc, vc, la, sm, rh, Sb, hs)
                psy = B.bank()
                B.mm(psy.ap[0:Q, 0:Vd], at.ap[0:Q, 0:Q], v_ap, True, True, [at] + v_keys, [psy])
                psys = B.bank()
                for i, qb in enumerate(qbs):
                    B.mm(psys.ap[0:Q, 0:Vd], qc.ap[:, qb, :], Sb.ap[:, i, hs], i == 0, i == len(qbs) - 1, [qc, ("Sb", h)], [psys])
                B.I("scalar", "copy", yc.ap[0:Q, hs], psy.ap[0:Q, 0:Vd], reads=[psy, yc], writes=[yc])
                if decay:
                    B.I("vector", "scalar_tensor_tensor", yc.ap[0:Q, hs], psys.ap[0:Q, 0:Vd], ec[:, h:h + 1], yc.ap[0:Q, hs],
                        ALU.mult, ALU.add, reads=[psys, sm, yc], writes=[yc])
                else:
                    B.I("vector", "tensor_tensor", yc.ap[0:Q, hs], psys.ap[0:Q, 0:Vd], yc.ap[0:Q, hs], ALU.add,
                        reads=[psys, yc], writes=[yc])
                if decay:
                    kw = kwp.next()
                    for i, qb in enumerate(qbs):
                        B.I("vector", "tensor_scalar", kw.ap[0:Q, i * 128:(i + 1) * 128], kk.ap[0:Q, qb * 128:(qb + 1) * 128],
                            w[:, h:h + 1], None, ALU.mult, reads=[kk, sm, kw], writes=[kw])
                for i, qb in enumerate(qbs):
                    psS = B.bank()
                    lhs = kw.ap[0:Q, i * 128:(i + 1) * 128] if decay else kk.ap[0:Q, qb * 128:(qb + 1) * 128]
                    B.mm(psS.ap[:, 0:Vd], lhs, v_ap, True, True, [kw if decay else kk] + v_keys, [psS])
                    if decay:
                        B.I("vector", "scalar_tensor_tensor", S.ap[:, i, hs], S.ap[:, i, hs], self._eend_ap(sm, h, Q), psS.ap[:, 0:Vd],
                            ALU.mult, ALU.add, reads=[("S", h), sm, psS], writes=[("S", h)])
                    else:
                        B.I("vector", "scalar_tensor_tensor", S.ap[:, i, hs], S.ap[:, i, hs], EE(qb, c, dr), psS.ap[:, 0:Vd],
                            ALU.mult, ALU.add, reads=[("S", h), psS, ("EE",)], writes=[("S", h)])
                    B.I("gpsimd", "tensor_copy", Sb.ap[:, i, hs], S.ap[:, i, hs], reads=[("S", h)], writes=[("Sb", h)])
            if dr == 1:
                yf = yp.next()
                B.dma(yf.ap[0:Q], self.Y[t0:t0 + Q, 0:VT], [], [yf])
                B.I("gpsimd", "tensor_tensor", yc.ap[0:Q], yc.ap[0:Q], yf.ap[0:Q], ALU.add, reads=[yc, yf], writes=[yc])
            B.dma(self.Y[t0:t0 + Q, 0:VT], yc.ap[0:Q], [yc], [])
        B.barrier()
    B.release(m)


def _eend_ap(self, sm, h, Q):
    return sm.ap[:, 4, h:h + 1]


Builder._eend_ap = _eend_ap


def _transpose_blocks2(self, src_fn, nblk, dst, dt_in):
    B = self
    m = B.mark()
    sp = Pool(B, 2, [128, T], dt_in, "tsrc")
    op = Pool(B, 3, [128, 1024], BF16, "tdst")
    ident = self.ident_b if dt_in == BF16 else self.ident_f
    nper = 8 if dt_in == BF16 else 4
    for b in range(nblk):
        s = sp.next()
        B.dma(s.ap, src_fn(b), [], [s])
        for c0 in range(0, NT, nper):
            nt = min(nper, NT - c0)
            ps = B.bank()
            pv = ps.ap.bitcast(BF16) if dt_in == BF16 else ps.ap
            for i in range(nt):
                B.I("tensor", "transpose", pv[:, i * 128:(i + 1) * 128], s.ap[:, (c0 + i) * 128:(c0 + i + 1) * 128], ident.ap,
                    reads=[s, ident], writes=[ps])
            o = op.next()
            B.I("vector", "tensor_copy", o.ap[:, 0:nt * 128], pv[:, 0:nt * 128], reads=[ps], writes=[o])
            B.dma(dst[c0 * 128:(c0 + nt) * 128, b * 128:(b + 1) * 128].rearrange("(c t) f -> t c f", t=128),
                  o.ap[:, 0:nt * 128].rearrange("t (c f) -> t c f", f=128), [o], [])
    B.release(m)


def _ret_mixer(self, li, last):
    B = self
    _inproj_dump(self, self.d["ret_w_in"], 96)
    m = B.mark()
    cos = B.alloc([128, T], F32, "cos"); sin = B.alloc([128, T], F32, "sin")
    B.dma(cos.ap, self.d["rope_cos"], [], [cos])
    B.dma(sin.ap, self.d["rope_sin"], [], [sin])
    ap_ = Pool(B, 2, [128, T], F32, "ra"); bp_ = Pool(B, 2, [128, T], F32, "rb")
    t1p = Pool(B, 2, [128, T], F32, "t1"); t2p = Pool(B, 2, [128, T], F32, "t2")
    oap = Pool(B, 2, [128, T], BF16, "oa"); obp = Pool(B, 2, [128, T], BF16, "ob")
    for h in range(8):
        for base, dst, scale in ((0, self.QT, 1.0), (16, self.KT, 1.0 / 16.0)):
            a = ap_.next(); b = bp_.next(); t1 = t1p.next(); t2 = t2p.next(); oa = oap.next(); ob = obp.next()
            B.dma(a.ap, self.U[base + 2 * h], [], [a])
            B.dma(b.ap, self.U[base + 2 * h + 1], [], [b])
            B.I("vector", "tensor_tensor", t1.ap, a.ap, cos.ap, ALU.mult, reads=[a, cos], writes=[t1])
            B.I("gpsimd", "tensor_tensor", t2.ap, b.ap, sin.ap, ALU.mult, reads=[b, sin], writes=[t2])
            B.I("vector", "tensor_tensor", t1.ap, t1.ap, t2.ap, ALU.subtract, reads=[t1, t2], writes=[t1])
            B.I("scalar", "activation", oa.ap, t1.ap, AF.Copy, scale=scale, reads=[t1], writes=[oa])
            B.dma(dst[2 * h], oa.ap, [oa], [])
            B.I("vector", "tensor_tensor", t1.ap, b.ap, cos.ap, ALU.mult, reads=[b, cos, t1], writes=[t1])
            B.I("gpsimd", "tensor_tensor", t2.ap, a.ap, sin.ap, ALU.mult, reads=[a, sin, t2], writes=[t2])
            B.I("vector", "tensor_tensor", t1.ap, t1.ap, t2.ap, ALU.add, reads=[t1, t2], writes=[t1])
            B.I("scalar", "activation", ob.ap, t1.ap, AF.Copy, scale=scale, reads=[t1], writes=[ob])
            B.dma(dst[2 * h + 1], ob.ap, [ob], [])
    B.release(m)
    _transpose_blocks2(self, lambda b: self.KT[b], 16, self.KK, BF16)
    _transpose_blocks2(self, lambda b: self.U[32 + b], 32, self.VV, F32)
    _scan(self, 8, 2, lambda h: [2 * h, 2 * h + 1], 512, True, False, LA=self.d["ret_la"])
    m = B.mark()
    yp = Pool(B, 2, [128, 4096], F32, "ey")
    ynp = Pool(B, 2, [128, 4096], BF16, "eyn")
    gp = Pool(B, 2, [128, 32, 128], F32, "eg")
    op = Pool(B, 2, [128, 32, 128], BF16, "eo")
    stp = Pool(B, 2, [128, 4, 8], F32, "est")
    junk = B.alloc([128, 512], F32, "junk")
    for c in range(2 if last else 0, NT):
        t0 = c * 128
        y = yp.next(); yn = ynp.next(); g = gp.next(); o = op.next(); st = stp.next()
        B.dma(y.ap, self.Y[t0:t0 + 128, :], [], [y])
        B.dma(g.ap, self.U[64:96, :, t0:t0 + 128].rearrange("b p t -> p b t"), [], [g])
        B.I("scalar", "activation", g.ap, g.ap, AF.Silu, reads=[g], writes=[g])
        mu, nmu, var, rstd = [st.ap[:, i, :] for i in range(4)]
        B.I("vector", "tensor_reduce", mu, y.ap.rearrange("p (h d) -> p h d", h=8), AX.X, ALU.add, reads=[y], writes=[st])
        B.I("vector", "tensor_scalar", nmu, mu, -1.0 / 512, None, ALU.mult, reads=[st], writes=[st])
        for h in range(8):
            B.I("scalar", "activation", junk.ap, y.ap[:, h * 512:(h + 1) * 512], AF.Square, bias=nmu[:, h:h + 1],
                accum_out=var[:, h:h + 1], reads=[y, st, junk], writes=[junk, st])
        B.I("scalar", "activation", rstd, var, AF.Ln, bias=self.eps_t.ap, scale=1.0 / 512, reads=[st, self.eps_t], writes=[st])
        B.I("scalar", "activation", rstd, rstd, AF.Exp, scale=-0.5, reads=[st], writes=[st])
        for h in range(8):
            B.I("vector", "tensor_scalar", yn.ap[:, h * 512:(h + 1) * 512], y.ap[:, h * 512:(h + 1) * 512],
                nmu[:, h:h + 1], rstd[:, h:h + 1], ALU.add, ALU.mult, reads=[y, st, yn], writes=[yn])
        for b0 in range(0, 32, 8):
            ps = B.bank()
            pv = ps.ap.bitcast(BF16)
            for i in range(8):
                B.I("tensor", "transpose", pv[:, i * 128:(i + 1) * 128], yn.ap[:, (b0 + i) * 128:(b0 + i + 1) * 128],
                    self.ident_b.ap, reads=[yn, self.ident_b], writes=[ps])
            B.I("vector", "tensor_tensor", o.ap[:, b0:b0 + 8, :], pv[:, 0:1024].rearrange("p (b t) -> p b t", b=8),
                g.ap[:, b0:b0 + 8, :], ALU.mult, reads=[ps, g, o], writes=[o])
        B.dma(self.YT[:, t0:t0 + 128].rearrange("(b p) t -> p b t", p=128), o.ap, [o], [])
    B.release(m)
    return self.d["ret_w_out"], 2, self.YT


def _mixer(self, li, last):
    return [_ssd_mixer, _ret_mixer, _hgrn_mixer, _gdn_mixer][li](self, li, last)


def host_mixer_inputs(inputs, b, layers=(0, 1, 2, 3)):
    f = np.float32
    m = {}
    idx = np.arange(128)
    tri0 = (idx[:, None] <= idx[None, :]).astype(f)
    tri1 = (idx[:, None] >= idx[None, :]).astype(f)
    m["tri"] = np.stack([tri0, tri1])
    m["neg"] = np.stack([(1 - tri0) * NEGV, (1 - tri1) * NEGV]).astype(f)
    if 0 in layers:
        m["ssd_w_in"] = np.asarray(inputs["ssd_w_in"], f)[0]
        m["ssd_w_out"] = np.asarray(inputs["ssd_w_out"], f)[0]
        cw = np.asarray(inputs["ssd_conv_w"], f)[0]
        m["ssd_cw"] = np.ascontiguousarray(cw.reshape(3, 48, 128).transpose(2, 1, 0))
        m["ssd_cb"] = np.ascontiguousarray(np.asarray(inputs["ssd_conv_b"], f)[0].reshape(48, 128).T)
        m["ssd_par"] = np.ascontiguousarray(np.stack([np.asarray(inputs["ssd_dt_bias"], f)[0].reshape(128),
                                                      np.asarray(inputs["ssd_a_log"], f)[0].reshape(128)], -1))
        m["ssd_dbc"] = np.ascontiguousarray(np.broadcast_to(np.asarray(inputs["ssd_d"], f)[0][None, :], (128, 64)))
        m["ssd_ng"] = np.ascontiguousarray(np.asarray(inputs["ssd_norm_g"], f)[0].reshape(32, 128).T)
    if 3 in layers:
        m["gdn_w_in"] = np.asarray(inputs["gdn_w_in"], f)[0]
        m["gdn_w_out"] = np.asarray(inputs["gdn_w_out"], f)[0]
        cw = np.asarray(inputs["gdn_conv_w"], f)[0]
        m["gdn_cw"] = np.ascontiguousarray(cw.reshape(3, 64, 128).transpose(2, 1, 0))
        par = np.zeros((128, 3), f)
        par[64:, 0] = np.asarray(inputs["gdn_dt_bias"], f)[0].reshape(64)
        par[64:, 1] = np.asarray(inputs["gdn_a_log"], f)[0].reshape(64)
        par[64:, 2] = -1.0
        m["gdn_par"] = par
        m["gdn_ng"] = np.ascontiguousarray(np.broadcast_to(np.asarray(inputs["gdn_norm_g"], f)[0][:, None], (128, 32)))
        st0 = (idx[None, :] < idx[:, None]).astype(f)
        st1 = (idx[None, :] > idx[:, None]).astype(f)
        bd = lambda n: (idx[:, None] // n == idx[None, :] // n).astype(f)
        m["bmask"] = np.stack([bd(16), bd(32) - bd(16), bd(64) - bd(32), 1 - bd(64)]).astype(f)
        m["pos"] = np.stack([(1 - st0) * (-NEGV), (1 - st1) * (-NEGV)]).astype(f)
    if 2 in layers:
        m["hgrn_w_in"] = np.asarray(inputs["hgrn_w_in"], f)[0]
        m["hgrn_w_out"] = np.asarray(inputs["hgrn_w_out"], f)[0]
        m["hgrn_lbl"] = np.ascontiguousarray(np.asarray(inputs["hgrn_lb_logits"], f).reshape(4, 16, 128).transpose(2, 1, 0))
        m["hgrn_ng"] = np.ascontiguousarray(np.asarray(inputs["hgrn_norm_g"], f)[0].reshape(16, 128).T)
    if 1 in layers:
        w = np.asarray(inputs["ret_w_in"], f)[0]
        perm = []
        for h in range(8):
            base = h * 256
            perm += list(range(base, base + 64)) + list(range(base + 128, base + 192))
            perm += list(range(base + 64, base + 128)) + list(range(base + 192, base + 256))
        perm = np.array(perm)
        cols = np.concatenate([perm, 2048 + perm, np.arange(4096, 12288)])
        m["ret_w_in"] = np.ascontiguousarray(w[:, cols])
        m["ret_w_out"] = np.asarray(inputs["ret_w_out"], f)[0]
        ld = np.asarray(inputs["ret_log_decay"], f)[0]
        m["ret_la"] = np.ascontiguousarray(np.broadcast_to(ld[:, None, :], (2, T, 8)))
        n = np.arange(T - LC)
        row = (n // 64).astype(f); col = (n % 64).astype(f)
        inv = (10000.0 ** (-np.arange(0, 128, 2, dtype=f) / 128)).astype(f)
        ang = np.concatenate([row[None, :] * inv[:, None], col[None, :] * inv[:, None]], 0)
        cos = np.ones((128, T), f); sin = np.zeros((128, T), f)
        cos[:, LC:] = np.cos(ang); sin[:, LC:] = np.sin(ang)
        m["rope_cos"] = cos; m["rope_sin"] = sin
    return m


def _conv_blocks(self, blocks, cw, cb, sink, silu=True):
    B = self
    m = B.mark()
    up = Pool(B, 2, [128, T], F32, "cu")
    op = Pool(B, 2, [128, T], F32, "co")
    for i, blk in enumerate(blocks):
        u = up.next(); o = op.next()
        B.dma(u.ap, self.U[blk], [], [u])
        bias = cb.ap[:, i:i + 1] if cb is not None else self.zero_t.ap
        B.I("scalar", "activation", o.ap, u.ap, AF.Identity, bias=bias, scale=cw.ap[:, i, 1:2],
            reads=[u, cw] + ([cb] if cb is not None else [self.zero_t]), writes=[o])
        for lo, hi in ((0, LC), (LC, T)):
            B.I("vector", "scalar_tensor_tensor", o.ap[:, lo + 1:hi], u.ap[:, lo:hi - 1], cw.ap[:, i, 0:1], o.ap[:, lo + 1:hi],
                ALU.mult, ALU.add, reads=[u, cw, o], writes=[o])
            B.I("vector", "scalar_tensor_tensor", o.ap[:, lo:hi - 1], u.ap[:, lo + 1:hi], cw.ap[:, i, 2:3], o.ap[:, lo:hi - 1],
                ALU.mult, ALU.add, reads=[u, cw, o], writes=[o])
        if silu:
            B.I("scalar", "activation", o.ap, o.ap, AF.Silu, reads=[o], writes=[o])
        sink(i, blk, o)
    B.release(m)


def _ssd_mixer(self, li, last):
    B = self
    _inproj_dump(self, self.d["ssd_w_in"], 81)
    m0 = B.mark()
    cw = B.alloc([128, 48, 3], F32, "cw"); cb = B.alloc([128, 48], F32, "cb")
    B.dma(cw.ap, self.d["ssd_cw"], [], [cw]); B.dma(cb.ap, self.d["ssd_cb"], [], [cb])
    obp = Pool(B, 2, [128, T], BF16, "cob")
    self.XS = self.P.dram("XS", [32, 128, T], BF16) if not hasattr(self, "XS") else self.XS

    def sink(i, blk, o):
        ob = obp.next()
        B.I("gpsimd", "tensor_copy", ob.ap, o.ap, reads=[o], writes=[ob])
        if blk < 64:
            B.dma(self.XS[blk - 32], ob.ap, [ob], [])
        elif blk < 72:
            B.dma(self.KT[blk - 64], ob.ap, [ob], [])
        else:
            B.dma(self.QT[blk - 72], ob.ap, [ob], [])
    _conv_blocks(self, list(range(32, 80)), cw, cb, sink)
    m = B.mark()
    par = B.alloc([128, 4], F32, "par")
    B.dma(par.ap[:, 0:2], self.d["ssd_par"], [], [par])
    B.I("scalar", "activation", par.ap[:, 2:3], par.ap[:, 1:2], AF.Exp, reads=[par], writes=[par])
    B.I("vector", "tensor_scalar", par.ap[:, 2:3], par.ap[:, 2:3], -1.0, None, ALU.mult, reads=[par], writes=[par])
    B.I("vector", "memset", par.ap[:, 3:4], 1.0, reads=[par], writes=[par])
    u = B.alloc([128, T], F32, "dtu"); la = B.alloc([128, T], F32, "dtla")
    B.dma(u.ap, self.U[80], [], [u])
    B.I("scalar", "activation", u.ap, u.ap, AF.Exp, bias=par.ap[:, 0:1], reads=[u, par], writes=[u])
    B.I("scalar", "activation", u.ap, u.ap, AF.Ln, bias=par.ap[:, 3:4], reads=[u, par], writes=[u])
    B.I("vector", "tensor_scalar", la.ap, u.ap, par.ap[:, 2:3], None, ALU.mult, reads=[u, par], writes=[la])
    B.I("scalar", "activation", u.ap, u.ap, AF.Ln, reads=[u], writes=[u])
    op = Pool(B, 2, [128, 4, 128], F32, "dto")
    for src, dst in ((la, self.LA), (u, self.WT)):
        for c0 in range(0, NT, 4):
            nt = min(4, NT - c0)
            ps = B.bank()
            for i in range(nt):
                B.I("tensor", "transpose", ps.ap[:, i * 128:(i + 1) * 128], src.ap[:, (c0 + i) * 128:(c0 + i + 1) * 128],
                    self.ident_f.ap, reads=[src, self.ident_f], writes=[ps])
            o = op.next()
            B.I("vector", "tensor_copy", o.ap[:, 0:nt, :], ps.ap[:, 0:nt * 128].rearrange("p (c f) -> p c f", f=128), reads=[ps], writes=[o])
            for dr in range(2):
                B.dma(dst[dr, c0 * 128:(c0 + nt) * 128, 0:64].rearrange("(c t) h -> t c h", t=128),
                      o.ap[:, 0:nt, dr * 64:(dr + 1) * 64], [o], [])
    B.release(m)
    B.release(m0)
    _transpose_blocks2(self, lambda b: self.KT[b], 8, self.KK, BF16)
    _transpose_blocks2(self, lambda b: self.XS[b], 32, self.VV, BF16)
    _scan(self, 64, 1, lambda h: [h // 8], 64, True, True)
    m = B.mark()
    dsk = B.alloc([128, 64], F32, "dsk")
    B.dma(dsk.ap, self.d["ssd_dbc"], [], [dsk])
    ng = B.alloc([128, 32], F32, "sng")
    B.dma(ng.ap, self.d["ssd_ng"], [], [ng])
    yp = Pool(B, 2, [128, 4096], F32, "ey")
    xp = Pool(B, 2, [128, 4096], BF16, "ex")
    zp = Pool(B, 2, [128, 32, 128], F32, "ez")
    gp = Pool(B, 1, [128, 32, 128], F32, "eg")
    sqp = Pool(B, 1, [128, 32, 128], F32, "esq")
    rsp = Pool(B, 2, [128, 8, 128], F32, "ers")
    op = Pool(B, 2, [128, 32, 128], BF16, "eo")
    for c in range(2 if last else 0, NT):
        t0 = c * 128
        y = yp.next(); xs = xp.next(); z = zp.next(); g = gp.next(); sq = sqp.next(); rs = rsp.next(); o = op.next()
        B.dma(y.ap, self.Y[t0:t0 + 128, :], [], [y])
        B.dma(xs.ap, self.VV[t0:t0 + 128, :], [], [xs])
        B.dma(z.ap, self.U[0:32, :, t0:t0 + 128].rearrange("b p t -> p b t"), [], [z])
        B.I("scalar", "activation", z.ap, z.ap, AF.Silu, reads=[z], writes=[z])
        B.I("gpsimd", "tensor_tensor", sq.ap.rearrange("p b t -> p (b t)").rearrange("p (h d) -> p h d", h=64),
            xs.ap.rearrange("p (h d) -> p h d", h=64), dsk.ap.unsqueeze(2).to_broadcast([128, 64, 64]), ALU.mult,
            reads=[xs, dsk, sq], writes=[sq])
        B.I("vector", "tensor_tensor", y.ap, y.ap, sq.ap.rearrange("p b t -> p (b t)"), ALU.add, reads=[y, sq], writes=[y])
        for b0 in range(0, 32, 4):
            ps = B.bank()
            for i in range(4):
                B.I("tensor", "transpose", ps.ap[:, i * 128:(i + 1) * 128], y.ap[:, (b0 + i) * 128:(b0 + i + 1) * 128],
                    self.ident_f.ap, reads=[y, self.ident_f], writes=[ps])
            B.I("vector", "tensor_tensor", g.ap[:, b0:b0 + 4, :], ps.ap.rearrange("p (b t) -> p b t", b=4),
                z.ap[:, b0:b0 + 4, :], ALU.mult, reads=[ps, z, g], writes=[g])
        B.I("scalar", "activation", sq.ap, g.ap, AF.Square, reads=[g, sq], writes=[sq])
        for half in range(2):
            ps = B.bank()
            for gi in range(4):
                G = half * 4 + gi
                for b in range(4):
                    B.mm(ps.ap[:, gi * 128:(gi + 1) * 128], self.ones_f.ap, sq.ap[:, G * 4 + b, :], b == 0, b == 3,
                         [self.ones_f, sq], [ps])
            B.I("scalar", "activation", rs.ap[:, half * 4:half * 4 + 4, :], ps.ap.rearrange("p (g t) -> p g t", g=4), AF.Ln,
                bias=self.eps_t.ap, scale=1.0 / 512, reads=[ps, self.eps_t, rs], writes=[rs])
        B.I("scalar", "activation", rs.ap, rs.ap, AF.Exp, scale=-0.5, reads=[rs], writes=[rs])
        for b in range(32):
            B.I("vector", "scalar_tensor_tensor", o.ap[:, b, :], g.ap[:, b, :], ng.ap[:, b:b + 1], rs.ap[:, b // 4, :],
                ALU.mult, ALU.mult, reads=[g, ng, rs, o], writes=[o])
        B.dma(self.YT[:, t0:t0 + 128].rearrange("(b p) t -> p b t", p=128), o.ap, [o], [])
    B.release(m)
    return self.d["ssd_w_out"], 2, self.YT


def _hgrn_mixer(self, li, last):
    B = self
    P = self.P
    _inproj_dump(self, self.d["hgrn_w_in"], 80)
    if not hasattr(self, "QT2"):
        self.QT2 = P.dram("QT2", [16, 128, T], BF16); self.KT2 = P.dram("KT2", [16, 128, T], BF16)
        self.KK2 = P.dram("KK2", [T, 2048], BF16); self.KS = P.dram("KS", [2, 16, 128, T], BF16)
    ee = B.alloc([128, 2, 16, 36], F32, "ee")
    m = B.mark()
    lg = B.alloc([128, 16, 4], F32, "lg"); lbt = B.alloc([128, 4, 16], F32, "lbt")
    B.dma(lg.ap, self.d["hgrn_lbl"], [], [lg])
    B.I("scalar", "activation", lg.ap, lg.ap, AF.Exp, reads=[lg], writes=[lg])
    ssum, lb, oml, noml = [lbt.ap[:, i, :] for i in range(4)]
    B.I("vector", "tensor_reduce", ssum, lg.ap, AX.X, ALU.add, reads=[lg], writes=[lbt])
    B.I("vector", "reciprocal", ssum, ssum, reads=[lbt], writes=[lbt])
    B.I("vector", "tensor_tensor", lb, lg.ap[:, :, 1], lg.ap[:, :, 2], ALU.add, reads=[lg, lbt], writes=[lbt])
    B.I("vector", "tensor_tensor", lb, lb, ssum, ALU.mult, reads=[lbt], writes=[lbt])
    B.I("vector", "tensor_scalar", oml, lb, -1.0, 1.0, ALU.mult, ALU.add, reads=[lbt], writes=[lbt])
    B.I("vector", "tensor_scalar", noml, oml, -1.0, None, ALU.mult, reads=[lbt], writes=[lbt])
    qp = Pool(B, 2, [128, T], F32, "hq"); fp = Pool(B, 2, [128, T], F32, "hf")
    kgp = Pool(B, 2, [128, T], F32, "hkg"); lfp = Pool(B, 2, [128, T], F32, "hlf")
    cp = Pool(B, 2, [128, T], F32, "hc"); clp = Pool(B, 2, [128, T], F32, "hcl")
    ep = Pool(B, 2, [128, T], F32, "he"); obp = Pool(B, 3, [128, T], BF16, "hob")
    totp = Pool(B, 2, [128, 36], F32, "htot")
    v3 = lambda t: t.ap.rearrange("p (c q) -> p c q", q=64)
    for b in range(16):
        q = qp.next()
        B.dma(q.ap, self.U[b], [], [q])
        for dr in range(2):
            f = fp.next(); kg = kgp.next(); lf = lfp.next(); C = cp.next(); cl = clp.next(); e = ep.next(); tot = totp.next()
            B.dma(f.ap, self.U[16 + 16 * dr + b], [], [f])
            B.I("scalar", "activation", f.ap, f.ap, AF.Sigmoid, reads=[f], writes=[f])
            B.I("vector", "tensor_scalar", lf.ap, f.ap, oml[:, b:b + 1], lb[:, b:b + 1], ALU.mult, ALU.add, reads=[f, lbt], writes=[lf])
            B.I("scalar", "activation", lf.ap, lf.ap, AF.Ln, reads=[lf], writes=[lf])
            B.I("gpsimd", "tensor_scalar", kg.ap, f.ap, noml[:, b:b + 1], oml[:, b:b + 1], ALU.mult, ALU.add, reads=[f, lbt], writes=[kg])
            B.I("vector", "tensor_tensor_scan", C.ap, lf.ap, lf.ap, 0.0, ALU.add, ALU.bypass, reads=[lf], writes=[C])
            B.I("vector", "tensor_tensor", tot.ap, v3(C)[:, :, 63], v3(C)[:, :, 0], ALU.subtract, reads=[C], writes=[tot])
            B.I("vector", "tensor_tensor", tot.ap, tot.ap, v3(lf)[:, :, 0], ALU.add, reads=[tot, lf], writes=[tot])
            if dr == 0:
                B.I("vector", "tensor_tensor", cl.ap[:, 0:36], v3(C)[:, :, 0], v3(lf)[:, :, 0], ALU.subtract, reads=[C, lf], writes=[cl])
                B.I("vector", "tensor_tensor", v3(e), v3(C), cl.ap[:, 0:36].unsqueeze(2).to_broadcast([128, 36, 64]), ALU.subtract,
                    reads=[C, cl], writes=[e])
                B.I("vector", "tensor_copy", cl.ap, e.ap, reads=[e, cl], writes=[cl])
            else:
                B.I("vector", "tensor_tensor", v3(cl), v3(C)[:, :, 63:64].to_broadcast([128, 36, 64]), v3(C), ALU.subtract,
                    reads=[C], writes=[cl])
                B.I("vector", "tensor_tensor", cl.ap, cl.ap, lf.ap, ALU.add, reads=[cl, lf], writes=[cl])
            B.I("scalar", "activation", ee.ap[:, dr, b, :], tot.ap, AF.Exp, reads=[tot, ("EE",)], writes=[("EE",)])
            B.I("scalar", "activation", e.ap, cl.ap, AF.Exp, reads=[cl, e], writes=[e])
            ob = obp.next()
            B.I("gpsimd", "tensor_tensor", ob.ap, q.ap, e.ap, ALU.mult, reads=[q, e], writes=[ob])
            B.dma((self.QT, self.QT2)[dr][b], ob.ap, [ob], [])
            B.I("scalar", "activation", e.ap, cl.ap, AF.Exp, scale=-1.0, reads=[cl, e], writes=[e])
            ob = obp.next()
            B.I("gpsimd", "tensor_tensor", ob.ap, kg.ap, e.ap, ALU.mult, reads=[kg, e], writes=[ob])
            B.dma((self.KT, self.KT2)[dr][b], ob.ap, [ob], [])
            B.I("vector", "tensor_tensor", v3(e), tot.ap.unsqueeze(2).to_broadcast([128, 36, 64]), v3(cl), ALU.subtract,
                reads=[tot, cl, e], writes=[e])
            B.I("scalar", "activation", e.ap, e.ap, AF.Exp, reads=[e], writes=[e])
            ob = obp.next()
            B.I("vector", "tensor_tensor", ob.ap, kg.ap, e.ap, ALU.mult, reads=[kg, e], writes=[ob])
            B.dma(self.KS[dr, b], ob.ap, [ob], [])
    B.release(m)
    _transpose_blocks2(self, lambda b: self.KS[0, b], 16, self.KK, BF16)
    _transpose_blocks2(self, lambda b: self.KS[1, b], 16, self.KK2, BF16)
    _transpose_blocks2(self, lambda b: self.U[48 + b], 16, self.VV, F32)
    _scan(self, 16, 1, lambda h: [h], 128, False, False, Q=64, EE=lambda qb, c, dr: ee.ap[:, dr, qb, c:c + 1],
          QTs=[self.QT, self.QT2], KTs=[self.KT, self.KT2], KKs=[self.KK, self.KK2])
    m = B.mark()
    ng = B.alloc([128, 16], F32, "hng")
    B.dma(ng.ap, self.d["hgrn_ng"], [], [ng])
    yp = Pool(B, 2, [128, 2048], F32, "ey"); sqp = Pool(B, 2, [128, 2048], F32, "esq")
    ynp = Pool(B, 2, [128, 2048], BF16, "eyn")
    gp = Pool(B, 2, [128, 16, 128], F32, "eg"); op = Pool(B, 2, [128, 16, 128], BF16, "eo")
    stp = Pool(B, 2, [128, 16], F32, "est")
    for c in range(2 if last else 0, NT):
        t0 = c * 128
        y = yp.next(); sq = sqp.next(); yn = ynp.next(); g = gp.next(); o = op.next(); st = stp.next()
        B.dma(y.ap, self.Y[t0:t0 + 128, 0:2048], [], [y])
        B.dma(g.ap, self.U[64:80, :, t0:t0 + 128].rearrange("b p t -> p b t"), [], [g])
        B.I("scalar", "activation", g.ap, g.ap, AF.Silu, reads=[g], writes=[g])
        B.I("scalar", "activation", sq.ap, y.ap, AF.Square, reads=[y], writes=[sq])
        B.I("vector", "tensor_reduce", st.ap, sq.ap.rearrange("p (h d) -> p h d", h=16), AX.X, ALU.add, reads=[sq], writes=[st])
        B.I("scalar", "activation", st.ap, st.ap, AF.Ln, bias=self.eps_t.ap, scale=1.0 / 128, reads=[st, self.eps_t], writes=[st])
        B.I("scalar", "activation", st.ap, st.ap, AF.Exp, scale=-0.5, reads=[st], writes=[st])
        B.I("vector", "tensor_tensor", yn.ap.rearrange("p (h d) -> p h d", h=16), y.ap.rearrange("p (h d) -> p h d", h=16),
            st.ap.unsqueeze(2).to_broadcast([128, 16, 128]), ALU.mult, reads=[y, st], writes=[yn])
        for b0 in range(0, 16, 8):
            ps = B.bank()
            pv = ps.ap.bitcast(BF16)
            for i in range(8):
                B.I("tensor", "transpose", pv[:, i * 128:(i + 1) * 128], yn.ap[:, (b0 + i) * 128:(b0 + i + 1) * 128],
                    self.ident_b.ap, reads=[yn, self.ident_b], writes=[ps])
            for i in range(8):
                b = b0 + i
                B.I("vector", "scalar_tensor_tensor", o.ap[:, b, :], pv[:, i * 128:(i + 1) * 128], ng.ap[:, b:b + 1], g.ap[:, b, :],
                    ALU.mult, ALU.mult, reads=[ps, ng, g, o], writes=[o])
        B.dma(self.YT[0:2048, t0:t0 + 128].rearrange("(b p) t -> p b t", p=128), o.ap, [o], [])
    B.release(m)
    B.release(B.mark())
    return self.d["hgrn_w_out"], 1, self.YT


def _epi_headnorm(self, nblk, gblk0, ng, last):
    B = self
    W = nblk * 128
    m = B.mark()
    yp = Pool(B, 2, [128, W], F32, "ey"); sqp = Pool(B, 1, [128, W], F32, "esq")
    ynp = Pool(B, 2, [128, W], BF16, "eyn")
    gp = Pool(B, 2, [128, nblk, 128], F32, "eg"); op = Pool(B, 2, [128, nblk, 128], BF16, "eo")
    stp = Pool(B, 2, [128, nblk], F32, "est")
    for c in range(2 if last else 0, NT):
        t0 = c * 128
        y = yp.next(); sq = sqp.next(); yn = ynp.next(); g = gp.next(); o = op.next(); st = stp.next()
        B.dma(y.ap, self.Y[t0:t0 + 128, 0:W], [], [y])
        B.dma(g.ap, self.U[gblk0:gblk0 + nblk, :, t0:t0 + 128].rearrange("b p t -> p b t"), [], [g])
        B.I("scalar", "activation", g.ap, g.ap, AF.Silu, reads=[g], writes=[g])
        B.I("scalar", "activation", sq.ap, y.ap, AF.Square, reads=[y], writes=[sq])
        B.I("vector", "tensor_reduce", st.ap, sq.ap.rearrange("p (h d) -> p h d", h=nblk), AX.X, ALU.add, reads=[sq], writes=[st])
        B.I("scalar", "activation", st.ap, st.ap, AF.Ln, bias=self.eps_t.ap, scale=1.0 / 128, reads=[st, self.eps_t], writes=[st])
        B.I("scalar", "activation", st.ap, st.ap, AF.Exp, scale=-0.5, reads=[st], writes=[st])
        B.I("vector", "tensor_tensor", yn.ap.rearrange("p (h d) -> p h d", h=nblk), y.ap.rearrange("p (h d) -> p h d", h=nblk),
            st.ap.unsqueeze(2).to_broadcast([128, nblk, 128]), ALU.mult, reads=[y, st], writes=[yn])
        for b0 in range(0, nblk, 8):
            ps = B.bank()
            pv = ps.ap.bitcast(BF16)
            for i in range(8):
                B.I("tensor", "transpose", pv[:, i * 128:(i + 1) * 128], yn.ap[:, (b0 + i) * 128:(b0 + i + 1) * 128],
                    self.ident_b.ap, reads=[yn, self.ident_b], writes=[ps])
            for i in range(8):
                b = b0 + i
                B.I("vector", "scalar_tensor_tensor", o.ap[:, b, :], pv[:, i * 128:(i + 1) * 128], ng.ap[:, b:b + 1], g.ap[:, b, :],
                    ALU.mult, ALU.mult, reads=[ps, ng, g, o], writes=[o])
        B.dma(self.YT[0:W, t0:t0 + 128].rearrange("(b p) t -> p b t", p=128), o.ap, [o], [])
    B.release(m)


def _gdn_mixer(self, li, last):
    B = self
    P = self.P
    _inproj_dump(self, self.d["gdn_w_in"], 97)
    if not hasattr(self, "BT"):
        self.BT = P.dram("BT", [2, T, 32], F32)
        self.VF = P.dram("VF", [32, 128, T], BF16)
    m0 = B.mark()
    cw = B.alloc([128, 64, 3], F32, "gcw")
    B.dma(cw.ap, self.d["gdn_cw"], [], [cw])
    obp = Pool(B, 2, [128, T], BF16, "gob")
    sqp = Pool(B, 2, [128, 512], F32, "gsq")
    rsp = Pool(B, 2, [128, 512], F32, "grs")

    def sink(i, blk, o):
        ob = obp.next()
        if blk < 32:
            for lo, hi in TG6:
                n = hi - lo
                sq = sqp.next(); rs = rsp.next()
                B.I("scalar", "activation", sq.ap[:, 0:n], o.ap[:, lo:hi], AF.Square, reads=[o], writes=[sq])
                ps = B.bank()
                B.mm(ps.ap[:, 0:n], self.ones_f.ap, sq.ap[:, 0:n], True, True, [self.ones_f, sq], [ps])
                B.I("scalar", "activation", rs.ap[:, 0:n], ps.ap[:, 0:n], AF.Ln, bias=self.eps_t.ap, reads=[ps, self.eps_t], writes=[rs])
                B.I("scalar", "activation", rs.ap[:, 0:n], rs.ap[:, 0:n], AF.Exp, scale=-0.5, reads=[rs], writes=[rs])
                B.I("vector", "scalar_tensor_tensor", ob.ap[:, lo:hi], o.ap[:, lo:hi], (128.0 ** -0.5) if blk < 16 else 1.0, rs.ap[:, 0:n],
                    ALU.mult, ALU.mult, reads=[o, rs, ob], writes=[ob])
            B.dma((self.QT[blk] if blk < 16 else self.KT[blk - 16]), ob.ap, [ob], [])
        else:
            B.I("gpsimd", "tensor_copy", ob.ap, o.ap, reads=[o], writes=[ob])
            B.dma(self.VF[blk - 32], ob.ap, [ob], [])
    _conv_blocks(self, list(range(0, 64)), cw, None, sink)
    m = B.mark()
    par = B.alloc([128, 4], F32, "gpar")
    B.dma(par.ap[:, 0:3], self.d["gdn_par"], [], [par])
    B.I("scalar", "activation", par.ap[:, 1:2], par.ap[:, 1:2], AF.Exp, reads=[par], writes=[par])
    B.I("vector", "tensor_tensor", par.ap[:, 2:3], par.ap[:, 2:3], par.ap[:, 1:2], ALU.mult, reads=[par], writes=[par])
    B.I("vector", "memset", par.ap[:, 3:4], 1.0, reads=[par], writes=[par])
    u = B.alloc([128, T], F32, "gu"); sg = B.alloc([128, T], F32, "gsg")
    B.dma(u.ap, self.U[96], [], [u])
    B.I("scalar", "activation", sg.ap, u.ap, AF.Sigmoid, reads=[u], writes=[sg])
    B.I("scalar", "activation", u.ap, u.ap, AF.Exp, bias=par.ap[:, 0:1], reads=[u, par], writes=[u])
    B.I("scalar", "activation", u.ap, u.ap, AF.Ln, bias=par.ap[:, 3:4], reads=[u, par], writes=[u])
    B.I("vector", "tensor_scalar", u.ap, u.ap, par.ap[:, 2:3], None, ALU.mult, reads=[u, par], writes=[u])
    op = Pool(B, 2, [128, 4, 128], F32, "gdo")
    for src, dst, coff in ((u, self.LA, 64), (sg, self.BT, 0)):
        for c0 in range(0, NT, 4):
            nt = min(4, NT - c0)
            ps = B.bank()
            for i in range(nt):
                B.I("tensor", "transpose", ps.ap[:, i * 128:(i + 1) * 128], src.ap[:, (c0 + i) * 128:(c0 + i + 1) * 128],
                    self.ident_f.ap, reads=[src, self.ident_f], writes=[ps])
            o = op.next()
            B.I("vector", "tensor_copy", o.ap[:, 0:nt, :], ps.ap[:, 0:nt * 128].rearrange("p (c f) -> p c f", f=128), reads=[ps], writes=[o])
            for dr in range(2):
                B.dma(dst[dr, c0 * 128:(c0 + nt) * 128, 0:32].rearrange("(c t) h -> t c h", t=128),
                      o.ap[:, 0:nt, coff + dr * 32:coff + (dr + 1) * 32], [o], [])
    B.release(m)
    B.release(m0)
    _transpose_blocks2(self, lambda b: self.KT[b], 16, self.KK, BF16)
    _transpose_blocks2(self, lambda b: self.VF[b], 32, self.VV, BF16)
    mh = B.mark()
    pos = [B.alloc([128, 128], F32, "pos0"), B.alloc([128, 128], F32, "pos1")]
    for i in range(2):
        B.dma(pos[i].ap, self.d["pos"][i], [], [pos[i]])
    kkp = Pool(B, 2, [128, 128], F32, "gkk")
    Lp = Pool(B, 3, [128, 128], F32, "gL"); LTp = Pool(B, 3, [128, 128], F32, "gLT")
    Xp = Pool(B, 3, [128, 128], F32, "gX"); Ep = Pool(B, 2, [128, 128], F32, "gE")
    vnp = Pool(B, 3, [128, 128], BF16, "gvn")
    Mp = Pool(B, 32, [128, 128], F32, "gM")
    bmask = [B.alloc([128, 128], F32, f"bm{i}") for i in range(4)]
    for i in range(4):
        B.dma(bmask[i].ap, self.d["bmask"][i], [], [bmask[i]])
    state = {}

    def vhook(h, dr, qbs, kc, vc, la, sm, rh, Sb, hs):
        g = qbs[0]
        cs, ec = sm.ap[:, 0, :], sm.ap[:, 2, :]
        beta = la.ap[:, 1, :]
        if state.get("g") != (g, id(kc)):
            state["g"] = (g, id(kc))
            pk = B.bank()
            B.mm(pk.ap[:, 0:128], kc.ap[:, g, :], kc.ap[:, g, :], True, True, [kc], [pk])
            kk = kkp.next()
            B.I("scalar", "copy", kk.ap, pk.ap[:, 0:128], reads=[pk], writes=[kk])
            state["kk"] = kk
        kk = state["kk"]
        p2 = B.bank()
        B.mm(p2.ap[:, 0:128], self.ones_f.ap, rh.ap, True, False, [self.ones_f, rh], [p2])
        B.mm(p2.ap[:, 0:128], self.ident_f.ap, pos[dr].ap, False, True, [self.ident_f, pos[dr]], [p2])
        E = Ep.next()
        B.I("scalar", "activation", E.ap, p2.ap[:, 0:128], AF.Exp, bias=cs[:, h:h + 1], scale=-1.0, reads=[p2, sm], writes=[E])
        L = Lp.next()
        B.I("vector", "scalar_tensor_tensor", L.ap, E.ap, beta[:, h:h + 1], kk.ap, ALU.mult, ALU.mult, reads=[E, la, kk], writes=[L])
        pt = B.bank()
        B.I("tensor", "transpose", pt.ap[:, 0:128], L.ap, self.ident_f.ap, reads=[L, self.ident_f], writes=[pt])
        LT = LTp.next()
        B.I("scalar", "copy", LT.ap, pt.ap[:, 0:128], reads=[pt], writes=[LT])
        pk2 = B.bank()
        B.mm(pk2.ap[:, 0:128], kc.ap[:, g, :], Sb.ap[:, 0, hs], True, True, [kc, ("Sb", h)], [pk2])
        X = Xp.next()
        B.I("vector", "tensor_scalar", X.ap, pk2.ap[:, 0:128], ec[:, h:h + 1], -1.0, ALU.mult, ALU.mult, reads=[pk2, sm], writes=[X])
        B.I("vector", "tensor_tensor", X.ap, X.ap, vc.ap[:, hs], ALU.add, reads=[X, vc], writes=[X])
        B.I("vector", "tensor_scalar", X.ap, X.ap, beta[:, h:h + 1], None, ALU.mult, reads=[X, la], writes=[X])
        def msk(src, mi):
            t = Mp.next()
            B.I("gpsimd", "tensor_tensor", t.ap, src.ap, bmask[mi].ap, ALU.mult, reads=[src, bmask[mi]], writes=[t])
            return t

        def mmf(a, b):
            ps = B.bank()
            B.mm(ps.ap[:, 0:128], a.ap, b.ap, True, True, [a, b], [ps])
            return ps

        def ev_copy(ps):
            t = Mp.next()
            B.I("scalar", "copy", t.ap, ps.ap[:, 0:128], reads=[ps], writes=[t])
            return t

        def ev_add(base, ps, op):
            t = Mp.next()
            B.I("vector", "tensor_tensor", t.ap, base.ap, ps.ap[:, 0:128], op, reads=[base, ps], writes=[t])
            return t
        Dm = msk(L, 0); DTm = msk(LT, 0)
        N = Mp.next(); NT = Mp.next()
        B.I("vector", "tensor_tensor", N.ap, self.ident_f.ap, Dm.ap, ALU.subtract, reads=[self.ident_f, Dm], writes=[N])
        B.I("vector", "tensor_tensor", NT.ap, self.ident_f.ap, DTm.ap, ALU.subtract, reads=[self.ident_f, DTm], writes=[NT])
        Pm, PT = Dm, DTm
        for lev in range(3):
            P2 = ev_copy(mmf(PT, Pm))
            P2T = ev_copy(mmf(Pm, PT)) if lev < 2 else None
            a = mmf(NT, P2); b = mmf(P2, NT)
            N2 = ev_add(N, a, ALU.add); NT2 = ev_add(NT, b, ALU.add)
            N, NT = N2, NT2
            Pm, PT = P2, P2T
        Mm, MT = N, NT
        for lev in range(1, 4):
            C = msk(L, lev); CT = msk(LT, lev)
            if lev < 3:
                Q1 = ev_copy(mmf(CT, Mm))
                Mn = ev_add(Mm, mmf(MT, Q1), ALU.subtract)
            R1 = ev_copy(mmf(C, MT))
            MTn = ev_add(MT, mmf(Mm, R1), ALU.subtract)
            if lev < 3:
                Mm = Mn
            MT = MTn
        px = B.bank()
        B.mm(px.ap[:, 0:128], MT.ap, X.ap, True, True, [MT, X], [px])
        X = px
        vn = vnp.next()
        B.I("scalar", "copy", vn.ap, X.ap[:, 0:128], reads=[X], writes=[vn])
        return vn.ap, [vn]
    _scan(self, 32, 1, lambda h: [h // 2], 128, True, False, aux=self.BT, vhook=(vhook if getattr(self, "use_hook", True) else None))
    B.release(mh)
    m = B.mark()
    ng = B.alloc([128, 32], F32, "gng")
    B.dma(ng.ap, self.d["gdn_ng"], [], [ng])
    _epi_headnorm(self, 32, 64, ng, last)
    B.release(m)
    return self.d["gdn_w_out"], 2, self.YT


def build_program(nlayers=4, stub=False, debug_xt=False, layers=None):
    B = Builder()
    B.dbg_mlp = debug_xt
    _tok_init(B, nlayers)
    layers = list(range(nlayers)) if layers is None else layers
    if not stub:
        _mix_init(B, layers)
    for li in layers:
        last = (li == 3)
        _mod(B, li)
        B.barrier()
        if not stub:
            _set_gain(B, 0)
            B.barrier()
            W, nkq, YT = _mixer(B, li, last)
            B.barrier()
            _outproj(B, li, W, nkq, YT, last)
        _mlp(B, li, last)
        if debug_xt:
            dl = B.P.dram(f"dbgL{li}", [D, T], F32, kind="ExternalOutput")
            B.barrier()
            for k in range(4):
                B.dma(dl[k * 512:(k + 1) * 512, :], B.XT[k * 512:(k + 1) * 512, :], [], [("dbgL", li, k)])
            B.dbgl_keys = getattr(B, "dbgl_keys", []) + [("dbgL", li, k) for k in range(4)]
            B.barrier()
    if debug_xt:
        dm = B.P.dram("dbg_mod", [128, 96, 2], F32, kind="ExternalOutput")
        B.dma(dm, B.modT.ap, [B.modT], [("dbgm",)])
        B.dbg = B.P.dram("dbgXT", [D, T], F32, kind="ExternalOutput")
        for k in range(4):
            B.dma(B.dbg[k * 512:(k + 1) * 512, :], B.XT[k * 512:(k + 1) * 512, :], [], [("dbg", k)])
        B.out_keys = [("dbg", k) for k in range(4)] + [("dbgm",), ("dbgh2",), ("dbghid",)] + getattr(B, "dbgl_keys", [])
    else:
        _final(B)
    nc = B.P.emit(final_keys=B.out_keys)
    return B, nc


def host_inputs(inputs, b, nl=4):
    f = np.float32
    x = np.asarray(inputs["x"], f); ctx = np.asarray(inputs["ctx"], f)
    m = {}
    m["xT0"] = np.ascontiguousarray(np.concatenate([ctx[b], x[b]], axis=0).T)
    cv = np.stack([np.asarray(inputs["c"], f)[b], np.asarray(inputs["c_ctx"], f)], axis=-1)
    m["cvec"] = np.ascontiguousarray(cv.reshape(16, 128, 2).transpose(1, 0, 2))
    m["ada_w"] = np.asarray(inputs["ada_w"], f)[:nl]
    m["ada_bT"] = np.ascontiguousarray(np.asarray(inputs["ada_b"], f).reshape(4, 96, 128).transpose(0, 2, 1))[:nl]
    m["norm_gT"] = np.ascontiguousarray(np.asarray(inputs["norm_g"], f).reshape(4, 2, 16, 128).transpose(0, 1, 3, 2))[:nl]
    m["mlp_w1"] = np.asarray(inputs["mlp_w1"], f)[:nl]
    m["mlp_w2"] = np.asarray(inputs["mlp_w2"], f)[:nl]
    m["final_gT"] = np.ascontiguousarray(np.asarray(inputs["final_g"], f).reshape(16, 128).T)
    m["ident_f"] = np.eye(128, dtype=f)
    m["ones_f"] = np.ones((128, 128), f)
    return m


_CACHE = {}


def kernel(**inputs):
    if "prog" not in _CACHE:
        _CACHE["prog"] = build_program()
    B, nc = _CACHE["prog"]
    in_maps = []
    for b in range(4):
        m = host_inputs(inputs, b)
        m.update(host_mixer_inputs(inputs, b))
        in_maps.append(m)
    res = run_bass_kernel_spmd(nc, in_maps, core_ids=list(range(4)))
    out = np.stack([np.ascontiguousarray(res.results[b]["outT"].T) for b in range(4)], axis=0)
    return out.astype(np.float32)
```

```python
import numpy as np
from concourse.bass_utils import run_bass_kernel_spmd
from contextlib import ExitStack
import numpy as np
import concourse.bass as bass
import concourse.mybir as mybir

F32 = mybir.dt.float32
BF16 = mybir.dt.bfloat16
AF = mybir.ActivationFunctionType
ALU = mybir.AluOpType
AX = mybir.AxisListType

COMPUTE = ("tensor", "vector", "scalar", "gpsimd")
ISSUERS = ("sync", "gpsimd", "scalar")
KDMA = 8
EPOCH = 20000


class Prog:
    def __init__(self):
        self.nc = bass.Bass("TRN2", target_bir_lowering=False)
        self.ops = []
        self.stack = ExitStack()
        self.last_w = {}
        self.readers = {}
        self.n_sb = 0

    def sbuf(self, shape, dtype, name=None):
        self.n_sb += 1
        name = name or f"sb{self.n_sb}"
        return self.stack.enter_context(self.nc.sbuf_tensor(name, list(shape), dtype))

    def psum(self, shape, dtype, name=None):
        self.n_sb += 1
        name = name or f"ps{self.n_sb}"
        return self.stack.enter_context(self.nc.psum_tensor(name, list(shape), dtype))

    def dram(self, name, shape, dtype, kind="Internal"):
        return self.nc.dram_tensor(name, list(shape), dtype, kind=kind).ap()

    def _deps(self, reads, writes):
        deps = set()
        for k in reads:
            if k in self.last_w:
                deps.add(self.last_w[k])
        for k in writes:
            if k in self.last_w:
                deps.add(self.last_w[k])
            for r in self.readers.get(k, ()):
                deps.add(r)
        return deps

    def _commit(self, idx, reads, writes):
        for k in reads:
            self.readers.setdefault(k, []).append(idx)
        for k in writes:
            self.last_w[k] = idx
            self.readers[k] = []

    def op(self, eng, fn, reads=(), writes=(), floor=None, extra=()):
        idx = len(self.ops)
        deps = self._deps(reads, writes)
        deps.update(extra)
        if floor is not None:
            deps.add(floor)
        self.ops.append(dict(eng=eng, fn=fn, deps=deps, dma=False))
        self._commit(idx, reads, writes)
        return idx

    def dma(self, issuer, out, in_, reads=(), writes=(), floor=None, **kw):
        idx = len(self.ops)
        deps = self._deps(reads, writes)
        if floor is not None:
            deps.add(floor)
        self.ops.append(dict(eng=issuer, fn=lambda e: e.dma_start(out=out, in_=in_, **kw),
                             deps=deps, dma=True))
        self._commit(idx, reads, writes)
        return idx

    def mm(self, out, lhsT, rhs, start, stop, reads, writes, **kw):
        return self.op("tensor", lambda e: e.matmul(out, lhsT, rhs, start=start, stop=stop, **kw),
                       reads, writes)

    def emit(self, final_keys=()):
        nc = self.nc
        ops = self.ops
        has_dep = [False] * len(ops)
        for o in ops:
            for d in o["deps"]:
                has_dep[d] = True
        final_deps = set()
        for k in final_keys:
            if k in self.last_w:
                final_deps.add(self.last_w[k])
        for d in final_deps:
            has_dep[d] = True
        cnt = {e: 0 for e in COMPUTE}
        dcnt = {e: 0 for e in ISSUERS}
        for i, o in enumerate(ops):
            if o["dma"]:
                n = dcnt[o["eng"]]
                dcnt[o["eng"]] += 1
                o["dn"] = n
                o["done"] = (("d", o["eng"], n % KDMA), 16 * (n // KDMA + 1))
            else:
                if has_dep[i]:
                    c = cnt[o["eng"]]
                    cnt[o["eng"]] += 1
                    o["done"] = (("c", o["eng"], c // EPOCH), c % EPOCH + 1)
                    o["inc"] = True
                else:
                    o["done"] = None
                    o["inc"] = False
        semkeys = set()
        for o in ops:
            if o.get("done"):
                semkeys.add(o["done"][0])
        sems = {}
        for k in sorted(semkeys):
            sems[k] = self.stack.enter_context(nc.semaphore("s_" + "_".join(map(str, k))))
        streams = {}
        for i, o in enumerate(ops):
            streams.setdefault(o["eng"], []).append(i)
        stats = dict(waits=0)

        def run_engine(engname, e):
            waited = {}
            dma_hist = []
            for i in streams.get(engname, []):
                o = ops[i]
                need = {}
                for d in o["deps"]:
                    od = ops[d]
                    if od["done"] is None:
                        continue
                    if (not od["dma"]) and od["eng"] == "tensor" and engname == "tensor" and not o["dma"]:
                        continue
                    sk, v = od["done"]
                    if need.get(sk, 0) < v:
                        need[sk] = v
                if o["dma"]:
                    n = o["dn"]
                    if n >= KDMA:
                        sk = ("d", engname, n % KDMA)
                        v = 16 * (n // KDMA)
                        if need.get(sk, 0) < v:
                            need[sk] = v
                for sk, v in need.items():
                    if waited.get(sk, 0) >= v:
                        continue
                    e.wait_ge(sems[sk], v)
                    waited[sk] = v
                    stats["waits"] += 1
                ins = o["fn"](e)
                if o["dma"]:
                    ins.then_inc(sems[o["done"][0]], 16)
                elif o["inc"]:
                    ins.then_inc(sems[o["done"][0]], 1)
            if engname == "sync":
                need = {}
                for d in final_deps:
                    sk, v = ops[d]["done"]
                    if need.get(sk, 0) < v:
                        need[sk] = v
                for sk, v in need.items():
                    if waited.get(sk, 0) < v:
                        e.wait_ge(sems[sk], v)

        with nc.Block() as block:
            @block.sync
            def _(e):
                run_engine("sync", e)

            @block.tensor
            def _(e):
                run_engine("tensor", e)

            @block.vector
            def _(e):
                run_engine("vector", e)

            @block.scalar
            def _(e):
                run_engine("scalar", e)

            @block.gpsimd
            def _(e):
                run_engine("gpsimd", e)
        self.stats = stats
        self.stack.close()
        return nc


import math

T = 2304
LC = 256
D = 2048
KC = 16
NT = 18
TG6 = [(0, 256), (256, 768), (768, 1280), (1280, 1536), (1536, 2048), (2048, 2304)]
SETS = [(0, 1), (2, 3), (4, 5)]
ARENA_BYTES = 184 * 1024


def dsize(dt):
    return 4 if dt == F32 else 2


class Tl:
    def __init__(self, ap, key):
        self.ap = ap
        self.key = key

    def __getitem__(self, idx):
        return self.ap[idx]


class Pool:
    def __init__(self, B, n, shape, dtype, name):
        self.tiles = [B.alloc(shape, dtype, f"{name}{i}") for i in range(n)]
        self.i = 0

    def next(self):
        t = self.tiles[self.i % len(self.tiles)]
        self.i += 1
        return t


class Builder:
    def __init__(self):
        self.P = Prog()
        P = self.P
        self.arena = P.sbuf([128, ARENA_BYTES // 4], F32, name="arena")
        self.aoff = 0
        self.nkey = 0
        self.banks = [Tl(P.psum([128, 512], F32, name=f"bank{i}"), ("bank", i)) for i in range(8)]
        self.bi = 0
        self.bar_tile = P.sbuf([128, 8], F32, name="bar")
        self.floor = None

    def mark(self):
        return self.aoff

    def release(self, m):
        if self.aoff > m:
            self.barrier()
        self.aoff = m

    def alloc(self, shape, dtype, name="t"):
        n = 1
        for s in shape[1:]:
            n *= s
        nb = n * dsize(dtype)
        nb = (nb + 63) // 64 * 64
        assert self.aoff + nb <= ARENA_BYTES, f"arena overflow {name} {self.aoff + nb}"
        v = self.arena[:, self.aoff // 4:(self.aoff + nb) // 4]
        self.aoff += nb
        if dtype != F32:
            v = v.bitcast(dtype)
        v = v[:, 0:n]
        if len(shape) == 3:
            v = v.rearrange("p (a b) -> p a b", a=shape[1])
        elif len(shape) == 4:
            v = v.rearrange("p (a b c) -> p a b c", a=shape[1], b=shape[2])
        if shape[0] != 128:
            v = v[0:shape[0]]
        self.nkey += 1
        return Tl(v, (name, self.nkey))

    def bank(self, track=False):
        if not hasattr(self, "held"):
            self.held = {}
        for _ in range(8):
            b = self.banks[self.bi % 8]
            self.bi += 1
            if not self.held.get(b.key):
                if track:
                    self.held[b.key] = True
                return b
        raise AssertionError("no free psum bank")

    def done(self, b):
        self.held[b.key] = False

    def op(self, eng, fn, reads=(), writes=()):
        return self.P.op(eng, fn, self._k(reads), self._k(writes), floor=self.floor)

    def I(self, eng, meth, *args, reads=(), writes=(), **kw):
        r = self._k(reads); w = list(self._k(writes))
        bk = [k for k in r if isinstance(k, tuple) and k and k[0] == "bank"]
        r = [k for k in r if k not in bk]
        for k in bk:
            if k not in w:
                w.append(k)
        return self.P.op(eng, lambda e: getattr(e, meth)(*args, **kw), r, w, floor=self.floor)

    def dma(self, out, in_, reads=(), writes=(), issuer="sync", **kw):
        return self.P.dma(issuer, out, in_, self._k(reads), self._k(writes), floor=self.floor, **kw)

    def mm(self, out, lhsT, rhs, start, stop, reads, writes):
        return self.I("tensor", "matmul", out, lhsT, rhs, start=start, stop=stop, reads=reads, writes=writes)

    @staticmethod
    def _k(lst):
        return [x.key if isinstance(x, Tl) else x for x in lst]

    def barrier(self):
        P = self.P
        last = {}
        dmas = {}
        for i, o in enumerate(P.ops):
            if o["dma"]:
                dmas.setdefault(o["eng"], []).append(i)
            else:
                last[o["eng"]] = i
        deps = set(last.values())
        for q, l in dmas.items():
            deps.update(l[-KDMA:])
        bt = self.bar_tile
        idx = P.op("vector", lambda e: e.memset(bt[:], 0.0), [], [("bar", len(P.ops))], floor=self.floor, extra=deps)
        self.floor = idx


def _tok_init(self, nl=4):
    P = self.P
    B = self
    self.d = {}

    def inp(name, shape, dt=F32):
        self.d[name] = P.dram(name, shape, dt, kind="ExternalInput")
        return self.d[name]
    self.inp = inp
    inp("xT0", [D, T]); inp("cvec", [128, 16, 2])
    inp("ada_w", [nl, D, 6 * D]); inp("ada_bT", [nl, 128, 96]); inp("norm_gT", [nl, 2, 128, 16])
    inp("mlp_w1", [nl, D, 4 * D]); inp("mlp_w2", [nl, 4 * D, D]); inp("final_gT", [128, 16])
    inp("ident_f", [128, 128]); inp("ones_f", [128, 128])
    self.XT = P.dram("XT", [D, T], F32)
    self.outT = P.dram("outT", [D, T - LC], F32, kind="ExternalOutput")
    self.ident_f = B.alloc([128, 128], F32, "ident_f")
    self.ones_f = B.alloc([128, 128], F32, "ones_f")
    self.ident_b = B.alloc([128, 128], BF16, "ident_b")
    self.cond = B.alloc([128, 16, 2], BF16, "cond")
    self.modT = B.alloc([128, 96, 2], F32, "modT")
    self.gA = B.alloc([128, 16, 2], F32, "gA")
    self.adab = B.alloc([128, 96], F32, "adab")
    self.ng = B.alloc([128, 2, 16], F32, "ng")
    self.eps_t = B.alloc([128, 1], F32, "eps")
    self.zero_t = B.alloc([128, 1], F32, "zero")
    B.dma(self.ident_f.ap, self.d["ident_f"], [], [self.ident_f])
    B.dma(self.ones_f.ap, self.d["ones_f"], [], [self.ones_f])
    B.I("vector", "tensor_copy", self.ident_b.ap, self.ident_f.ap, reads=[self.ident_f], writes=[self.ident_b])
    B.I("vector", "memset", self.eps_t.ap, 1e-6, writes=[self.eps_t])
    B.I("vector", "memset", self.zero_t.ap, 0.0, writes=[self.zero_t])
    m = B.mark()
    cv = B.alloc([128, 16, 2], F32, "cv")
    B.dma(cv.ap, self.d["cvec"], [], [cv])
    B.I("scalar", "activation", self.cond.ap, cv.ap, AF.Silu, reads=[cv], writes=[self.cond])
    B.barrier()
    B.release(m)
    for k in range(4):
        B.dma(self.XT[k * 512:(k + 1) * 512, :], self.d["xT0"][k * 512:(k + 1) * 512, :], [], [])
    B.barrier()


def _seg_tok(lo):
    return 1 if lo < LC else 0


def _gemm(self, W, nkq, col_blocks, act, groups, evac, kc=16):
    B = self
    m = B.mark()
    stage = Pool(B, 2, [128, kc, 128], F32, "wst")
    wbf = Pool(B, 3, [128, kc, 128], BF16, "wbf")
    for j in col_blocks:
        bk = {}
        for kq in range(nkq):
            st = stage.next()
            src = W[kq * kc * 128:(kq + 1) * kc * 128, j * 128:(j + 1) * 128].rearrange("(k p) c -> p k c", p=128)
            B.dma(st.ap, src, [], [st])
            wb = wbf.next()
            B.I("vector", "tensor_copy", wb.ap, st.ap, reads=[st], writes=[wb])
            for gi, (lo, hi) in enumerate(groups):
                if kq == 0:
                    bk[gi] = B.bank()
                ps = bk[gi]
                for k in range(kc):
                    a_ap, a_keys = act(kq * kc + k, lo, hi)
                    B.mm(ps.ap[:, 0:hi - lo], wb.ap[:, k, :], a_ap, kq == 0 and k == 0,
                         kq == nkq - 1 and k == kc - 1, [wb] + list(a_keys), [ps])
        for gi, (lo, hi) in enumerate(groups):
            evac(j, gi, lo, hi, bk[gi])
    B.release(m)


def _mod(self, li):
    B = self
    m = B.mark()
    B.dma(self.adab.ap, self.d["ada_bT"][li], [], [self.adab])
    B.dma(self.ng.ap, self.d["norm_gT"][li].rearrange("a p k -> p a k"), [], [self.ng])
    cond = self.cond

    def act(k, lo, hi):
        return cond.ap[:, k, :], [cond]

    def evac(j, gi, lo, hi, ps):
        B.I("vector", "tensor_scalar", self.modT.ap[:, j, :], ps.ap[:, 0:2], self.adab.ap[:, j:j + 1], None, ALU.add,
            reads=[ps, self.adab, self.modT], writes=[self.modT])
    _gemm(self, self.d["ada_w"][li], 1, range(96), act, [(0, 2)], evac)
    B.release(m)


def _set_gain(self, which):
    B = self
    sc = self.modT.ap[:, (16 + 48 * which):(32 + 48 * which), :]
    B.I("vector", "tensor_scalar", self.gA.ap, sc, 1.0, None, ALU.add, reads=[self.modT, self.gA], writes=[self.gA])
    B.I("vector", "tensor_tensor", self.gA.ap, self.gA.ap,
        self.ng.ap[:, which, :].unsqueeze(2).to_broadcast([128, 16, 2]), ALU.mult,
        reads=[self.gA, self.ng], writes=[self.gA])


def _adaln(self, src, groups, segs, gain_ap, shift_ap, out_fn, post=None):
    B = self
    m = B.mark()
    xp = Pool(B, 2, [128, 16, 512], F32, "xg")
    sqp = Pool(B, 3, [128, 512], F32, "sq")
    rsp = Pool(B, 2, [128, 512], F32, "rs")
    tmp = Pool(B, 3, [128, 512], F32, "tmp")
    for gi, (lo, hi) in enumerate(groups):
        n = hi - lo
        xg = xp.next()
        B.dma(xg.ap[:, :, 0:n], src[:, lo:hi].rearrange("(k p) t -> p k t", p=128), [], [xg])
        ps = B.bank()
        for k in range(16):
            sq = sqp.next()
            B.I("scalar", "activation", sq.ap[:, 0:n], xg.ap[:, k, 0:n], AF.Square, reads=[xg], writes=[sq])
            B.mm(ps.ap[:, 0:n], self.ones_f.ap, sq.ap[:, 0:n], k == 0, k == 15, [self.ones_f, sq], [ps])
        rs = rsp.next()
        B.I("scalar", "activation", rs.ap[:, 0:n], ps.ap[:, 0:n], AF.Ln, bias=self.eps_t.ap, scale=1.0 / D,
            reads=[ps, self.eps_t], writes=[rs])
        B.I("scalar", "activation", rs.ap[:, 0:n], rs.ap[:, 0:n], AF.Exp, scale=-0.5, reads=[rs], writes=[rs])
        s = segs[gi]
        for k in range(16):
            t = tmp.next()
            B.I("vector", "tensor_tensor", t.ap[:, 0:n], xg.ap[:, k, 0:n], rs.ap[:, 0:n], ALU.mult, reads=[xg, rs], writes=[t])
            o_ap, o_keys = out_fn(k, gi, lo, hi)
            g_ap, g_keys = gain_ap(k, s)
            b_ap, b_keys = shift_ap(k, s)
            B.I("scalar", "activation", o_ap, t.ap[:, 0:n], AF.Identity, bias=b_ap, scale=g_ap,
                reads=[t] + list(g_keys) + list(b_keys), writes=list(o_keys))
            if post is not None:
                post(k, gi, lo, hi)
    B.release(m)


def _residual_evac(self, gate_blk0, xpool):
    B = self

    def evac(j, gi, lo, hi, ps):
        n = hi - lo
        xp = xpool.next()
        B.dma(xp.ap[:, 0:n], self.XT[j * 128:(j + 1) * 128, lo:hi], [], [xp])
        s = _seg_tok(lo)
        B.I("vector", "scalar_tensor_tensor", xp.ap[:, 0:n], ps.ap[:, 0:n], self.modT.ap[:, gate_blk0 + j, s:s + 1],
            xp.ap[:, 0:n], ALU.mult, ALU.add, reads=[ps, self.modT, xp], writes=[xp])
        B.dma(self.XT[j * 128:(j + 1) * 128, lo:hi], xp.ap[:, 0:n], [xp], [])
    return evac


def _mlp(self, li, last):
    B = self
    _set_gain(self, 1)
    B.barrier()
    for si, (ga, gb) in enumerate(SETS):
        groups = [TG6[ga], TG6[gb]]
        if last and si == 0:
            groups = [TG6[gb]]
        base = groups[0][0]
        m = B.mark()
        h2 = B.alloc([128, 16, 768], BF16, "h2")
        _adaln(self, self.XT, groups, [_seg_tok(g[0]) for g in groups],
               lambda k, s: (self.gA.ap[:, k, s:s + 1], [self.gA]),
               lambda k, s: (self.modT.ap[:, 48 + k, s:s + 1], [self.modT]),
               lambda k, gi, lo, hi, h2=h2, base=base: (h2.ap[:, k, lo - base:hi - base], [h2]))
        hid = B.alloc([128, 64, 768], BF16, "hid")
        rp = Pool(B, 3, [128, 512], F32, "relu")

        def act1(k, lo, hi, h2=h2, base=base):
            return h2.ap[:, k, lo - base:hi - base], [h2]

        def evac1(j, gi, lo, hi, ps, hid=hid, base=base, rp=rp):
            n = hi - lo
            r = rp.next()
            B.I("scalar", "activation", r.ap[:, 0:n], ps.ap[:, 0:n], AF.Relu, reads=[ps], writes=[r])
            B.I("gpsimd", "tensor_tensor", hid.ap[:, j, lo - base:hi - base], r.ap[:, 0:n], r.ap[:, 0:n], ALU.mult,
                reads=[r, hid], writes=[hid])
        _gemm(self, self.d["mlp_w1"][li], 1, range(64), act1, groups, evac1)

        if getattr(self, "dbg_mlp", False) and si == 0 and li == 0:
            d1 = B.P.dram("dbg_h2", [128, 16, 768], BF16, kind="ExternalOutput")
            d2 = B.P.dram("dbg_hid", [128, 64, 768], BF16, kind="ExternalOutput")
            B.dma(d1, h2.ap, [h2], [("dbgh2",)])
            B.dma(d2, hid.ap, [hid], [("dbghid",)])

        def act2(k, lo, hi, hid=hid, base=base):
            return hid.ap[:, k, lo - base:hi - base], [hid]
        xpool = Pool(B, 3, [128, 512], F32, "xres")
        _gemm(self, self.d["mlp_w2"][li], 4, range(16), act2, groups, _residual_evac(self, 80, xpool))
        B.barrier()
        B.release(m)


def _outproj(self, li, W, nkq, YT, last):
    B = self
    halves = [[TG6[0], TG6[1], TG6[2]], [TG6[3], TG6[4], TG6[5]]]
    if last:
        halves[0] = [TG6[1], TG6[2]]
    for groups in halves:
        base = groups[0][0]
        ntok = groups[-1][1] - base
        m = B.mark()
        y = B.alloc([128, nkq * 16, 1280], BF16, "yT")
        for q in range(nkq * 2):
            B.dma(y.ap[:, q * 8:(q + 1) * 8, 0:ntok],
                  YT[q * 1024:(q + 1) * 1024, base:base + ntok].rearrange("(k p) t -> p k t", p=128), [], [y])

        def act(k, lo, hi, y=y, base=base):
            return y.ap[:, k, lo - base:hi - base], [y]
        xpool = Pool(B, 3, [128, 512], F32, "xres")
        _gemm(self, W, nkq, range(16), act, groups, _residual_evac(self, 32, xpool))
        B.barrier()
        B.release(m)


def _final(self):
    B = self
    m = B.mark()
    fg = B.alloc([128, 16], F32, "fg")
    B.dma(fg.ap, self.d["final_gT"], [], [fg])
    op = Pool(B, 3, [128, 512], F32, "fo")
    groups = TG6[1:]
    cur = {}

    def out_fn(k, gi, lo, hi):
        t = op.next()
        cur[0] = t
        return t.ap[:, 0:hi - lo], [t]

    def post(k, gi, lo, hi):
        t = cur[0]
        B.dma(self.outT[k * 128:(k + 1) * 128, lo - LC:hi - LC], t.ap[:, 0:hi - lo], [t], [("out", k, gi)])
        self.out_keys.append(("out", k, gi))
    self.out_keys = []
    _adaln(self, self.XT, groups, [0] * len(groups), lambda k, s: (fg.ap[:, k:k + 1], [fg]),
           lambda k, s: (self.zero_t.ap, [self.zero_t]), out_fn, post)
    B.release(m)


NEGV = -30000.0


def _mix_init(self, layers=(0, 1, 2, 3)):
    B = self
    P = self.P
    inp = self.inp
    inp("tri", [2, 128, 128]); inp("neg", [2, 128, 128])
    self.tri = [B.alloc([128, 128], F32, "tri0"), B.alloc([128, 128], F32, "tri1")]
    self.neg = [B.alloc([128, 128], F32, "neg0"), B.alloc([128, 128], F32, "neg1")]
    for i in range(2):
        B.dma(self.tri[i].ap, self.d["tri"][i], [], [self.tri[i]])
        B.dma(self.neg[i].ap, self.d["neg"][i], [], [self.neg[i]])
    self.U = P.dram("U", [97, 128, T], F32)
    self.QT = P.dram("QT", [16, 128, T], BF16)
    self.KT = P.dram("KT", [16, 128, T], BF16)
    self.KK = P.dram("KK", [T, 2048], BF16)
    self.VV = P.dram("VV", [T, 4096], BF16)
    self.LA = P.dram("LA", [2, T, 64], F32)
    self.WT = P.dram("WT", [2, T, 64], F32)
    self.Y = P.dram("Y", [T, 4096], F32)
    self.YT = P.dram("YT", [4096, T], BF16)
    if 1 in layers:
        inp("ret_w_in", [D, 12288]); inp("ret_w_out", [4096, D]); inp("ret_la", [2, T, 8])
        inp("rope_cos", [128, T]); inp("rope_sin", [128, T])
    if 3 in layers:
        inp("gdn_w_in", [D, 12416]); inp("gdn_w_out", [4096, D]); inp("gdn_cw", [128, 64, 3]); inp("gdn_par", [128, 3])
        inp("gdn_ng", [128, 32]); inp("pos", [2, 128, 128]); inp("bmask", [4, 128, 128])
    if 2 in layers:
        inp("hgrn_w_in", [D, 10240]); inp("hgrn_w_out", [D, D]); inp("hgrn_lbl", [128, 16, 4]); inp("hgrn_ng", [128, 16])
    if 0 in layers:
        inp("ssd_w_in", [D, 10368]); inp("ssd_w_out", [4096, D]); inp("ssd_cw", [128, 48, 3]); inp("ssd_cb", [128, 48])
        inp("ssd_par", [128, 2]); inp("ssd_dbc", [128, 64]); inp("ssd_ng", [128, 32])
    B.barrier()


def _inproj_dump(self, W, nblk):
    B = self
    m = B.mark()
    hT = B.alloc([128, 16, T], BF16, "hT")
    _adaln(self, self.XT, TG6, [_seg_tok(g[0]) for g in TG6],
           lambda k, s: (self.gA.ap[:, k, s:s + 1], [self.gA]),
           lambda k, s: (self.modT.ap[:, k, s:s + 1], [self.modT]),
           lambda k, gi, lo, hi: (hT.ap[:, k, lo:hi], [hT]))
    up = Pool(B, 3, [128, 512], F32, "uo")

    def act(k, lo, hi):
        return hT.ap[:, k, lo:hi], [hT]

    def evac(j, gi, lo, hi, ps):
        n = hi - lo
        u = up.next()
        B.I("scalar", "copy", u.ap[:, 0:n], ps.ap[:, 0:n], reads=[ps], writes=[u])
        B.dma(self.U[j, :, lo:hi], u.ap[:, 0:n], [u], [])
    _gemm(self, W, 1, range(nblk), act, TG6, evac)
    B.release(m)


def _transpose_blocks(self, src_fn, nblk, dst, dt_in, chunkcols=None):
    B = self
    m = B.mark()
    sp = Pool(B, 3, [128, T], dt_in, "tsrc")
    op = Pool(B, 3, [128, 1024], BF16, "tdst")
    ident = self.ident_b if dt_in == BF16 else self.ident_f
    for b in range(nblk):
        s = sp.next()
        B.dma(s.ap, src_fn(b), [], [s])
        for c0 in range(0, NT, 8):
            nt = min(8, NT - c0)
            ps = B.bank()
            pv = ps.ap.bitcast(BF16) if dt_in == BF16 else ps.ap
            for i in range(nt if dt_in == BF16 else 0):
                B.I("tensor", "transpose", pv[:, i * 128:(i + 1) * 128], s.ap[:, (c0 + i) * 128:(c0 + i + 1) * 128], ident.ap,
                    reads=[s, ident], writes=[ps])
            o = op.next()
            B.I("vector", "tensor_copy", o.ap[:, 0:nt * 128], pv[:, 0:nt * 128], reads=[ps], writes=[o])
            B.dma(dst[c0 * 128:(c0 + nt) * 128, b * 128:(b + 1) * 128].rearrange("(c t) f -> t c f", t=128),
                  o.ap[:, 0:nt * 128].rearrange("t (c f) -> t c f", f=128), [o], [])
    B.release(m)


def _scan(self, H, kq, qmap, Vd, decay, has_wt, Q=128, EE=None, LA=None, WT=None, QTs=None, KTs=None, KKs=None, aux=None, vhook=None, G=4):
    B = self
    LA = self.LA if LA is None else LA
    WT = self.WT if WT is None else WT
    QTs = QTs or [self.QT, self.QT]; KTs = KTs or [self.KT, self.KT]; KKs = KKs or [self.KK, self.KK]
    VT = H * Vd
    NQB = max(max(qmap(h)) for h in range(H)) + 1
    nch = T // Q
    nctx = LC // Q
    m = B.mark()
    S = B.alloc([128, kq, VT], F32, "S")
    Sb = B.alloc([128, kq, VT], BF16, "Sb")
    qp = Pool(B, 2, [128, NQB, Q], BF16, "qc")
    kp = Pool(B, 2, [128, NQB, Q], BF16, "kc")
    kkp = Pool(B, 2, [128, NQB * 128], BF16, "kkc")
    vp = Pool(B, 2, [128, VT], BF16, "vc")
    yp = Pool(B, 2, [128, VT], F32, "yc")
    lap = Pool(B, 2, [128, 2, H], F32, "lac")
    smallp = Pool(B, 2, [128, 6, H], F32, "small")
    rhp = Pool(B, 2 * G, [128, 128], F32, "rh")
    dtp = Pool(B, 2 * G, [128, 128], F32, "dt")
    scp = Pool(B, 2 * G, [128, 128], F32, "sc")
    atp = Pool(B, 2 * G, [128, 128], BF16, "at")
    kwp = Pool(B, 2 * G, [128, kq * 128], BF16, "kw")
    for dr in (0, 1):
        order = list(range(nch)) if dr == 0 else list(range(nctx - 1, -1, -1)) + list(range(nch - 1, nctx - 1, -1))
        for h in range(H):
            B.I("vector", "memset", S.ap[:, :, h * Vd:(h + 1) * Vd], 0.0, writes=[("S", h)])
            B.I("gpsimd", "memset", Sb.ap[:, :, h * Vd:(h + 1) * Vd], 0.0, writes=[("Sb", h)])
        tri = self.tri[dr]
        neg = self.neg[dr]
        for c in order:
            t0 = c * Q
            qc = qp.next(); kc = kp.next(); kk = kkp.next(); vc = vp.next(); yc = yp.next()
            ykeys = [(yc.key, h) for h in range(H)]
            B.dma(qc.ap, QTs[dr][0:NQB, :, t0:t0 + Q].rearrange("b p t -> p b t"), [], [qc])
            B.dma(kc.ap, KTs[dr][0:NQB, :, t0:t0 + Q].rearrange("b p t -> p b t"), [], [kc])
            B.dma(kk.ap[0:Q], KKs[dr][t0:t0 + Q, 0:NQB * 128], [], [kk])
            B.dma(vc.ap[0:Q], self.VV[t0:t0 + Q, 0:VT], [], [vc])
            la = sm = None
            cs = negb = ec = w = eend = tmp = None
            if decay:
                la = lap.next(); sm = smallp.next()
                B.dma(la.ap[0:Q, 0, :], LA[dr, t0:t0 + Q, 0:H], [], [la])
                if has_wt:
                    B.dma(la.ap[0:Q, 1, :], WT[dr, t0:t0 + Q, 0:H], [], [la])
                if aux is not None:
                    B.dma(la.ap[0:Q, 1, :], aux[dr, t0:t0 + Q, 0:H], [], [la])
                psc = B.bank()
                B.mm(psc.ap[0:Q, 0:H], tri.ap[0:Q, 0:Q], la.ap[0:Q, 0, :], True, True, [tri, la], [psc])
                B.mm(psc.ap[0:Q, 128:128 + H], self.ones_f.ap[0:Q, 0:Q], la.ap[0:Q, 0, :], True, True, [self.ones_f, la], [psc])
                cs, negb, ec, w, eend, tmp = [sm.ap[0:Q, i, :] for i in range(6)]
                B.I("vector", "tensor_copy", cs, psc.ap[0:Q, 0:H], reads=[psc], writes=[sm])
                if has_wt:
                    B.I("vector", "tensor_tensor", negb, la.ap[0:Q, 1, :], cs, ALU.subtract, reads=[sm, la], writes=[sm])
                else:
                    B.I("vector", "tensor_scalar", negb, cs, -1.0, None, ALU.mult, reads=[sm], writes=[sm])
                B.I("scalar", "activation", ec, cs, AF.Exp, reads=[sm], writes=[sm])
                B.I("vector", "tensor_tensor", tmp, psc.ap[0:Q, 128:128 + H], negb, ALU.add, reads=[psc, sm], writes=[sm])
                B.I("scalar", "activation", w, tmp, AF.Exp, reads=[sm], writes=[sm])
                B.I("scalar", "activation", eend, psc.ap[0:Q, 128:128 + H], AF.Exp, reads=[psc, sm], writes=[sm])
            shared = {"g": None, "sc": None}

            def head_gen(h, slot, qc=qc, kc=kc, kk=kk, vc=vc, yc=yc, la=la, sm=sm, shared=shared,
                         negb=negb, ec=ec, w=w, c=c, dr=dr):
                qbs = qmap(h)
                hs = slice(h * Vd, (h + 1) * Vd)
                yk = (yc.key, h)
                rh = None
                psd = None
                if decay:
                    rh = rhp.next()
                    B.I("vector", "tensor_scalar", rh.ap[0:Q, 0:Q], tri.ap[0:Q, 0:Q], la.ap[0:Q, 0, h:h + 1], None, ALU.mult,
                        reads=[tri, la], writes=[rh])
                    yield
                    psd = B.bank(True)
                    B.mm(psd.ap[0:Q, 0:Q], self.ones_f.ap[0:Q, 0:Q], rh.ap[0:Q, 0:Q], True, False, [self.ones_f, rh], [psd])
                    B.mm(psd.ap[0:Q, 0:Q], self.ident_f.ap[0:Q, 0:Q], neg.ap[0:Q, 0:Q], False, True, [self.ident_f, neg], [psd])
                if tuple(qbs) != shared["g"]:
                    shared["g"] = tuple(qbs)
                    pss = B.bank(True)
                    for i, qb in enumerate(qbs):
                        B.mm(pss.ap[0:Q, 0:Q], kc.ap[:, qb, :], qc.ap[:, qb, :], i == 0, i == len(qbs) - 1, [kc, qc], [pss])
                    sc = scp.next()
                    shared["sc"] = sc
                    yield
                    B.I("scalar", "copy", sc.ap[0:Q, 0:Q], pss.ap[0:Q, 0:Q], reads=[pss], writes=[sc])
                    B.done(pss)
                else:
                    sc = shared["sc"]
                    yield
                if decay:
                    dtm = dtp.next()
                    B.I("scalar", "activation", dtm.ap[0:Q, 0:Q], psd.ap[0:Q, 0:Q], AF.Exp, bias=negb[:, h:h + 1],
                        reads=[psd, sm], writes=[dtm])
                    B.done(psd)
                    dmat = dtm
                    yield
                else:
                    dmat = tri
                at = atp.next()
                B.I("vector", "tensor_tensor", at.ap[0:Q, 0:Q], dmat.ap[0:Q, 0:Q], sc.ap[0:Q, 0:Q], ALU.mult,
                    reads=[dmat, sc], writes=[at])
                kw = None
                if decay:
                    kw = kwp.next()
                    for i, qb in enumerate(qbs):
                        B.I("vector", "tensor_scalar", kw.ap[0:Q, i * 128:(i + 1) * 128], kk.ap[0:Q, qb * 128:(qb + 1) * 128],
                            w[:, h:h + 1], None, ALU.mult, reads=[kk, sm, kw], writes=[kw])
                yield
                v_ap, v_keys = vc.ap[0:Q, hs], [vc]
                if vhook is not None:
                    v_ap, v_keys = yield from vhook(h, slot, dr, qbs, kc, vc, la, sm, rh, Sb, hs)
                psy = B.bank(True)
                B.mm(psy.ap[0:Q, 0:Vd], at.ap[0:Q, 0:Q], v_ap, True, True, [at] + v_keys, [psy])
                psys = B.bank(True)
                for i, qb in enumerate(qbs):
                    B.mm(psys.ap[0:Q, 0:Vd], qc.ap[:, qb, :], Sb.ap[:, i, hs], i == 0, i == len(qbs) - 1, [qc, ("Sb", h)], [psys])
                yield
                B.I("scalar", "copy", yc.ap[0:Q, hs], psy.ap[0:Q, 0:Vd], reads=[psy, yk], writes=[yk])
                B.done(psy)
                if decay:
                    B.I("vector", "scalar_tensor_tensor", yc.ap[0:Q, hs], psys.ap[0:Q, 0:Vd], ec[:, h:h + 1], yc.ap[0:Q, hs],
                        ALU.mult, ALU.add, reads=[psys, sm, yk], writes=[yk])
                else:
                    B.I("vector", "tensor_tensor", yc.ap[0:Q, hs], psys.ap[0:Q, 0:Vd], yc.ap[0:Q, hs], ALU.add,
                        reads=[psys, yk], writes=[yk])
                B.done(psys)
                for i, qb in enumerate(qbs):
                    psS = B.bank(True)
                    lhs = kw.ap[0:Q, i * 128:(i + 1) * 128] if decay else kk.ap[0:Q, qb * 128:(qb + 1) * 128]
                    B.mm(psS.ap[:, 0:Vd], lhs, v_ap, True, True, [kw if decay else kk] + v_keys, [psS])
                    yield
                    if decay:
                        B.I("vector", "scalar_tensor_tensor", S.ap[:, i, hs], S.ap[:, i, hs], self._eend_ap(sm, h, Q), psS.ap[:, 0:Vd],
                            ALU.mult, ALU.add, reads=[("S", h), sm, psS], writes=[("S", h)])
                    else:
                        B.I("vector", "scalar_tensor_tensor", S.ap[:, i, hs], S.ap[:, i, hs], EE(qb, c, dr), psS.ap[:, 0:Vd],
                            ALU.mult, ALU.add, reads=[("S", h), psS, ("EE",)], writes=[("S", h)])
                    B.done(psS)
                yield
                for i, qb in enumerate(qbs):
                    B.I("gpsimd", "tensor_copy", Sb.ap[:, i, hs], S.ap[:, i, hs], reads=[("S", h)], writes=[("Sb", h)])

            for h0 in range(0, H, G):
                gens = [head_gen(h, h - h0) for h in range(h0, min(H, h0 + G))]
                while gens:
                    nxt = []
                    for g in gens:
                        try:
                            next(g)
                            nxt.append(g)
                        except StopIteration:
                            pass
                    gens = nxt
            if dr == 1:
                yf = yp.next()
                B.dma(yf.ap[0:Q], self.Y[t0:t0 + Q, 0:VT], [], [yf])
                B.I("gpsimd", "tensor_tensor", yc.ap[0:Q], yc.ap[0:Q], yf.ap[0:Q], ALU.add, reads=ykeys + [yf], writes=ykeys)
            B.dma(self.Y[t0:t0 + Q, 0:VT], yc.ap[0:Q], ykeys, [])
        B.barrier()
    B.release(m)


def _eend_ap(self, sm, h, Q):
    return sm.ap[:, 4, h:h + 1]


Builder._eend_ap = _eend_ap


def _transpose_blocks2(self, src_fn, nblk, dst, dt_in):
    B = self
    m = B.mark()
    sp = Pool(B, 2, [128, T], dt_in, "tsrc")
    op = Pool(B, 3, [128, 1024], BF16, "tdst")
    ident = self.ident_b if dt_in == BF16 else self.ident_f
    nper = 8 if dt_in == BF16 else 4
    for b in range(nblk):
        s = sp.next()
        B.dma(s.ap, src_fn(b), [], [s])
        for c0 in range(0, NT, nper):
            nt = min(nper, NT - c0)
            ps = B.bank()
            pv = ps.ap.bitcast(BF16) if dt_in == BF16 else ps.ap
            for i in range(nt):
                B.I("tensor", "transpose", pv[:, i * 128:(i + 1) * 128], s.ap[:, (c0 + i) * 128:(c0 + i + 1) * 128], ident.ap,
                    reads=[s, ident], writes=[ps])
            o = op.next()
            B.I("vector", "tensor_copy", o.ap[:, 0:nt * 128], pv[:, 0:nt * 128], reads=[ps], writes=[o])
            B.dma(dst[c0 * 128:(c0 + nt) * 128, b * 128:(b + 1) * 128].rearrange("(c t) f -> t c f", t=128),
                  o.ap[:, 0:nt * 128].rearrange("t (c f) -> t c f", f=128), [o], [])
    B.release(m)


def _ret_mixer(self, li, last):
    B = self
    _inproj_dump(self, self.d["ret_w_in"], 96)
    m = B.mark()
    cos = B.alloc([128, T], F32, "cos"); sin = B.alloc([128, T], F32, "sin")
    B.dma(cos.ap, self.d["rope_cos"], [], [cos])
    B.dma(sin.ap, self.d["rope_sin"], [], [sin])
    ap_ = Pool(B, 2, [128, T], F32, "ra"); bp_ = Pool(B, 2, [128, T], F32, "rb")
    t1p = Pool(B, 2, [128, T], F32, "t1"); t2p = Pool(B, 2, [128, T], F32, "t2")
    oap = Pool(B, 2, [128, T], BF16, "oa"); obp = Pool(B, 2, [128, T], BF16, "ob")
    for h in range(8):
        for base, dst, scale in ((0, self.QT, 1.0), (16, self.KT, 1.0 / 16.0)):
            a = ap_.next(); b = bp_.next(); t1 = t1p.next(); t2 = t2p.next(); oa = oap.next(); ob = obp.next()
            B.dma(a.ap, self.U[base + 2 * h], [], [a])
            B.dma(b.ap, self.U[base + 2 * h + 1], [], [b])
            B.I("vector", "tensor_tensor", t1.ap, a.ap, cos.ap, ALU.mult, reads=[a, cos], writes=[t1])
            B.I("gpsimd", "tensor_tensor", t2.ap, b.ap, sin.ap, ALU.mult, reads=[b, sin], writes=[t2])
            B.I("vector", "tensor_tensor", t1.ap, t1.ap, t2.ap, ALU.subtract, reads=[t1, t2], writes=[t1])
            B.I("scalar", "activation", oa.ap, t1.ap, AF.Copy, scale=scale, reads=[t1], writes=[oa])
            B.dma(dst[2 * h], oa.ap, [oa], [])
            B.I("vector", "tensor_tensor", t1.ap, b.ap, cos.ap, ALU.mult, reads=[b, cos, t1], writes=[t1])
            B.I("gpsimd", "tensor_tensor", t2.ap, a.ap, sin.ap, ALU.mult, reads=[a, sin, t2], writes=[t2])
            B.I("vector", "tensor_tensor", t1.ap, t1.ap, t2.ap, ALU.add, reads=[t1, t2], writes=[t1])
            B.I("scalar", "activation", ob.ap, t1.ap, AF.Copy, scale=scale, reads=[t1], writes=[ob])
            B.dma(dst[2 * h + 1], ob.ap, [ob], [])
    B.release(m)
    _transpose_blocks2(self, lambda b: self.KT[b], 16, self.KK, BF16)
    _transpose_blocks2(self, lambda b: self.U[32 + b], 32, self.VV, F32)
    _scan(self, 8, 2, lambda h: [2 * h, 2 * h + 1], 512, True, False, LA=self.d["ret_la"])
    m = B.mark()
    yp = Pool(B, 2, [128, 4096], F32, "ey")
    ynp = Pool(B, 2, [128, 4096], BF16, "eyn")
    gp = Pool(B, 2, [128, 32, 128], F32, "eg")
    op = Pool(B, 2, [128, 32, 128], BF16, "eo")
    stp = Pool(B, 2, [128, 4, 8], F32, "est")
    junk = B.alloc([128, 512], F32, "junk")
    for c in range(2 if last else 0, NT):
        t0 = c * 128
        y = yp.next(); yn = ynp.next(); g = gp.next(); o = op.next(); st = stp.next()
        B.dma(y.ap, self.Y[t0:t0 + 128, :], [], [y])
        B.dma(g.ap, self.U[64:96, :, t0:t0 + 128].rearrange("b p t -> p b t"), [], [g])
        B.I("scalar", "activation", g.ap, g.ap, AF.Silu, reads=[g], writes=[g])
        mu, nmu, var, rstd = [st.ap[:, i, :] for i in range(4)]
        B.I("vector", "tensor_reduce", mu, y.ap.rearrange("p (h d) -> p h d", h=8), AX.X, ALU.add, reads=[y], writes=[st])
        B.I("vector", "tensor_scalar", nmu, mu, -1.0 / 512, None, ALU.mult, reads=[st], writes=[st])
        for h in range(8):
            B.I("scalar", "activation", junk.ap, y.ap[:, h * 512:(h + 1) * 512], AF.Square, bias=nmu[:, h:h + 1],
                accum_out=var[:, h:h + 1], reads=[y, st, junk], writes=[junk, st])
        B.I("scalar", "activation", rstd, var, AF.Ln, bias=self.eps_t.ap, scale=1.0 / 512, reads=[st, self.eps_t], writes=[st])
        B.I("scalar", "activation", rstd, rstd, AF.Exp, scale=-0.5, reads=[st], writes=[st])
        for h in range(8):
            B.I("vector", "tensor_scalar", yn.ap[:, h * 512:(h + 1) * 512], y.ap[:, h * 512:(h + 1) * 512],
                nmu[:, h:h + 1], rstd[:, h:h + 1], ALU.add, ALU.mult, reads=[y, st, yn], writes=[yn])
        for b0 in range(0, 32, 8):
            ps = B.bank()
            pv = ps.ap.bitcast(BF16)
            for i in range(8):
                B.I("tensor", "transpose", pv[:, i * 128:(i + 1) * 128], yn.ap[:, (b0 + i) * 128:(b0 + i + 1) * 128],
                    self.ident_b.ap, reads=[yn, self.ident_b], writes=[ps])
            B.I("vector", "tensor_tensor", o.ap[:, b0:b0 + 8, :], pv[:, 0:1024].rearrange("p (b t) -> p b t", b=8),
                g.ap[:, b0:b0 + 8, :], ALU.mult, reads=[ps, g, o], writes=[o])
        B.dma(self.YT[:, t0:t0 + 128].rearrange("(b p) t -> p b t", p=128), o.ap, [o], [])
    B.release(m)
    return self.d["ret_w_out"], 2, self.YT


def _mixer(self, li, last):
    return [_ssd_mixer, _ret_mixer, _hgrn_mixer, _gdn_mixer][li](self, li, last)


def host_mixer_inputs(inputs, b, layers=(0, 1, 2, 3)):
    f = np.float32
    m = {}
    idx = np.arange(128)
    tri0 = (idx[:, None] <= idx[None, :]).astype(f)
    tri1 = (idx[:, None] >= idx[None, :]).astype(f)
    m["tri"] = np.stack([tri0, tri1])
    m["neg"] = np.stack([(1 - tri0) * NEGV, (1 - tri1) * NEGV]).astype(f)
    if 0 in layers:
        m["ssd_w_in"] = np.asarray(inputs["ssd_w_in"], f)[0]
        m["ssd_w_out"] = np.asarray(inputs["ssd_w_out"], f)[0]
        cw = np.asarray(inputs["ssd_conv_w"], f)[0]
        m["ssd_cw"] = np.ascontiguousarray(cw.reshape(3, 48, 128).transpose(2, 1, 0))
        m["ssd_cb"] = np.ascontiguousarray(np.asarray(inputs["ssd_conv_b"], f)[0].reshape(48, 128).T)
        m["ssd_par"] = np.ascontiguousarray(np.stack([np.asarray(inputs["ssd_dt_bias"], f)[0].reshape(128),
                                                      np.asarray(inputs["ssd_a_log"], f)[0].reshape(128)], -1))
        m["ssd_dbc"] = np.ascontiguousarray(np.broadcast_to(np.asarray(inputs["ssd_d"], f)[0][None, :], (128, 64)))
        m["ssd_ng"] = np.ascontiguousarray(np.asarray(inputs["ssd_norm_g"], f)[0].reshape(32, 128).T)
    if 3 in layers:
        m["gdn_w_in"] = np.asarray(inputs["gdn_w_in"], f)[0]
        m["gdn_w_out"] = np.asarray(inputs["gdn_w_out"], f)[0]
        cw = np.asarray(inputs["gdn_conv_w"], f)[0]
        m["gdn_cw"] = np.ascontiguousarray(cw.reshape(3, 64, 128).transpose(2, 1, 0))
        par = np.zeros((128, 3), f)
        par[64:, 0] = np.asarray(inputs["gdn_dt_bias"], f)[0].reshape(64)
        par[64:, 1] = np.asarray(inputs["gdn_a_log"], f)[0].reshape(64)
        par[64:, 2] = -1.0
        m["gdn_par"] = par
        m["gdn_ng"] = np.ascontiguousarray(np.broadcast_to(np.asarray(inputs["gdn_norm_g"], f)[0][:, None], (128, 32)))
        st0 = (idx[None, :] < idx[:, None]).astype(f)
        st1 = (idx[None, :] > idx[:, None]).astype(f)
        bd = lambda n: (idx[:, None] // n == idx[None, :] // n).astype(f)
        m["bmask"] = np.stack([bd(16), bd(32) - bd(16), bd(64) - bd(32), 1 - bd(64)]).astype(f)
        m["pos"] = np.stack([(1 - st0) * (-NEGV), (1 - st1) * (-NEGV)]).astype(f)
    if 2 in layers:
        m["hgrn_w_in"] = np.asarray(inputs["hgrn_w_in"], f)[0]
        m["hgrn_w_out"] = np.asarray(inputs["hgrn_w_out"], f)[0]
        m["hgrn_lbl"] = np.ascontiguousarray(np.asarray(inputs["hgrn_lb_logits"], f).reshape(4, 16, 128).transpose(2, 1, 0))
        m["hgrn_ng"] = np.ascontiguousarray(np.asarray(inputs["hgrn_norm_g"], f)[0].reshape(16, 128).T)
    if 1 in layers:
        w = np.asarray(inputs["ret_w_in"], f)[0]
        perm = []
        for h in range(8):
            base = h * 256
            perm += list(range(base, base + 64)) + list(range(base + 128, base + 192))
            perm += list(range(base + 64, base + 128)) + list(range(base + 192, base + 256))
        perm = np.array(perm)
        cols = np.concatenate([perm, 2048 + perm, np.arange(4096, 12288)])
        m["ret_w_in"] = np.ascontiguousarray(w[:, cols])
        m["ret_w_out"] = np.asarray(inputs["ret_w_out"], f)[0]
        ld = np.asarray(inputs["ret_log_decay"], f)[0]
        m["ret_la"] = np.ascontiguousarray(np.broadcast_to(ld[:, None, :], (2, T, 8)))
        n = np.arange(T - LC)
        row = (n // 64).astype(f); col = (n % 64).astype(f)
        inv = (10000.0 ** (-np.arange(0, 128, 2, dtype=f) / 128)).astype(f)
        ang = np.concatenate([row[None, :] * inv[:, None], col[None, :] * inv[:, None]], 0)
        cos = np.ones((128, T), f); sin = np.zeros((128, T), f)
        cos[:, LC:] = np.cos(ang); sin[:, LC:] = np.sin(ang)
        m["rope_cos"] = cos; m["rope_sin"] = sin
    return m


def _conv_blocks(self, blocks, cw, cb, sink, silu=True):
    B = self
    m = B.mark()
    up = Pool(B, 2, [128, T], F32, "cu")
    op = Pool(B, 2, [128, T], F32, "co")
    for i, blk in enumerate(blocks):
        u = up.next(); o = op.next()
        B.dma(u.ap, self.U[blk], [], [u])
        bias = cb.ap[:, i:i + 1] if cb is not None else self.zero_t.ap
        B.I("scalar", "activation", o.ap, u.ap, AF.Identity, bias=bias, scale=cw.ap[:, i, 1:2],
            reads=[u, cw] + ([cb] if cb is not None else [self.zero_t]), writes=[o])
        for lo, hi in ((0, LC), (LC, T)):
            B.I("vector", "scalar_tensor_tensor", o.ap[:, lo + 1:hi], u.ap[:, lo:hi - 1], cw.ap[:, i, 0:1], o.ap[:, lo + 1:hi],
                ALU.mult, ALU.add, reads=[u, cw, o], writes=[o])
            B.I("vector", "scalar_tensor_tensor", o.ap[:, lo:hi - 1], u.ap[:, lo + 1:hi], cw.ap[:, i, 2:3], o.ap[:, lo:hi - 1],
                ALU.mult, ALU.add, reads=[u, cw, o], writes=[o])
        if silu:
            B.I("scalar", "activation", o.ap, o.ap, AF.Silu, reads=[o], writes=[o])
        sink(i, blk, o)
    B.release(m)


def _ssd_mixer(self, li, last):
    B = self
    _inproj_dump(self, self.d["ssd_w_in"], 81)
    m0 = B.mark()
    cw = B.alloc([128, 48, 3], F32, "cw"); cb = B.alloc([128, 48], F32, "cb")
    B.dma(cw.ap, self.d["ssd_cw"], [], [cw]); B.dma(cb.ap, self.d["ssd_cb"], [], [cb])
    obp = Pool(B, 2, [128, T], BF16, "cob")
    self.XS = self.P.dram("XS", [32, 128, T], BF16) if not hasattr(self, "XS") else self.XS

    def sink(i, blk, o):
        ob = obp.next()
        B.I("gpsimd", "tensor_copy", ob.ap, o.ap, reads=[o], writes=[ob])
        if blk < 64:
            B.dma(self.XS[blk - 32], ob.ap, [ob], [])
        elif blk < 72:
            B.dma(self.KT[blk - 64], ob.ap, [ob], [])
        else:
            B.dma(self.QT[blk - 72], ob.ap, [ob], [])
    _conv_blocks(self, list(range(32, 80)), cw, cb, sink)
    m = B.mark()
    par = B.alloc([128, 4], F32, "par")
    B.dma(par.ap[:, 0:2], self.d["ssd_par"], [], [par])
    B.I("scalar", "activation", par.ap[:, 2:3], par.ap[:, 1:2], AF.Exp, reads=[par], writes=[par])
    B.I("vector", "tensor_scalar", par.ap[:, 2:3], par.ap[:, 2:3], -1.0, None, ALU.mult, reads=[par], writes=[par])
    B.I("vector", "memset", par.ap[:, 3:4], 1.0, reads=[par], writes=[par])
    u = B.alloc([128, T], F32, "dtu"); la = B.alloc([128, T], F32, "dtla")
    B.dma(u.ap, self.U[80], [], [u])
    B.I("scalar", "activation", u.ap, u.ap, AF.Exp, bias=par.ap[:, 0:1], reads=[u, par], writes=[u])
    B.I("scalar", "activation", u.ap, u.ap, AF.Ln, bias=par.ap[:, 3:4], reads=[u, par], writes=[u])
    B.I("vector", "tensor_scalar", la.ap, u.ap, par.ap[:, 2:3], None, ALU.mult, reads=[u, par], writes=[la])
    B.I("scalar", "activation", u.ap, u.ap, AF.Ln, reads=[u], writes=[u])
    op = Pool(B, 2, [128, 4, 128], F32, "dto")
    for src, dst in ((la, self.LA), (u, self.WT)):
        for c0 in range(0, NT, 4):
            nt = min(4, NT - c0)
            ps = B.bank()
            for i in range(nt):
                B.I("tensor", "transpose", ps.ap[:, i * 128:(i + 1) * 128], src.ap[:, (c0 + i) * 128:(c0 + i + 1) * 128],
                    self.ident_f.ap, reads=[src, self.ident_f], writes=[ps])
            o = op.next()
            B.I("vector", "tensor_copy", o.ap[:, 0:nt, :], ps.ap[:, 0:nt * 128].rearrange("p (c f) -> p c f", f=128), reads=[ps], writes=[o])
            for dr in range(2):
                B.dma(dst[dr, c0 * 128:(c0 + nt) * 128, 0:64].rearrange("(c t) h -> t c h", t=128),
                      o.ap[:, 0:nt, dr * 64:(dr + 1) * 64], [o], [])
    B.release(m)
    B.release(m0)
    _transpose_blocks2(self, lambda b: self.KT[b], 8, self.KK, BF16)
    _transpose_blocks2(self, lambda b: self.XS[b], 32, self.VV, BF16)
    _scan(self, 64, 1, lambda h: [h // 8], 64, True, True)
    m = B.mark()
    dsk = B.alloc([128, 64], F32, "dsk")
    B.dma(dsk.ap, self.d["ssd_dbc"], [], [dsk])
    ng = B.alloc([128, 32], F32, "sng")
    B.dma(ng.ap, self.d["ssd_ng"], [], [ng])
    yp = Pool(B, 2, [128, 4096], F32, "ey")
    xp = Pool(B, 2, [128, 4096], BF16, "ex")
    zp = Pool(B, 2, [128, 32, 128], F32, "ez")
    gp = Pool(B, 1, [128, 32, 128], F32, "eg")
    sqp = Pool(B, 1, [128, 32, 128], F32, "esq")
    rsp = Pool(B, 2, [128, 8, 128], F32, "ers")
    op = Pool(B, 2, [128, 32, 128], BF16, "eo")
    for c in range(2 if last else 0, NT):
        t0 = c * 128
        y = yp.next(); xs = xp.next(); z = zp.next(); g = gp.next(); sq = sqp.next(); rs = rsp.next(); o = op.next()
        B.dma(y.ap, self.Y[t0:t0 + 128, :], [], [y])
        B.dma(xs.ap, self.VV[t0:t0 + 128, :], [], [xs])
        B.dma(z.ap, self.U[0:32, :, t0:t0 + 128].rearrange("b p t -> p b t"), [], [z])
        B.I("scalar", "activation", z.ap, z.ap, AF.Silu, reads=[z], writes=[z])
        B.I("gpsimd", "tensor_tensor", sq.ap.rearrange("p b t -> p (b t)").rearrange("p (h d) -> p h d", h=64),
            xs.ap.rearrange("p (h d) -> p h d", h=64), dsk.ap.unsqueeze(2).to_broadcast([128, 64, 64]), ALU.mult,
            reads=[xs, dsk, sq], writes=[sq])
        B.I("vector", "tensor_tensor", y.ap, y.ap, sq.ap.rearrange("p b t -> p (b t)"), ALU.add, reads=[y, sq], writes=[y])
        for b0 in range(0, 32, 4):
            ps = B.bank()
            for i in range(4):
                B.I("tensor", "transpose", ps.ap[:, i * 128:(i + 1) * 128], y.ap[:, (b0 + i) * 128:(b0 + i + 1) * 128],
                    self.ident_f.ap, reads=[y, self.ident_f], writes=[ps])
            B.I("vector", "tensor_tensor", g.ap[:, b0:b0 + 4, :], ps.ap.rearrange("p (b t) -> p b t", b=4),
                z.ap[:, b0:b0 + 4, :], ALU.mult, reads=[ps, z, g], writes=[g])
        B.I("scalar", "activation", sq.ap, g.ap, AF.Square, reads=[g, sq], writes=[sq])
        for half in range(2):
            ps = B.bank()
            for gi in range(4):
                G = half * 4 + gi
                for b in range(4):
                    B.mm(ps.ap[:, gi * 128:(gi + 1) * 128], self.ones_f.ap, sq.ap[:, G * 4 + b, :], b == 0, b == 3,
                         [self.ones_f, sq], [ps])
            B.I("scalar", "activation", rs.ap[:, half * 4:half * 4 + 4, :], ps.ap.rearrange("p (g t) -> p g t", g=4), AF.Ln,
                bias=self.eps_t.ap, scale=1.0 / 512, reads=[ps, self.eps_t, rs], writes=[rs])
        B.I("scalar", "activation", rs.ap, rs.ap, AF.Exp, scale=-0.5, reads=[rs], writes=[rs])
        for b in range(32):
            B.I("vector", "scalar_tensor_tensor", o.ap[:, b, :], g.ap[:, b, :], ng.ap[:, b:b + 1], rs.ap[:, b // 4, :],
                ALU.mult, ALU.mult, reads=[g, ng, rs, o], writes=[o])
        B.dma(self.YT[:, t0:t0 + 128].rearrange("(b p) t -> p b t", p=128), o.ap, [o], [])
    B.release(m)
    return self.d["ssd_w_out"], 2, self.YT


def _hgrn_mixer(self, li, last):
    B = self
    P = self.P
    _inproj_dump(self, self.d["hgrn_w_in"], 80)
    if not hasattr(self, "QT2"):
        self.QT2 = P.dram("QT2", [16, 128, T], BF16); self.KT2 = P.dram("KT2", [16, 128, T], BF16)
        self.KK2 = P.dram("KK2", [T, 2048], BF16); self.KS = P.dram("KS", [2, 16, 128, T], BF16)
    m_ee = B.mark()
    ee = B.alloc([128, 2, 16, 36], F32, "ee")
    m = B.mark()
    lg = B.alloc([128, 16, 4], F32, "lg"); lbt = B.alloc([128, 4, 16], F32, "lbt")
    B.dma(lg.ap, self.d["hgrn_lbl"], [], [lg])
    B.I("scalar", "activation", lg.ap, lg.ap, AF.Exp, reads=[lg], writes=[lg])
    ssum, lb, oml, noml = [lbt.ap[:, i, :] for i in range(4)]
    B.I("vector", "tensor_reduce", ssum, lg.ap, AX.X, ALU.add, reads=[lg], writes=[lbt])
    B.I("vector", "reciprocal", ssum, ssum, reads=[lbt], writes=[lbt])
    B.I("vector", "tensor_tensor", lb, lg.ap[:, :, 1], lg.ap[:, :, 2], ALU.add, reads=[lg, lbt], writes=[lbt])
    B.I("vector", "tensor_tensor", lb, lb, ssum, ALU.mult, reads=[lbt], writes=[lbt])
    B.I("vector", "tensor_scalar", oml, lb, -1.0, 1.0, ALU.mult, ALU.add, reads=[lbt], writes=[lbt])
    B.I("vector", "tensor_scalar", noml, oml, -1.0, None, ALU.mult, reads=[lbt], writes=[lbt])
    qp = Pool(B, 2, [128, T], F32, "hq"); fp = Pool(B, 2, [128, T], F32, "hf")
    kgp = Pool(B, 2, [128, T], F32, "hkg"); lfp = Pool(B, 2, [128, T], F32, "hlf")
    cp = Pool(B, 2, [128, T], F32, "hc"); clp = Pool(B, 2, [128, T], F32, "hcl")
    ep = Pool(B, 2, [128, T], F32, "he"); obp = Pool(B, 3, [128, T], BF16, "hob")
    totp = Pool(B, 2, [128, 36], F32, "htot")
    v3 = lambda t: t.ap.rearrange("p (c q) -> p c q", q=64)
    for b in range(16):
        q = qp.next()
        B.dma(q.ap, self.U[b], [], [q])
        for dr in range(2):
            f = fp.next(); kg = kgp.next(); lf = lfp.next(); C = cp.next(); cl = clp.next(); e = ep.next(); tot = totp.next()
            B.dma(f.ap, self.U[16 + 16 * dr + b], [], [f])
            B.I("scalar", "activation", f.ap, f.ap, AF.Sigmoid, reads=[f], writes=[f])
            B.I("vector", "tensor_scalar", lf.ap, f.ap, oml[:, b:b + 1], lb[:, b:b + 1], ALU.mult, ALU.add, reads=[f, lbt], writes=[lf])
            B.I("scalar", "activation", lf.ap, lf.ap, AF.Ln, reads=[lf], writes=[lf])
            B.I("gpsimd", "tensor_scalar", kg.ap, f.ap, noml[:, b:b + 1], oml[:, b:b + 1], ALU.mult, ALU.add, reads=[f, lbt], writes=[kg])
            B.I("vector", "tensor_tensor_scan", C.ap, lf.ap, lf.ap, 0.0, ALU.add, ALU.bypass, reads=[lf], writes=[C])
            B.I("vector", "tensor_tensor", tot.ap, v3(C)[:, :, 63], v3(C)[:, :, 0], ALU.subtract, reads=[C], writes=[tot])
            B.I("vector", "tensor_tensor", tot.ap, tot.ap, v3(lf)[:, :, 0], ALU.add, reads=[tot, lf], writes=[tot])
            if dr == 0:
                B.I("vector", "tensor_tensor", cl.ap[:, 0:36], v3(C)[:, :, 0], v3(lf)[:, :, 0], ALU.subtract, reads=[C, lf], writes=[cl])
                B.I("vector", "tensor_tensor", v3(e), v3(C), cl.ap[:, 0:36].unsqueeze(2).to_broadcast([128, 36, 64]), ALU.subtract,
                    reads=[C, cl], writes=[e])
                B.I("vector", "tensor_copy", cl.ap, e.ap, reads=[e, cl], writes=[cl])
            else:
                B.I("vector", "tensor_tensor", v3(cl), v3(C)[:, :, 63:64].to_broadcast([128, 36, 64]), v3(C), ALU.subtract,
                    reads=[C], writes=[cl])
                B.I("vector", "tensor_tensor", cl.ap, cl.ap, lf.ap, ALU.add, reads=[cl, lf], writes=[cl])
            B.I("scalar", "activation", ee.ap[:, dr, b, :], tot.ap, AF.Exp, reads=[tot, ("EE",)], writes=[("EE",)])
            B.I("scalar", "activation", e.ap, cl.ap, AF.Exp, reads=[cl, e], writes=[e])
            ob = obp.next()
            B.I("gpsimd", "tensor_tensor", ob.ap, q.ap, e.ap, ALU.mult, reads=[q, e], writes=[ob])
            B.dma((self.QT, self.QT2)[dr][b], ob.ap, [ob], [])
            B.I("scalar", "activation", e.ap, cl.ap, AF.Exp, scale=-1.0, reads=[cl, e], writes=[e])
            ob = obp.next()
            B.I("gpsimd", "tensor_tensor", ob.ap, kg.ap, e.ap, ALU.mult, reads=[kg, e], writes=[ob])
            B.dma((self.KT, self.KT2)[dr][b], ob.ap, [ob], [])
            B.I("vector", "tensor_tensor", v3(e), tot.ap.unsqueeze(2).to_broadcast([128, 36, 64]), v3(cl), ALU.subtract,
                reads=[tot, cl, e], writes=[e])
            B.I("scalar", "activation", e.ap, e.ap, AF.Exp, reads=[e], writes=[e])
            ob = obp.next()
            B.I("vector", "tensor_tensor", ob.ap, kg.ap, e.ap, ALU.mult, reads=[kg, e], writes=[ob])
            B.dma(self.KS[dr, b], ob.ap, [ob], [])
    B.release(m)
    _transpose_blocks2(self, lambda b: self.KS[0, b], 16, self.KK, BF16)
    _transpose_blocks2(self, lambda b: self.KS[1, b], 16, self.KK2, BF16)
    _transpose_blocks2(self, lambda b: self.U[48 + b], 16, self.VV, F32)
    _scan(self, 16, 1, lambda h: [h], 128, False, False, Q=64, EE=lambda qb, c, dr: ee.ap[:, dr, qb, c:c + 1],
          QTs=[self.QT, self.QT2], KTs=[self.KT, self.KT2], KKs=[self.KK, self.KK2])
    m = B.mark()
    ng = B.alloc([128, 16], F32, "hng")
    B.dma(ng.ap, self.d["hgrn_ng"], [], [ng])
    yp = Pool(B, 2, [128, 2048], F32, "ey"); sqp = Pool(B, 2, [128, 2048], F32, "esq")
    ynp = Pool(B, 2, [128, 2048], BF16, "eyn")
    gp = Pool(B, 2, [128, 16, 128], F32, "eg"); op = Pool(B, 2, [128, 16, 128], BF16, "eo")
    stp = Pool(B, 2, [128, 16], F32, "est")
    for c in range(2 if last else 0, NT):
        t0 = c * 128
        y = yp.next(); sq = sqp.next(); yn = ynp.next(); g = gp.next(); o = op.next(); st = stp.next()
        B.dma(y.ap, self.Y[t0:t0 + 128, 0:2048], [], [y])
        B.dma(g.ap, self.U[64:80, :, t0:t0 + 128].rearrange("b p t -> p b t"), [], [g])
        B.I("scalar", "activation", g.ap, g.ap, AF.Silu, reads=[g], writes=[g])
        B.I("scalar", "activation", sq.ap, y.ap, AF.Square, reads=[y], writes=[sq])
        B.I("vector", "tensor_reduce", st.ap, sq.ap.rearrange("p (h d) -> p h d", h=16), AX.X, ALU.add, reads=[sq], writes=[st])
        B.I("scalar", "activation", st.ap, st.ap, AF.Ln, bias=self.eps_t.ap, scale=1.0 / 128, reads=[st, self.eps_t], writes=[st])
        B.I("scalar", "activation", st.ap, st.ap, AF.Exp, scale=-0.5, reads=[st], writes=[st])
        B.I("vector", "tensor_tensor", yn.ap.rearrange("p (h d) -> p h d", h=16), y.ap.rearrange("p (h d) -> p h d", h=16),
            st.ap.unsqueeze(2).to_broadcast([128, 16, 128]), ALU.mult, reads=[y, st], writes=[yn])
        for b0 in range(0, 16, 8):
            ps = B.bank()
            pv = ps.ap.bitcast(BF16)
            for i in range(8):
                B.I("tensor", "transpose", pv[:, i * 128:(i + 1) * 128], yn.ap[:, (b0 + i) * 128:(b0 + i + 1) * 128],
                    self.ident_b.ap, reads=[yn, self.ident_b], writes=[ps])
            for i in range(8):
                b = b0 + i
                B.I("vector", "scalar_tensor_tensor", o.ap[:, b, :], pv[:, i * 128:(i + 1) * 128], ng.ap[:, b:b + 1], g.ap[:, b, :],
                    ALU.mult, ALU.mult, reads=[ps, ng, g, o], writes=[o])
        B.dma(self.YT[0:2048, t0:t0 + 128].rearrange("(b p) t -> p b t", p=128), o.ap, [o], [])
    B.release(m)
    B.release(m_ee)
    return self.d["hgrn_w_out"], 1, self.YT


def _epi_headnorm(self, nblk, gblk0, ng, last):
    B = self
    W = nblk * 128
    m = B.mark()
    yp = Pool(B, 2, [128, W], F32, "ey"); sqp = Pool(B, 1, [128, W], F32, "esq")
    ynp = Pool(B, 2, [128, W], BF16, "eyn")
    gp = Pool(B, 2, [128, nblk, 128], F32, "eg"); op = Pool(B, 2, [128, nblk, 128], BF16, "eo")
    stp = Pool(B, 2, [128, nblk], F32, "est")
    for c in range(2 if last else 0, NT):
        t0 = c * 128
        y = yp.next(); sq = sqp.next(); yn = ynp.next(); g = gp.next(); o = op.next(); st = stp.next()
        B.dma(y.ap, self.Y[t0:t0 + 128, 0:W], [], [y])
        B.dma(g.ap, self.U[gblk0:gblk0 + nblk, :, t0:t0 + 128].rearrange("b p t -> p b t"), [], [g])
        B.I("scalar", "activation", g.ap, g.ap, AF.Silu, reads=[g], writes=[g])
        B.I("scalar", "activation", sq.ap, y.ap, AF.Square, reads=[y], writes=[sq])
        B.I("vector", "tensor_reduce", st.ap, sq.ap.rearrange("p (h d) -> p h d", h=nblk), AX.X, ALU.add, reads=[sq], writes=[st])
        B.I("scalar", "activation", st.ap, st.ap, AF.Ln, bias=self.eps_t.ap, scale=1.0 / 128, reads=[st, self.eps_t], writes=[st])
        B.I("scalar", "activation", st.ap, st.ap, AF.Exp, scale=-0.5, reads=[st], writes=[st])
        B.I("vector", "tensor_tensor", yn.ap.rearrange("p (h d) -> p h d", h=nblk), y.ap.rearrange("p (h d) -> p h d", h=nblk),
            st.ap.unsqueeze(2).to_broadcast([128, nblk, 128]), ALU.mult, reads=[y, st], writes=[yn])
        for b0 in range(0, nblk, 8):
            ps = B.bank()
            pv = ps.ap.bitcast(BF16)
            for i in range(8):
                B.I("tensor", "transpose", pv[:, i * 128:(i + 1) * 128], yn.ap[:, (b0 + i) * 128:(b0 + i + 1) * 128],
                    self.ident_b.ap, reads=[yn, self.ident_b], writes=[ps])
            for i in range(8):
                b = b0 + i
                B.I("vector", "scalar_tensor_tensor", o.ap[:, b, :], pv[:, i * 128:(i + 1) * 128], ng.ap[:, b:b + 1], g.ap[:, b, :],
                    ALU.mult, ALU.mult, reads=[ps, ng, g, o], writes=[o])
        B.dma(self.YT[0:W, t0:t0 + 128].rearrange("(b p) t -> p b t", p=128), o.ap, [o], [])
    B.release(m)


def _gdn_mixer(self, li, last):
    B = self
    P = self.P
    _inproj_dump(self, self.d["gdn_w_in"], 97)
    if not hasattr(self, "BT"):
        self.BT = P.dram("BT", [2, T, 32], F32)
        self.VF = P.dram("VF", [32, 128, T], BF16)
    m0 = B.mark()
    cw = B.alloc([128, 64, 3], F32, "gcw")
    B.dma(cw.ap, self.d["gdn_cw"], [], [cw])
    obp = Pool(B, 2, [128, T], BF16, "gob")
    sqp = Pool(B, 2, [128, 512], F32, "gsq")
    rsp = Pool(B, 2, [128, 512], F32, "grs")

    def sink(i, blk, o):
        ob = obp.next()
        if blk < 32:
            for lo, hi in TG6:
                n = hi - lo
                sq = sqp.next(); rs = rsp.next()
                B.I("scalar", "activation", sq.ap[:, 0:n], o.ap[:, lo:hi], AF.Square, reads=[o], writes=[sq])
                ps = B.bank()
                B.mm(ps.ap[:, 0:n], self.ones_f.ap, sq.ap[:, 0:n], True, True, [self.ones_f, sq], [ps])
                B.I("scalar", "activation", rs.ap[:, 0:n], ps.ap[:, 0:n], AF.Ln, bias=self.eps_t.ap, reads=[ps, self.eps_t], writes=[rs])
                B.I("scalar", "activation", rs.ap[:, 0:n], rs.ap[:, 0:n], AF.Exp, scale=-0.5, reads=[rs], writes=[rs])
                B.I("vector", "scalar_tensor_tensor", ob.ap[:, lo:hi], o.ap[:, lo:hi], (128.0 ** -0.5) if blk < 16 else 1.0, rs.ap[:, 0:n],
                    ALU.mult, ALU.mult, reads=[o, rs, ob], writes=[ob])
            B.dma((self.QT[blk] if blk < 16 else self.KT[blk - 16]), ob.ap, [ob], [])
        else:
            B.I("gpsimd", "tensor_copy", ob.ap, o.ap, reads=[o], writes=[ob])
            B.dma(self.VF[blk - 32], ob.ap, [ob], [])
    _conv_blocks(self, list(range(0, 64)), cw, None, sink)
    m = B.mark()
    par = B.alloc([128, 4], F32, "gpar")
    B.dma(par.ap[:, 0:3], self.d["gdn_par"], [], [par])
    B.I("scalar", "activation", par.ap[:, 1:2], par.ap[:, 1:2], AF.Exp, reads=[par], writes=[par])
    B.I("vector", "tensor_tensor", par.ap[:, 2:3], par.ap[:, 2:3], par.ap[:, 1:2], ALU.mult, reads=[par], writes=[par])
    B.I("vector", "memset", par.ap[:, 3:4], 1.0, reads=[par], writes=[par])
    u = B.alloc([128, T], F32, "gu"); sg = B.alloc([128, T], F32, "gsg")
    B.dma(u.ap, self.U[96], [], [u])
    B.I("scalar", "activation", sg.ap, u.ap, AF.Sigmoid, reads=[u], writes=[sg])
    B.I("scalar", "activation", u.ap, u.ap, AF.Exp, bias=par.ap[:, 0:1], reads=[u, par], writes=[u])
    B.I("scalar", "activation", u.ap, u.ap, AF.Ln, bias=par.ap[:, 3:4], reads=[u, par], writes=[u])
    B.I("vector", "tensor_scalar", u.ap, u.ap, par.ap[:, 2:3], None, ALU.mult, reads=[u, par], writes=[u])
    op = Pool(B, 2, [128, 4, 128], F32, "gdo")
    for src, dst, coff in ((u, self.LA, 64), (sg, self.BT, 0)):
        for c0 in range(0, NT, 4):
            nt = min(4, NT - c0)
            ps = B.bank()
            for i in range(nt):
                B.I("tensor", "transpose", ps.ap[:, i * 128:(i + 1) * 128], src.ap[:, (c0 + i) * 128:(c0 + i + 1) * 128],
                    self.ident_f.ap, reads=[src, self.ident_f], writes=[ps])
            o = op.next()
            B.I("vector", "tensor_copy", o.ap[:, 0:nt, :], ps.ap[:, 0:nt * 128].rearrange("p (c f) -> p c f", f=128), reads=[ps], writes=[o])
            for dr in range(2):
                B.dma(dst[dr, c0 * 128:(c0 + nt) * 128, 0:32].rearrange("(c t) h -> t c h", t=128),
                      o.ap[:, 0:nt, coff + dr * 32:coff + (dr + 1) * 32], [o], [])
    B.release(m)
    B.release(m0)
    _transpose_blocks2(self, lambda b: self.KT[b], 16, self.KK, BF16)
    _transpose_blocks2(self, lambda b: self.VF[b], 32, self.VV, BF16)
    mh = B.mark()
    pos = [B.alloc([128, 128], F32, "pos0"), B.alloc([128, 128], F32, "pos1")]
    for i in range(2):
        B.dma(pos[i].ap, self.d["pos"][i], [], [pos[i]])
    kkp = Pool(B, 4, [128, 128], F32, "gkk")
    Lp = Pool(B, 8, [128, 128], F32, "gL"); LTp = Pool(B, 8, [128, 128], F32, "gLT")
    Xp = Pool(B, 8, [128, 128], F32, "gX"); Ep = Pool(B, 8, [128, 128], F32, "gE")
    vnp = Pool(B, 8, [128, 128], BF16, "gvn")
    GH = 3
    Mps = [Pool(B, 28, [128, 128], F32, f"gM{i}") for i in range(GH)]
    bmask = [B.alloc([128, 128], F32, f"bm{i}") for i in range(4)]
    for i in range(4):
        B.dma(bmask[i].ap, self.d["bmask"][i], [], [bmask[i]])
    state = {}

    def vhook(h, slot, dr, qbs, kc, vc, la, sm, rh, Sb, hs):
        Mp = Mps[slot]
        g = qbs[0]
        cs, ec = sm.ap[:, 0, :], sm.ap[:, 2, :]
        beta = la.ap[:, 1, :]

        def msk(src, mi):
            t = Mp.next()
            B.I("gpsimd", "tensor_tensor", t.ap, src.ap, bmask[mi].ap, ALU.mult, reads=[src, bmask[mi]], writes=[t])
            return t

        def mmf(a, b):
            ps = B.bank(True)
            B.mm(ps.ap[:, 0:128], a.ap, b.ap, True, True, [a, b], [ps])
            return ps

        def ev_copy(ps):
            t = Mp.next()
            B.I("scalar", "copy", t.ap, ps.ap[:, 0:128], reads=[ps], writes=[t])
            B.done(ps)
            return t

        def ev_add(base, ps, op):
            t = Mp.next()
            B.I("vector", "tensor_tensor", t.ap, base.ap, ps.ap[:, 0:128], op, reads=[base, ps], writes=[t])
            B.done(ps)
            return t
        pk = None
        if state.get("g") != (g, id(kc)):
            state["g"] = (g, id(kc))
            pk = B.bank(True)
            B.mm(pk.ap[:, 0:128], kc.ap[:, g, :], kc.ap[:, g, :], True, True, [kc], [pk])
            state["kk"] = kkp.next()
        kk = state["kk"]
        p2 = B.bank(True)
        B.mm(p2.ap[:, 0:128], self.ones_f.ap, rh.ap, True, False, [self.ones_f, rh], [p2])
        B.mm(p2.ap[:, 0:128], self.ident_f.ap, pos[dr].ap, False, True, [self.ident_f, pos[dr]], [p2])
        pk2 = B.bank(True)
        B.mm(pk2.ap[:, 0:128], kc.ap[:, g, :], Sb.ap[:, 0, hs], True, True, [kc, ("Sb", h)], [pk2])
        yield
        if pk is not None:
            B.I("scalar", "copy", kk.ap, pk.ap[:, 0:128], reads=[pk], writes=[kk])
            B.done(pk)
        E = Ep.next()
        B.I("scalar", "activation", E.ap, p2.ap[:, 0:128], AF.Exp, bias=cs[:, h:h + 1], scale=-1.0, reads=[p2, sm], writes=[E])
        B.done(p2)
        X = Xp.next()
        B.I("vector", "tensor_scalar", X.ap, pk2.ap[:, 0:128], ec[:, h:h + 1], -1.0, ALU.mult, ALU.mult, reads=[pk2, sm], writes=[X])
        B.done(pk2)
        B.I("vector", "tensor_tensor", X.ap, X.ap, vc.ap[:, hs], ALU.add, reads=[X, vc], writes=[X])
        B.I("vector", "tensor_scalar", X.ap, X.ap, beta[:, h:h + 1], None, ALU.mult, reads=[X, la], writes=[X])
        yield
        L = Lp.next()
        B.I("vector", "scalar_tensor_tensor", L.ap, E.ap, beta[:, h:h + 1], kk.ap, ALU.mult, ALU.mult, reads=[E, la, kk], writes=[L])
        yield
        pt = B.bank(True)
        B.I("tensor", "transpose", pt.ap[:, 0:128], L.ap, self.ident_f.ap, reads=[L, self.ident_f], writes=[pt])
        Dm = msk(L, 0)
        Cs = [None, msk(L, 1), msk(L, 2), msk(L, 3)]
        yield
        LT = LTp.next()
        B.I("scalar", "copy", LT.ap, pt.ap[:, 0:128], reads=[pt], writes=[LT])
        B.done(pt)
        N = Mp.next()
        B.I("vector", "tensor_tensor", N.ap, self.ident_f.ap, Dm.ap, ALU.subtract, reads=[self.ident_f, Dm], writes=[N])
        yield
        DTm = msk(LT, 0)
        CTs = [None, msk(LT, 1), msk(LT, 2), msk(LT, 3)]
        yield
        NT = Mp.next()
        B.I("vector", "tensor_tensor", NT.ap, self.ident_f.ap, DTm.ap, ALU.subtract, reads=[self.ident_f, DTm], writes=[NT])
        Pm, PT = Dm, DTm
        for lev in range(3):
            q1 = mmf(PT, Pm)
            q2 = mmf(Pm, PT) if lev < 2 else None
            yield
            P2 = ev_copy(q1)
            P2T = ev_copy(q2) if lev < 2 else None
            yield
            a = mmf(NT, P2); b = mmf(P2, NT)
            yield
            N2 = ev_add(N, a, ALU.add); NT2 = ev_add(NT, b, ALU.add)
            N, NT = N2, NT2
            Pm, PT = P2, P2T
            yield
        Mm, MT = N, NT
        for lev in range(1, 4):
            C, CT = Cs[lev], CTs[lev]
            qa = mmf(CT, Mm) if lev < 3 else None
            qb_ = mmf(C, MT)
            yield
            Q1 = ev_copy(qa) if lev < 3 else None
            R1 = ev_copy(qb_)
            yield
            qc_ = mmf(MT, Q1) if lev < 3 else None
            qd = mmf(Mm, R1)
            yield
            if lev < 3:
                Mn = ev_add(Mm, qc_, ALU.subtract)
            MTn = ev_add(MT, qd, ALU.subtract)
            if lev < 3:
                Mm = Mn
            MT = MTn
            yield
        px = B.bank(True)
        B.mm(px.ap[:, 0:128], MT.ap, X.ap, True, True, [MT, X], [px])
        yield
        vn = vnp.next()
        B.I("scalar", "copy", vn.ap, px.ap[:, 0:128], reads=[px], writes=[vn])
        B.done(px)
        yield
        return vn.ap, [vn]
    _scan(self, 32, 1, lambda h: [h // 2], 128, True, False, aux=self.BT, vhook=(vhook if getattr(self, "use_hook", True) else None), G=GH)
    B.release(mh)
    m = B.mark()
    ng = B.alloc([128, 32], F32, "gng")
    B.dma(ng.ap, self.d["gdn_ng"], [], [ng])
    _epi_headnorm(self, 32, 64, ng, last)
    B.release(m)
    return self.d["gdn_w_out"], 2, self.YT


def build_program(nlayers=4, stub=False, debug_xt=False, layers=None):
    B = Builder()
    B.dbg_mlp = debug_xt
    _tok_init(B, nlayers)
    layers = list(range(nlayers)) if layers is None else layers
    if not stub:
        _mix_init(B, layers)
    for li in layers:
        last = (li == 3)
        _mod(B, li)
        B.barrier()
        if not stub:
            _set_gain(B, 0)
            B.barrier()
            W, nkq, YT = _mixer(B, li, last)
            B.barrier()
            _outproj(B, li, W, nkq, YT, last)
        _mlp(B, li, last)
        if debug_xt:
            dl = B.P.dram(f"dbgL{li}", [D, T], F32, kind="ExternalOutput")
            B.barrier()
            for k in range(4):
                B.dma(dl[k * 512:(k + 1) * 512, :], B.XT[k * 512:(k + 1) * 512, :], [], [("dbgL", li, k)])
            B.dbgl_keys = getattr(B, "dbgl_keys", []) + [("dbgL", li, k) for k in range(4)]
            B.barrier()
    if debug_xt:
        dm = B.P.dram("dbg_mod", [128, 96, 2], F32, kind="ExternalOutput")
        B.dma(dm, B.modT.ap, [B.modT], [("dbgm",)])
        B.dbg = B.P.dram("dbgXT", [D, T], F32, kind="ExternalOutput")
        for k in range(4):
            B.dma(B.dbg[k * 512:(k + 1) * 512, :], B.XT[k * 512:(k + 1) * 512, :], [], [("dbg", k)])
        B.out_keys = [("dbg", k) for k in range(4)] + [("dbgm",), ("dbgh2",), ("dbghid",)] + getattr(B, "dbgl_keys", [])
    else:
        _final(B)
    nc = B.P.emit(final_keys=B.out_keys)
    return B, nc


def host_inputs(inputs, b, nl=4):
    f = np.float32
    x = np.asarray(inputs["x"], f); ctx = np.asarray(inputs["ctx"], f)
    m = {}
    m["xT0"] = np.ascontiguousarray(np.concatenate([ctx[b], x[b]], axis=0).T)
    cv = np.stack([np.asarray(inputs["c"], f)[b], np.asarray(inputs["c_ctx"], f)], axis=-1)
    m["cvec"] = np.ascontiguousarray(cv.reshape(16, 128, 2).transpose(1, 0, 2))
    m["ada_w"] = np.asarray(inputs["ada_w"], f)[:nl]
    m["ada_bT"] = np.ascontiguousarray(np.asarray(inputs["ada_b"], f).reshape(4, 96, 128).transpose(0, 2, 1))[:nl]
    m["norm_gT"] = np.ascontiguousarray(np.asarray(inputs["norm_g"], f).reshape(4, 2, 16, 128).transpose(0, 1, 3, 2))[:nl]
    m["mlp_w1"] = np.asarray(inputs["mlp_w1"], f)[:nl]
    m["mlp_w2"] = np.asarray(inputs["mlp_w2"], f)[:nl]
    m["final_gT"] = np.ascontiguousarray(np.asarray(inputs["final_g"], f).reshape(16, 128).T)
    m["ident_f"] = np.eye(128, dtype=f)
    m["ones_f"] = np.ones((128, 128), f)
    return m


_CACHE = {}


def kernel(**inputs):
    if "prog" not in _CACHE:
        _CACHE["prog"] = build_program()
    B, nc = _CACHE["prog"]
    in_maps = []
    for b in range(4):
        m = host_inputs(inputs, b)
        m.update(host_mixer_inputs(inputs, b))
        in_maps.append(m)
    res = run_bass_kernel_spmd(nc, in_maps, core_ids=list(range(4)))
    out = np.stack([np.ascontiguousarray(res.results[b]["outT"].T) for b in range(4)], axis=0)
    return out.astype(np.float32)
```

```python
import numpy as np
from concourse.bass_utils import run_bass_kernel_spmd
from contextlib import ExitStack
import numpy as np
import concourse.bass as bass
import concourse.mybir as mybir

F32 = mybir.dt.float32
BF16 = mybir.dt.bfloat16
AF = mybir.ActivationFunctionType
ALU = mybir.AluOpType
AX = mybir.AxisListType

COMPUTE = ("tensor", "vector", "scalar", "gpsimd")
ISSUERS = ("sync", "gpsimd", "scalar")
KDMA = 12
EPOCH = 20000


class Prog:
    def __init__(self):
        self.nc = bass.Bass("TRN2", target_bir_lowering=False)
        self.ops = []
        self.stack = ExitStack()
        self.last_w = {}
        self.readers = {}
        self.n_sb = 0

    def sbuf(self, shape, dtype, name=None):
        self.n_sb += 1
        name = name or f"sb{self.n_sb}"
        return self.stack.enter_context(self.nc.sbuf_tensor(name, list(shape), dtype))

    def psum(self, shape, dtype, name=None):
        self.n_sb += 1
        name = name or f"ps{self.n_sb}"
        return self.stack.enter_context(self.nc.psum_tensor(name, list(shape), dtype))

    def dram(self, name, shape, dtype, kind="Internal"):
        return self.nc.dram_tensor(name, list(shape), dtype, kind=kind).ap()

    def _deps(self, reads, writes):
        deps = set()
        for k in reads:
            if k in self.last_w:
                deps.add(self.last_w[k])
        for k in writes:
            if k in self.last_w:
                deps.add(self.last_w[k])
            for r in self.readers.get(k, ()):
                deps.add(r)
        return deps

    def _commit(self, idx, reads, writes):
        for k in reads:
            self.readers.setdefault(k, []).append(idx)
        for k in writes:
            self.last_w[k] = idx
            self.readers[k] = []

    def op(self, eng, fn, reads=(), writes=(), floor=None, extra=()):
        idx = len(self.ops)
        deps = self._deps(reads, writes)
        deps.update(extra)
        if floor is not None:
            deps.add(floor)
        self.ops.append(dict(eng=eng, fn=fn, deps=deps, dma=False))
        self._commit(idx, reads, writes)
        return idx

    def dma(self, issuer, out, in_, reads=(), writes=(), floor=None, **kw):
        idx = len(self.ops)
        deps = self._deps(reads, writes)
        if floor is not None:
            deps.add(floor)
        self.ops.append(dict(eng=issuer, fn=lambda e: e.dma_start(out=out, in_=in_, **kw),
                             deps=deps, dma=True))
        self._commit(idx, reads, writes)
        return idx

    def mm(self, out, lhsT, rhs, start, stop, reads, writes, **kw):
        return self.op("tensor", lambda e: e.matmul(out, lhsT, rhs, start=start, stop=stop, **kw),
                       reads, writes)

    def emit(self, final_keys=()):
        nc = self.nc
        ops = self.ops
        has_dep = [False] * len(ops)
        for o in ops:
            for d in o["deps"]:
                has_dep[d] = True
        final_deps = set()
        for k in final_keys:
            if k in self.last_w:
                final_deps.add(self.last_w[k])
        for d in final_deps:
            has_dep[d] = True
        cnt = {e: 0 for e in COMPUTE}
        dcnt = {e: 0 for e in ISSUERS}
        for i, o in enumerate(ops):
            if o["dma"]:
                n = dcnt[o["eng"]]
                dcnt[o["eng"]] += 1
                o["dn"] = n
                o["done"] = (("d", o["eng"], n % KDMA), 16 * (n // KDMA + 1))
            else:
                if has_dep[i]:
                    c = cnt[o["eng"]]
                    cnt[o["eng"]] += 1
                    o["done"] = (("c", o["eng"], c // EPOCH), c % EPOCH + 1)
                    o["inc"] = True
                else:
                    o["done"] = None
                    o["inc"] = False
        semkeys = set()
        for o in ops:
            if o.get("done"):
                semkeys.add(o["done"][0])
        sems = {}
        for k in sorted(semkeys):
            sems[k] = self.stack.enter_context(nc.semaphore("s_" + "_".join(map(str, k))))
        streams = {}
        for i, o in enumerate(ops):
            streams.setdefault(o["eng"], []).append(i)
        stats = dict(waits=0)

        def run_engine(engname, e):
            waited = {}
            dma_hist = []
            for i in streams.get(engname, []):
                o = ops[i]
                need = {}
                for d in o["deps"]:
                    od = ops[d]
                    if od["done"] is None:
                        continue
                    if (not od["dma"]) and od["eng"] == "tensor" and engname == "tensor" and not o["dma"]:
                        continue
                    sk, v = od["done"]
                    if need.get(sk, 0) < v:
                        need[sk] = v
                if o["dma"]:
                    n = o["dn"]
                    if n >= KDMA:
                        sk = ("d", engname, n % KDMA)
                        v = 16 * (n // KDMA)
                        if need.get(sk, 0) < v:
                            need[sk] = v
                for sk, v in need.items():
                    if waited.get(sk, 0) >= v:
                        continue
                    e.wait_ge(sems[sk], v)
                    waited[sk] = v
                    stats["waits"] += 1
                ins = o["fn"](e)
                if o["dma"]:
                    ins.then_inc(sems[o["done"][0]], 16)
                elif o["inc"]:
                    ins.then_inc(sems[o["done"][0]], 1)
            if engname == "sync":
                need = {}
                for d in final_deps:
                    sk, v = ops[d]["done"]
                    if need.get(sk, 0) < v:
                        need[sk] = v
                for sk, v in need.items():
                    if waited.get(sk, 0) < v:
                        e.wait_ge(sems[sk], v)

        with nc.Block() as block:
            @block.sync
            def _(e):
                run_engine("sync", e)

            @block.tensor
            def _(e):
                run_engine("tensor", e)

            @block.vector
            def _(e):
                run_engine("vector", e)

            @block.scalar
            def _(e):
                run_engine("scalar", e)

            @block.gpsimd
            def _(e):
                run_engine("gpsimd", e)
        self.stats = stats
        self.stack.close()
        return nc


import math

T = 2304
LC = 256
D = 2048
KC = 16
NT = 18
TG6 = [(0, 256), (256, 768), (768, 1280), (1280, 1536), (1536, 2048), (2048, 2304)]
SETS = [(0, 1), (2, 3), (4, 5)]
ARENA_BYTES = 204 * 1024


def dsize(dt):
    return 4 if dt == F32 else 2


class Tl:
    def __init__(self, ap, key):
        self.ap = ap
        self.key = key

    def __getitem__(self, idx):
        return self.ap[idx]


class Pool:
    def __init__(self, B, n, shape, dtype, name):
        self.tiles = [B.alloc(shape, dtype, f"{name}{i}") for i in range(n)]
        self.i = 0

    def next(self):
        t = self.tiles[self.i % len(self.tiles)]
        self.i += 1
        return t


class Builder:
    def __init__(self):
        self.P = Prog()
        P = self.P
        self.arena = P.sbuf([128, ARENA_BYTES // 4], F32, name="arena")
        self.aoff = 0
        self.nkey = 0
        self.banks = [Tl(P.psum([128, 512], F32, name=f"bank{i}"), ("bank", i)) for i in range(8)]
        self.bi = 0
        self.bar_tile = P.sbuf([128, 8], F32, name="bar")
        self.floor = None

    def mark(self):
        return self.aoff

    def release(self, m):
        if self.aoff > m:
            self.barrier()
        self.aoff = m

    def alloc(self, shape, dtype, name="t"):
        n = 1
        for s in shape[1:]:
            n *= s
        nb = n * dsize(dtype)
        nb = (nb + 63) // 64 * 64
        assert self.aoff + nb <= ARENA_BYTES, f"arena overflow {name} {self.aoff + nb}"
        v = self.arena[:, self.aoff // 4:(self.aoff + nb) // 4]
        self.aoff += nb
        if dtype != F32:
            v = v.bitcast(dtype)
        v = v[:, 0:n]
        if len(shape) == 3:
            v = v.rearrange("p (a b) -> p a b", a=shape[1])
        elif len(shape) == 4:
            v = v.rearrange("p (a b c) -> p a b c", a=shape[1], b=shape[2])
        if shape[0] != 128:
            v = v[0:shape[0]]
        self.nkey += 1
        return Tl(v, (name, self.nkey))

    def bank(self, track=False):
        if not hasattr(self, "held"):
            self.held = {}
        for _ in range(8):
            b = self.banks[self.bi % 8]
            self.bi += 1
            if not self.held.get(b.key):
                if track:
                    self.held[b.key] = True
                return b
        raise AssertionError("no free psum bank")

    def done(self, b):
        self.held[b.key] = False

    def op(self, eng, fn, reads=(), writes=()):
        return self.P.op(eng, fn, self._k(reads), self._k(writes), floor=self.floor)

    def I(self, eng, meth, *args, reads=(), writes=(), **kw):
        r = self._k(reads); w = list(self._k(writes))
        bk = [k for k in r if isinstance(k, tuple) and k and k[0] == "bank"]
        r = [k for k in r if k not in bk]
        for k in bk:
            if k not in w:
                w.append(k)
        return self.P.op(eng, lambda e: getattr(e, meth)(*args, **kw), r, w, floor=self.floor)

    def dma(self, out, in_, reads=(), writes=(), issuer="sync", **kw):
        return self.P.dma(issuer, out, in_, self._k(reads), self._k(writes), floor=self.floor, **kw)

    def mm(self, out, lhsT, rhs, start, stop, reads, writes):
        return self.I("tensor", "matmul", out, lhsT, rhs, start=start, stop=stop, reads=reads, writes=writes)

    @staticmethod
    def _k(lst):
        return [x.key if isinstance(x, Tl) else x for x in lst]

    def barrier(self):
        P = self.P
        last = {}
        dmas = {}
        for i, o in enumerate(P.ops):
            if o["dma"]:
                dmas.setdefault(o["eng"], []).append(i)
            else:
                last[o["eng"]] = i
        deps = set(last.values())
        for q, l in dmas.items():
            deps.update(l[-KDMA:])
        bt = self.bar_tile
        idx = P.op("vector", lambda e: e.memset(bt[:], 0.0), [], [("bar", len(P.ops))], floor=self.floor, extra=deps)
        self.floor = idx


def _tok_init(self, nl=4):
    P = self.P
    B = self
    self.d = {}

    def inp(name, shape, dt=F32):
        self.d[name] = P.dram(name, shape, dt, kind="ExternalInput")
        return self.d[name]
    self.inp = inp
    inp("xT0", [D, T]); inp("cvec", [128, 16, 2])
    inp("ada_w", [nl, D, 6 * D]); inp("ada_bT", [nl, 128, 96]); inp("norm_gT", [nl, 2, 128, 16])
    inp("mlp_w1", [nl, D, 4 * D]); inp("mlp_w2", [nl, 4 * D, D]); inp("final_gT", [128, 16])
    inp("ident_f", [128, 128]); inp("ones_f", [128, 128])
    self.XT = P.dram("XT", [D, T], F32)
    self.outT = P.dram("outT", [D, T - LC], F32, kind="ExternalOutput")
    self.ident_f = B.alloc([128, 128], F32, "ident_f")
    self.ones_f = B.alloc([128, 128], F32, "ones_f")
    self.ident_b = B.alloc([128, 128], BF16, "ident_b")
    self.cond = B.alloc([128, 16, 2], BF16, "cond")
    self.modT = B.alloc([128, 96, 2], F32, "modT")
    self.gA = B.alloc([128, 16, 2], F32, "gA")
    self.adab = B.alloc([128, 96], F32, "adab")
    self.ng = B.alloc([128, 2, 16], F32, "ng")
    self.eps_t = B.alloc([128, 1], F32, "eps")
    self.zero_t = B.alloc([128, 1], F32, "zero")
    B.dma(self.ident_f.ap, self.d["ident_f"], [], [self.ident_f])
    B.dma(self.ones_f.ap, self.d["ones_f"], [], [self.ones_f])
    B.I("vector", "tensor_copy", self.ident_b.ap, self.ident_f.ap, reads=[self.ident_f], writes=[self.ident_b])
    B.I("vector", "memset", self.eps_t.ap, 1e-6, writes=[self.eps_t])
    B.I("vector", "memset", self.zero_t.ap, 0.0, writes=[self.zero_t])
    m = B.mark()
    cv = B.alloc([128, 16, 2], F32, "cv")
    B.dma(cv.ap, self.d["cvec"], [], [cv])
    B.I("scalar", "activation", self.cond.ap, cv.ap, AF.Silu, reads=[cv], writes=[self.cond])
    B.barrier()
    B.release(m)
    for k in range(4):
        B.dma(self.XT[k * 512:(k + 1) * 512, :], self.d["xT0"][k * 512:(k + 1) * 512, :], [], [])
    B.barrier()


def _seg_tok(lo):
    return 1 if lo < LC else 0


def _gemm(self, W, nkq, col_blocks, act, groups, evac, kc=16):
    B = self
    m = B.mark()
    stage = Pool(B, 3, [128, kc, 128], F32, "wst")
    wbf = Pool(B, 4, [128, kc, 128], BF16, "wbf")
    for j in col_blocks:
        bk = {}
        for kq in range(nkq):
            st = stage.next()
            src = W[kq * kc * 128:(kq + 1) * kc * 128, j * 128:(j + 1) * 128].rearrange("(k p) c -> p k c", p=128)
            B.dma(st.ap, src, [], [st])
            wb = wbf.next()
            B.I("vector", "tensor_copy", wb.ap, st.ap, reads=[st], writes=[wb])
            for gi, (lo, hi) in enumerate(groups):
                if kq == 0:
                    bk[gi] = B.bank()
                ps = bk[gi]
                for k in range(kc):
                    a_ap, a_keys = act(kq * kc + k, lo, hi)
                    B.mm(ps.ap[:, 0:hi - lo], wb.ap[:, k, :], a_ap, kq == 0 and k == 0,
                         kq == nkq - 1 and k == kc - 1, [wb] + list(a_keys), [ps])
        for gi, (lo, hi) in enumerate(groups):
            evac(j, gi, lo, hi, bk[gi])
    B.release(m)


def _mod(self, li):
    B = self
    m = B.mark()
    B.dma(self.adab.ap, self.d["ada_bT"][li], [], [self.adab])
    B.dma(self.ng.ap, self.d["norm_gT"][li].rearrange("a p k -> p a k"), [], [self.ng])
    cond = self.cond

    def act(k, lo, hi):
        return cond.ap[:, k, :], [cond]

    def evac(j, gi, lo, hi, ps):
        B.I("vector", "tensor_scalar", self.modT.ap[:, j, :], ps.ap[:, 0:2], self.adab.ap[:, j:j + 1], None, ALU.add,
            reads=[ps, self.adab, self.modT], writes=[self.modT])
    _gemm(self, self.d["ada_w"][li], 1, range(96), act, [(0, 2)], evac)
    B.release(m)


def _set_gain(self, which):
    B = self
    sc = self.modT.ap[:, (16 + 48 * which):(32 + 48 * which), :]
    B.I("vector", "tensor_scalar", self.gA.ap, sc, 1.0, None, ALU.add, reads=[self.modT, self.gA], writes=[self.gA])
    B.I("vector", "tensor_tensor", self.gA.ap, self.gA.ap,
        self.ng.ap[:, which, :].unsqueeze(2).to_broadcast([128, 16, 2]), ALU.mult,
        reads=[self.gA, self.ng], writes=[self.gA])


def _adaln(self, src, groups, segs, gain_ap, shift_ap, out_fn, post=None):
    B = self
    m = B.mark()
    xp = Pool(B, 2, [128, 16, 512], F32, "xg")
    sqp = Pool(B, 3, [128, 512], F32, "sq")
    rsp = Pool(B, 2, [128, 512], F32, "rs")
    tmp = Pool(B, 3, [128, 512], F32, "tmp")
    for gi, (lo, hi) in enumerate(groups):
        n = hi - lo
        xg = xp.next()
        B.dma(xg.ap[:, :, 0:n], src[:, lo:hi].rearrange("(k p) t -> p k t", p=128), [], [xg])
        ps = B.bank()
        for k in range(16):
            sq = sqp.next()
            B.I("scalar", "activation", sq.ap[:, 0:n], xg.ap[:, k, 0:n], AF.Square, reads=[xg], writes=[sq])
            B.mm(ps.ap[:, 0:n], self.ones_f.ap, sq.ap[:, 0:n], k == 0, k == 15, [self.ones_f, sq], [ps])
        rs = rsp.next()
        B.I("scalar", "activation", rs.ap[:, 0:n], ps.ap[:, 0:n], AF.Ln, bias=self.eps_t.ap, scale=1.0 / D,
            reads=[ps, self.eps_t], writes=[rs])
        B.I("scalar", "activation", rs.ap[:, 0:n], rs.ap[:, 0:n], AF.Exp, scale=-0.5, reads=[rs], writes=[rs])
        s = segs[gi]
        for k in range(16):
            t = tmp.next()
            B.I("vector", "tensor_tensor", t.ap[:, 0:n], xg.ap[:, k, 0:n], rs.ap[:, 0:n], ALU.mult, reads=[xg, rs], writes=[t])
            o_ap, o_keys = out_fn(k, gi, lo, hi)
            g_ap, g_keys = gain_ap(k, s)
            b_ap, b_keys = shift_ap(k, s)
            B.I("scalar", "activation", o_ap, t.ap[:, 0:n], AF.Identity, bias=b_ap, scale=g_ap,
                reads=[t] + list(g_keys) + list(b_keys), writes=list(o_keys))
            if post is not None:
                post(k, gi, lo, hi)
    B.release(m)


def _residual_evac(self, gate_blk0, xpool):
    B = self

    def evac(j, gi, lo, hi, ps):
        n = hi - lo
        xp = xpool.next()
        B.dma(xp.ap[:, 0:n], self.XT[j * 128:(j + 1) * 128, lo:hi], [], [xp])
        s = _seg_tok(lo)
        B.I("vector", "scalar_tensor_tensor", xp.ap[:, 0:n], ps.ap[:, 0:n], self.modT.ap[:, gate_blk0 + j, s:s + 1],
            xp.ap[:, 0:n], ALU.mult, ALU.add, reads=[ps, self.modT, xp], writes=[xp])
        B.dma(self.XT[j * 128:(j + 1) * 128, lo:hi], xp.ap[:, 0:n], [xp], [])
    return evac


def _mlp(self, li, last):
    B = self
    _set_gain(self, 1)
    B.barrier()
    for si, (ga, gb) in enumerate(SETS):
        groups = [TG6[ga], TG6[gb]]
        if last and si == 0:
            groups = [TG6[gb]]
        base = groups[0][0]
        m = B.mark()
        h2 = B.alloc([128, 16, 768], BF16, "h2")
        _adaln(self, self.XT, groups, [_seg_tok(g[0]) for g in groups],
               lambda k, s: (self.gA.ap[:, k, s:s + 1], [self.gA]),
               lambda k, s: (self.modT.ap[:, 48 + k, s:s + 1], [self.modT]),
               lambda k, gi, lo, hi, h2=h2, base=base: (h2.ap[:, k, lo - base:hi - base], [h2]))
        hid = B.alloc([128, 64, 768], BF16, "hid")
        rp = Pool(B, 3, [128, 512], F32, "relu")

        def act1(k, lo, hi, h2=h2, base=base):
            return h2.ap[:, k, lo - base:hi - base], [h2]

        def evac1(j, gi, lo, hi, ps, hid=hid, base=base, rp=rp):
            n = hi - lo
            r = rp.next()
            B.I("scalar", "activation", r.ap[:, 0:n], ps.ap[:, 0:n], AF.Relu, reads=[ps], writes=[r])
            B.I("gpsimd", "tensor_tensor", hid.ap[:, j, lo - base:hi - base], r.ap[:, 0:n], r.ap[:, 0:n], ALU.mult,
                reads=[r, hid], writes=[hid])
        _gemm(self, self.d["mlp_w1"][li], 1, range(64), act1, groups, evac1)

        if getattr(self, "dbg_mlp", False) and si == 0 and li == 0:
            d1 = B.P.dram("dbg_h2", [128, 16, 768], BF16, kind="ExternalOutput")
            d2 = B.P.dram("dbg_hid", [128, 64, 768], BF16, kind="ExternalOutput")
            B.dma(d1, h2.ap, [h2], [("dbgh2",)])
            B.dma(d2, hid.ap, [hid], [("dbghid",)])

        def act2(k, lo, hi, hid=hid, base=base):
            return hid.ap[:, k, lo - base:hi - base], [hid]
        xpool = Pool(B, 3, [128, 512], F32, "xres")
        _gemm(self, self.d["mlp_w2"][li], 4, range(16), act2, groups, _residual_evac(self, 80, xpool))
        B.barrier()
        B.release(m)


def _outproj(self, li, W, nkq, YT, last):
    B = self
    halves = [[TG6[0], TG6[1], TG6[2]], [TG6[3], TG6[4], TG6[5]]]
    if last:
        halves[0] = [TG6[1], TG6[2]]
    for groups in halves:
        base = groups[0][0]
        ntok = groups[-1][1] - base
        m = B.mark()
        y = B.alloc([128, nkq * 16, 1280], BF16, "yT")
        for q in range(nkq * 2):
            B.dma(y.ap[:, q * 8:(q + 1) * 8, 0:ntok],
                  YT[q * 1024:(q + 1) * 1024, base:base + ntok].rearrange("(k p) t -> p k t", p=128), [], [y])

        def act(k, lo, hi, y=y, base=base):
            return y.ap[:, k, lo - base:hi - base], [y]
        xpool = Pool(B, 3, [128, 512], F32, "xres")
        _gemm(self, W, nkq, range(16), act, groups, _residual_evac(self, 32, xpool))
        B.barrier()
        B.release(m)


def _final(self):
    B = self
    m = B.mark()
    fg = B.alloc([128, 16], F32, "fg")
    B.dma(fg.ap, self.d["final_gT"], [], [fg])
    op = Pool(B, 3, [128, 512], F32, "fo")
    groups = TG6[1:]
    cur = {}

    def out_fn(k, gi, lo, hi):
        t = op.next()
        cur[0] = t
        return t.ap[:, 0:hi - lo], [t]

    def post(k, gi, lo, hi):
        t = cur[0]
        B.dma(self.outT[k * 128:(k + 1) * 128, lo - LC:hi - LC], t.ap[:, 0:hi - lo], [t], [("out", k, gi)])
        self.out_keys.append(("out", k, gi))
    self.out_keys = []
    _adaln(self, self.XT, groups, [0] * len(groups), lambda k, s: (fg.ap[:, k:k + 1], [fg]),
           lambda k, s: (self.zero_t.ap, [self.zero_t]), out_fn, post)
    B.release(m)


NEGV = -30000.0


def _mix_init(self, layers=(0, 1, 2, 3)):
    B = self
    P = self.P
    inp = self.inp
    inp("tri", [2, 128, 128]); inp("neg", [2, 128, 128])
    self.tri = [B.alloc([128, 128], F32, "tri0"), B.alloc([128, 128], F32, "tri1")]
    self.neg = [B.alloc([128, 128], F32, "neg0"), B.alloc([128, 128], F32, "neg1")]
    for i in range(2):
        B.dma(self.tri[i].ap, self.d["tri"][i], [], [self.tri[i]])
        B.dma(self.neg[i].ap, self.d["neg"][i], [], [self.neg[i]])
    self.U = P.dram("U", [97, 128, T], F32)
    self.QT = P.dram("QT", [16, 128, T], BF16)
    self.KT = P.dram("KT", [16, 128, T], BF16)
    self.KK = P.dram("KK", [T, 2048], BF16)
    self.VV = P.dram("VV", [T, 4096], BF16)
    self.LA = P.dram("LA", [2, T, 64], F32)
    self.WT = P.dram("WT", [2, T, 64], F32)
    self.Y = P.dram("Y", [T, 4096], F32)
    self.YT = P.dram("YT", [4096, T], BF16)
    if 1 in layers:
        inp("ret_w_in", [D, 12288]); inp("ret_w_out", [4096, D]); inp("ret_la", [2, T, 8])
        inp("rope_cos", [128, T]); inp("rope_sin", [128, T])
    if 3 in layers:
        inp("gdn_w_in", [D, 12416]); inp("gdn_w_out", [4096, D]); inp("gdn_cw", [128, 64, 3]); inp("gdn_par", [128, 3])
        inp("gdn_ng", [128, 32]); inp("pos", [2, 128, 128]); inp("bmask", [4, 128, 128])
    if 2 in layers:
        inp("hgrn_w_in", [D, 10240]); inp("hgrn_w_out", [D, D]); inp("hgrn_lbl", [128, 16, 4]); inp("hgrn_ng", [128, 16])
    if 0 in layers:
        inp("ssd_w_in", [D, 10368]); inp("ssd_w_out", [4096, D]); inp("ssd_cw", [128, 48, 3]); inp("ssd_cb", [128, 48])
        inp("ssd_par", [128, 2]); inp("ssd_dbc", [128, 64]); inp("ssd_ng", [128, 32])
    B.barrier()


def _inproj_dump(self, W, nblk):
    B = self
    m = B.mark()
    hT = B.alloc([128, 16, T], BF16, "hT")
    _adaln(self, self.XT, TG6, [_seg_tok(g[0]) for g in TG6],
           lambda k, s: (self.gA.ap[:, k, s:s + 1], [self.gA]),
           lambda k, s: (self.modT.ap[:, k, s:s + 1], [self.modT]),
           lambda k, gi, lo, hi: (hT.ap[:, k, lo:hi], [hT]))
    up = Pool(B, 3, [128, 512], F32, "uo")

    def act(k, lo, hi):
        return hT.ap[:, k, lo:hi], [hT]

    def evac(j, gi, lo, hi, ps):
        n = hi - lo
        u = up.next()
        B.I("scalar", "copy", u.ap[:, 0:n], ps.ap[:, 0:n], reads=[ps], writes=[u])
        B.dma(self.U[j, :, lo:hi], u.ap[:, 0:n], [u], [])
    _gemm(self, W, 1, range(nblk), act, TG6, evac)
    B.release(m)


def _transpose_blocks(self, src_fn, nblk, dst, dt_in, chunkcols=None):
    B = self
    m = B.mark()
    sp = Pool(B, 3, [128, T], dt_in, "tsrc")
    op = Pool(B, 3, [128, 1024], BF16, "tdst")
    ident = self.ident_b if dt_in == BF16 else self.ident_f
    for b in range(nblk):
        s = sp.next()
        B.dma(s.ap, src_fn(b), [], [s])
        for c0 in range(0, NT, 8):
            nt = min(8, NT - c0)
            ps = B.bank()
            pv = ps.ap.bitcast(BF16) if dt_in == BF16 else ps.ap
            for i in range(nt if dt_in == BF16 else 0):
                B.I("tensor", "transpose", pv[:, i * 128:(i + 1) * 128], s.ap[:, (c0 + i) * 128:(c0 + i + 1) * 128], ident.ap,
                    reads=[s, ident], writes=[ps])
            o = op.next()
            B.I("vector", "tensor_copy", o.ap[:, 0:nt * 128], pv[:, 0:nt * 128], reads=[ps], writes=[o])
            B.dma(dst[c0 * 128:(c0 + nt) * 128, b * 128:(b + 1) * 128].rearrange("(c t) f -> t c f", t=128),
                  o.ap[:, 0:nt * 128].rearrange("t (c f) -> t c f", f=128), [o], [])
    B.release(m)


def _scan(self, H, kq, qmap, Vd, decay, has_wt, Q=128, EE=None, LA=None, WT=None, QTs=None, KTs=None, KKs=None, aux=None, vhook=None, G=4):
    B = self
    LA = self.LA if LA is None else LA
    WT = self.WT if WT is None else WT
    QTs = QTs or [self.QT, self.QT]; KTs = KTs or [self.KT, self.KT]; KKs = KKs or [self.KK, self.KK]
    VT = H * Vd
    NQB = max(max(qmap(h)) for h in range(H)) + 1
    nch = T // Q
    nctx = LC // Q
    m = B.mark()
    S = B.alloc([128, kq, VT], F32, "S")
    Sb = B.alloc([128, kq, VT], BF16, "Sb")
    qp = Pool(B, 2, [128, NQB, Q], BF16, "qc")
    kp = Pool(B, 2, [128, NQB, Q], BF16, "kc")
    kkp = Pool(B, 2, [128, NQB * 128], BF16, "kkc")
    vp = Pool(B, 2, [128, VT], BF16, "vc")
    yp = Pool(B, 2, [128, VT], F32, "yc")
    lap = Pool(B, 2, [128, 2, H], F32, "lac")
    smallp = Pool(B, 2, [128, 6, H], F32, "small")
    rhp = Pool(B, 2 * G, [128, 128], F32, "rh")
    dtp = Pool(B, 2 * G, [128, 128], F32, "dt")
    scp = Pool(B, 2 * G, [128, 128], F32, "sc")
    atp = Pool(B, 2 * G, [128, 128], BF16, "at")
    kwp = Pool(B, 2 * G, [128, kq * 128], BF16, "kw")
    for dr in (0, 1):
        order = list(range(nch)) if dr == 0 else list(range(nctx - 1, -1, -1)) + list(range(nch - 1, nctx - 1, -1))
        for h in range(H):
            B.I("vector", "memset", S.ap[:, :, h * Vd:(h + 1) * Vd], 0.0, writes=[("S", h)])
            B.I("gpsimd", "memset", Sb.ap[:, :, h * Vd:(h + 1) * Vd], 0.0, writes=[("Sb", h)])
        tri = self.tri[dr]
        neg = self.neg[dr]
        for c in order:
            t0 = c * Q
            qc = qp.next(); kc = kp.next(); kk = kkp.next(); vc = vp.next(); yc = yp.next()
            ykeys = [(yc.key, h) for h in range(H)]
            B.dma(qc.ap, QTs[dr][0:NQB, :, t0:t0 + Q].rearrange("b p t -> p b t"), [], [qc])
            B.dma(kc.ap, KTs[dr][0:NQB, :, t0:t0 + Q].rearrange("b p t -> p b t"), [], [kc])
            B.dma(kk.ap[0:Q], KKs[dr][t0:t0 + Q, 0:NQB * 128], [], [kk])
            B.dma(vc.ap[0:Q], self.VV[t0:t0 + Q, 0:VT], [], [vc])
            la = sm = None
            cs = negb = ec = w = eend = tmp = None
            if decay:
                la = lap.next(); sm = smallp.next()
                B.dma(la.ap[0:Q, 0, :], LA[dr, t0:t0 + Q, 0:H], [], [la])
                if has_wt:
                    B.dma(la.ap[0:Q, 1, :], WT[dr, t0:t0 + Q, 0:H], [], [la])
                if aux is not None:
                    B.dma(la.ap[0:Q, 1, :], aux[dr, t0:t0 + Q, 0:H], [], [la])
                psc = B.bank()
                B.mm(psc.ap[0:Q, 0:H], tri.ap[0:Q, 0:Q], la.ap[0:Q, 0, :], True, True, [tri, la], [psc])
                B.mm(psc.ap[0:Q, 128:128 + H], self.ones_f.ap[0:Q, 0:Q], la.ap[0:Q, 0, :], True, True, [self.ones_f, la], [psc])
                cs, negb, ec, w, eend, tmp = [sm.ap[0:Q, i, :] for i in range(6)]
                B.I("vector", "tensor_copy", cs, psc.ap[0:Q, 0:H], reads=[psc], writes=[sm])
                if has_wt:
                    B.I("vector", "tensor_tensor", negb, la.ap[0:Q, 1, :], cs, ALU.subtract, reads=[sm, la], writes=[sm])
                else:
                    B.I("vector", "tensor_scalar", negb, cs, -1.0, None, ALU.mult, reads=[sm], writes=[sm])
                B.I("scalar", "activation", ec, cs, AF.Exp, reads=[sm], writes=[sm])
                B.I("vector", "tensor_tensor", tmp, psc.ap[0:Q, 128:128 + H], negb, ALU.add, reads=[psc, sm], writes=[sm])
                B.I("scalar", "activation", w, tmp, AF.Exp, reads=[sm], writes=[sm])
                B.I("scalar", "activation", eend, psc.ap[0:Q, 128:128 + H], AF.Exp, reads=[psc, sm], writes=[sm])
            shared = {"g": None, "sc": None}

            def head_gen(h, slot, qc=qc, kc=kc, kk=kk, vc=vc, yc=yc, la=la, sm=sm, shared=shared,
                         negb=negb, ec=ec, w=w, c=c, dr=dr):
                qbs = qmap(h)
                hs = slice(h * Vd, (h + 1) * Vd)
                yk = (yc.key, h)
                rh = None
                psd = None
                if decay:
                    rh = rhp.next()
                    B.I("vector", "tensor_scalar", rh.ap[0:Q, 0:Q], tri.ap[0:Q, 0:Q], la.ap[0:Q, 0, h:h + 1], None, ALU.mult,
                        reads=[tri, la], writes=[rh])
                    yield
                    psd = B.bank(True)
                    B.mm(psd.ap[0:Q, 0:Q], self.ones_f.ap[0:Q, 0:Q], rh.ap[0:Q, 0:Q], True, False, [self.ones_f, rh], [psd])
                    B.mm(psd.ap[0:Q, 0:Q], self.ident_f.ap[0:Q, 0:Q], neg.ap[0:Q, 0:Q], False, True, [self.ident_f, neg], [psd])
                if tuple(qbs) != shared["g"]:
                    shared["g"] = tuple(qbs)
                    pss = B.bank(True)
                    for i, qb in enumerate(qbs):
                        B.mm(pss.ap[0:Q, 0:Q], kc.ap[:, qb, :], qc.ap[:, qb, :], i == 0, i == len(qbs) - 1, [kc, qc], [pss])
                    sc = scp.next()
                    shared["sc"] = sc
                    yield
                    B.I("scalar", "copy", sc.ap[0:Q, 0:Q], pss.ap[0:Q, 0:Q], reads=[pss], writes=[sc])
                    B.done(pss)
                else:
                    sc = shared["sc"]
                    yield
                if decay:
                    dtm = dtp.next()
                    B.I("scalar", "activation", dtm.ap[0:Q, 0:Q], psd.ap[0:Q, 0:Q], AF.Exp, bias=negb[:, h:h + 1],
                        reads=[psd, sm], writes=[dtm])
                    B.done(psd)
                    dmat = dtm
                    yield
                else:
                    dmat = tri
                at = atp.next()
                B.I("vector", "tensor_tensor", at.ap[0:Q, 0:Q], dmat.ap[0:Q, 0:Q], sc.ap[0:Q, 0:Q], ALU.mult,
                    reads=[dmat, sc], writes=[at])
                kw = None
                if decay:
                    kw = kwp.next()
                    for i, qb in enumerate(qbs):
                        B.I("vector", "tensor_scalar", kw.ap[0:Q, i * 128:(i + 1) * 128], kk.ap[0:Q, qb * 128:(qb + 1) * 128],
                            w[:, h:h + 1], None, ALU.mult, reads=[kk, sm, kw], writes=[kw])
                yield
                v_ap, v_keys = vc.ap[0:Q, hs], [vc]
                if vhook is not None:
                    v_ap, v_keys = yield from vhook(h, slot, dr, qbs, kc, vc, la, sm, rh, Sb, hs)
                psy = B.bank(True)
                B.mm(psy.ap[0:Q, 0:Vd], at.ap[0:Q, 0:Q], v_ap, True, True, [at] + v_keys, [psy])
                psys = B.bank(True)
                for i, qb in enumerate(qbs):
                    B.mm(psys.ap[0:Q, 0:Vd], qc.ap[:, qb, :], Sb.ap[:, i, hs], i == 0, i == len(qbs) - 1, [qc, ("Sb", h)], [psys])
                yield
                B.I("scalar", "copy", yc.ap[0:Q, hs], psy.ap[0:Q, 0:Vd], reads=[psy, yk], writes=[yk])
                B.done(psy)
                if decay:
                    B.I("vector", "scalar_tensor_tensor", yc.ap[0:Q, hs], psys.ap[0:Q, 0:Vd], ec[:, h:h + 1], yc.ap[0:Q, hs],
                        ALU.mult, ALU.add, reads=[psys, sm, yk], writes=[yk])
                else:
                    B.I("vector", "tensor_tensor", yc.ap[0:Q, hs], psys.ap[0:Q, 0:Vd], yc.ap[0:Q, hs], ALU.add,
                        reads=[psys, yk], writes=[yk])
                B.done(psys)
                for i, qb in enumerate(qbs):
                    psS = B.bank(True)
                    lhs = kw.ap[0:Q, i * 128:(i + 1) * 128] if decay else kk.ap[0:Q, qb * 128:(qb + 1) * 128]
                    B.mm(psS.ap[:, 0:Vd], lhs, v_ap, True, True, [kw if decay else kk] + v_keys, [psS])
                    yield
                    if decay:
                        B.I("vector", "scalar_tensor_tensor", S.ap[:, i, hs], S.ap[:, i, hs], self._eend_ap(sm, h, Q), psS.ap[:, 0:Vd],
                            ALU.mult, ALU.add, reads=[("S", h), sm, psS], writes=[("S", h)])
                    else:
                        B.I("vector", "scalar_tensor_tensor", S.ap[:, i, hs], S.ap[:, i, hs], EE(qb, c, dr), psS.ap[:, 0:Vd],
                            ALU.mult, ALU.add, reads=[("S", h), psS, ("EE",)], writes=[("S", h)])
                    B.done(psS)
                yield
                for i, qb in enumerate(qbs):
                    B.I("gpsimd", "tensor_copy", Sb.ap[:, i, hs], S.ap[:, i, hs], reads=[("S", h)], writes=[("Sb", h)])

            for h0 in range(0, H, G):
                gens = [head_gen(h, h - h0) for h in range(h0, min(H, h0 + G))]
                while gens:
                    nxt = []
                    for g in gens:
                        try:
                            next(g)
                            nxt.append(g)
                        except StopIteration:
                            pass
                    gens = nxt
            if dr == 1:
                yf = yp.next()
                B.dma(yf.ap[0:Q], self.Y[t0:t0 + Q, 0:VT], [], [yf])
                B.I("gpsimd", "tensor_tensor", yc.ap[0:Q], yc.ap[0:Q], yf.ap[0:Q], ALU.add, reads=ykeys + [yf], writes=ykeys)
            B.dma(self.Y[t0:t0 + Q, 0:VT], yc.ap[0:Q], ykeys, [])
        B.barrier()
    B.release(m)


def _eend_ap(self, sm, h, Q):
    return sm.ap[:, 4, h:h + 1]


Builder._eend_ap = _eend_ap


def _transpose_blocks2(self, src_fn, nblk, dst, dt_in):
    B = self
    m = B.mark()
    sp = Pool(B, 2, [128, T], dt_in, "tsrc")
    op = Pool(B, 3, [128, 1024], BF16, "tdst")
    ident = self.ident_b if dt_in == BF16 else self.ident_f
    nper = 8 if dt_in == BF16 else 4
    for b in range(nblk):
        s = sp.next()
        B.dma(s.ap, src_fn(b), [], [s])
        for c0 in range(0, NT, nper):
            nt = min(nper, NT - c0)
            ps = B.bank()
            pv = ps.ap.bitcast(BF16) if dt_in == BF16 else ps.ap
            for i in range(nt):
                B.I("tensor", "transpose", pv[:, i * 128:(i + 1) * 128], s.ap[:, (c0 + i) * 128:(c0 + i + 1) * 128], ident.ap,
                    reads=[s, ident], writes=[ps])
            o = op.next()
            B.I("vector", "tensor_copy", o.ap[:, 0:nt * 128], pv[:, 0:nt * 128], reads=[ps], writes=[o])
            B.dma(dst[c0 * 128:(c0 + nt) * 128, b * 128:(b + 1) * 128].rearrange("(c t) f -> t c f", t=128),
                  o.ap[:, 0:nt * 128].rearrange("t (c f) -> t c f", f=128), [o], [])
    B.release(m)


def _ret_mixer(self, li, last):
    B = self
    _inproj_dump(self, self.d["ret_w_in"], 96)
    m = B.mark()
    cos = B.alloc([128, T], F32, "cos"); sin = B.alloc([128, T], F32, "sin")
    B.dma(cos.ap, self.d["rope_cos"], [], [cos])
    B.dma(sin.ap, self.d["rope_sin"], [], [sin])
    ap_ = Pool(B, 2, [128, T], F32, "ra"); bp_ = Pool(B, 2, [128, T], F32, "rb")
    t1p = Pool(B, 2, [128, T], F32, "t1"); t2p = Pool(B, 2, [128, T], F32, "t2")
    oap = Pool(B, 2, [128, T], BF16, "oa"); obp = Pool(B, 2, [128, T], BF16, "ob")
    for h in range(8):
        for base, dst, scale in ((0, self.QT, 1.0), (16, self.KT, 1.0 / 16.0)):
            a = ap_.next(); b = bp_.next(); t1 = t1p.next(); t2 = t2p.next(); oa = oap.next(); ob = obp.next()
            B.dma(a.ap, self.U[base + 2 * h], [], [a])
            B.dma(b.ap, self.U[base + 2 * h + 1], [], [b])
            B.I("vector", "tensor_tensor", t1.ap, a.ap, cos.ap, ALU.mult, reads=[a, cos], writes=[t1])
            B.I("gpsimd", "tensor_tensor", t2.ap, b.ap, sin.ap, ALU.mult, reads=[b, sin], writes=[t2])
            B.I("vector", "tensor_tensor", t1.ap, t1.ap, t2.ap, ALU.subtract, reads=[t1, t2], writes=[t1])
            B.I("scalar", "activation", oa.ap, t1.ap, AF.Copy, scale=scale, reads=[t1], writes=[oa])
            B.dma(dst[2 * h], oa.ap, [oa], [])
            B.I("vector", "tensor_tensor", t1.ap, b.ap, cos.ap, ALU.mult, reads=[b, cos, t1], writes=[t1])
            B.I("gpsimd", "tensor_tensor", t2.ap, a.ap, sin.ap, ALU.mult, reads=[a, sin, t2], writes=[t2])
            B.I("vector", "tensor_tensor", t1.ap, t1.ap, t2.ap, ALU.add, reads=[t1, t2], writes=[t1])
            B.I("scalar", "activation", ob.ap, t1.ap, AF.Copy, scale=scale, reads=[t1], writes=[ob])
            B.dma(dst[2 * h + 1], ob.ap, [ob], [])
    B.release(m)
    _transpose_blocks2(self, lambda b: self.KT[b], 16, self.KK, BF16)
    _transpose_blocks2(self, lambda b: self.U[32 + b], 32, self.VV, F32)
    _scan(self, 8, 2, lambda h: [2 * h, 2 * h + 1], 512, True, False, LA=self.d["ret_la"])
    m = B.mark()
    yp = Pool(B, 2, [128, 4096], F32, "ey")
    ynp = Pool(B, 2, [128, 4096], BF16, "eyn")
    gp = Pool(B, 2, [128, 32, 128], F32, "eg")
    op = Pool(B, 2, [128, 32, 128], BF16, "eo")
    stp = Pool(B, 2, [128, 4, 8], F32, "est")
    junk = B.alloc([128, 512], F32, "junk")
    for c in range(2 if last else 0, NT):
        t0 = c * 128
        y = yp.next(); yn = ynp.next(); g = gp.next(); o = op.next(); st = stp.next()
        B.dma(y.ap, self.Y[t0:t0 + 128, :], [], [y])
        B.dma(g.ap, self.U[64:96, :, t0:t0 + 128].rearrange("b p t -> p b t"), [], [g])
        B.I("scalar", "activation", g.ap, g.ap, AF.Silu, reads=[g], writes=[g])
        mu, nmu, var, rstd = [st.ap[:, i, :] for i in range(4)]
        B.I("vector", "tensor_reduce", mu, y.ap.rearrange("p (h d) -> p h d", h=8), AX.X, ALU.add, reads=[y], writes=[st])
        B.I("vector", "tensor_scalar", nmu, mu, -1.0 / 512, None, ALU.mult, reads=[st], writes=[st])
        for h in range(8):
            B.I("scalar", "activation", junk.ap, y.ap[:, h * 512:(h + 1) * 512], AF.Square, bias=nmu[:, h:h + 1],
                accum_out=var[:, h:h + 1], reads=[y, st, junk], writes=[junk, st])
        B.I("scalar", "activation", rstd, var, AF.Ln, bias=self.eps_t.ap, scale=1.0 / 512, reads=[st, self.eps_t], writes=[st])
        B.I("scalar", "activation", rstd, rstd, AF.Exp, scale=-0.5, reads=[st], writes=[st])
        for h in range(8):
            B.I("vector", "tensor_scalar", yn.ap[:, h * 512:(h + 1) * 512], y.ap[:, h * 512:(h + 1) * 512],
                nmu[:, h:h + 1], rstd[:, h:h + 1], ALU.add, ALU.mult, reads=[y, st, yn], writes=[yn])
        for b0 in range(0, 32, 8):
            ps = B.bank()
            pv = ps.ap.bitcast(BF16)
            for i in range(8):
                B.I("tensor", "transpose", pv[:, i * 128:(i + 1) * 128], yn.ap[:, (b0 + i) * 128:(b0 + i + 1) * 128],
                    self.ident_b.ap, reads=[yn, self.ident_b], writes=[ps])
            B.I("vector", "tensor_tensor", o.ap[:, b0:b0 + 8, :], pv[:, 0:1024].rearrange("p (b t) -> p b t", b=8),
                g.ap[:, b0:b0 + 8, :], ALU.mult, reads=[ps, g, o], writes=[o])
        B.dma(self.YT[:, t0:t0 + 128].rearrange("(b p) t -> p b t", p=128), o.ap, [o], [])
    B.release(m)
    return self.d["ret_w_out"], 2, self.YT


def _mixer(self, li, last):
    return [_ssd_mixer, _ret_mixer, _hgrn_mixer, _gdn_mixer][li](self, li, last)


def host_mixer_inputs(inputs, b, layers=(0, 1, 2, 3)):
    f = np.float32
    m = {}
    idx = np.arange(128)
    tri0 = (idx[:, None] <= idx[None, :]).astype(f)
    tri1 = (idx[:, None] >= idx[None, :]).astype(f)
    m["tri"] = np.stack([tri0, tri1])
    m["neg"] = np.stack([(1 - tri0) * NEGV, (1 - tri1) * NEGV]).astype(f)
    if 0 in layers:
        m["ssd_w_in"] = np.asarray(inputs["ssd_w_in"], f)[0]
        m["ssd_w_out"] = np.asarray(inputs["ssd_w_out"], f)[0]
        cw = np.asarray(inputs["ssd_conv_w"], f)[0]
        m["ssd_cw"] = np.ascontiguousarray(cw.reshape(3, 48, 128).transpose(2, 1, 0))
        m["ssd_cb"] = np.ascontiguousarray(np.asarray(inputs["ssd_conv_b"], f)[0].reshape(48, 128).T)
        m["ssd_par"] = np.ascontiguousarray(np.stack([np.asarray(inputs["ssd_dt_bias"], f)[0].reshape(128),
                                                      np.asarray(inputs["ssd_a_log"], f)[0].reshape(128)], -1))
        m["ssd_dbc"] = np.ascontiguousarray(np.broadcast_to(np.asarray(inputs["ssd_d"], f)[0][None, :], (128, 64)))
        m["ssd_ng"] = np.ascontiguousarray(np.asarray(inputs["ssd_norm_g"], f)[0].reshape(32, 128).T)
    if 3 in layers:
        m["gdn_w_in"] = np.asarray(inputs["gdn_w_in"], f)[0]
        m["gdn_w_out"] = np.asarray(inputs["gdn_w_out"], f)[0]
        cw = np.asarray(inputs["gdn_conv_w"], f)[0]
        m["gdn_cw"] = np.ascontiguousarray(cw.reshape(3, 64, 128).transpose(2, 1, 0))
        par = np.zeros((128, 3), f)
        par[64:, 0] = np.asarray(inputs["gdn_dt_bias"], f)[0].reshape(64)
        par[64:, 1] = np.asarray(inputs["gdn_a_log"], f)[0].reshape(64)
        par[64:, 2] = -1.0
        m["gdn_par"] = par
        m["gdn_ng"] = np.ascontiguousarray(np.broadcast_to(np.asarray(inputs["gdn_norm_g"], f)[0][:, None], (128, 32)))
        st0 = (idx[None, :] < idx[:, None]).astype(f)
        st1 = (idx[None, :] > idx[:, None]).astype(f)
        bd = lambda n: (idx[:, None] // n == idx[None, :] // n).astype(f)
        m["bmask"] = np.stack([bd(16), bd(32) - bd(16), bd(64) - bd(32), 1 - bd(64)]).astype(f)
        m["pos"] = np.stack([(1 - st0) * (-NEGV), (1 - st1) * (-NEGV)]).astype(f)
    if 2 in layers:
        m["hgrn_w_in"] = np.asarray(inputs["hgrn_w_in"], f)[0]
        m["hgrn_w_out"] = np.asarray(inputs["hgrn_w_out"], f)[0]
        m["hgrn_lbl"] = np.ascontiguousarray(np.asarray(inputs["hgrn_lb_logits"], f).reshape(4, 16, 128).transpose(2, 1, 0))
        m["hgrn_ng"] = np.ascontiguousarray(np.asarray(inputs["hgrn_norm_g"], f)[0].reshape(16, 128).T)
    if 1 in layers:
        w = np.asarray(inputs["ret_w_in"], f)[0]
        perm = []
        for h in range(8):
            base = h * 256
            perm += list(range(base, base + 64)) + list(range(base + 128, base + 192))
            perm += list(range(base + 64, base + 128)) + list(range(base + 192, base + 256))
        perm = np.array(perm)
        cols = np.concatenate([perm, 2048 + perm, np.arange(4096, 12288)])
        m["ret_w_in"] = np.ascontiguousarray(w[:, cols])
        m["ret_w_out"] = np.asarray(inputs["ret_w_out"], f)[0]
        ld = np.asarray(inputs["ret_log_decay"], f)[0]
        m["ret_la"] = np.ascontiguousarray(np.broadcast_to(ld[:, None, :], (2, T, 8)))
        n = np.arange(T - LC)
        row = (n // 64).astype(f); col = (n % 64).astype(f)
        inv = (10000.0 ** (-np.arange(0, 128, 2, dtype=f) / 128)).astype(f)
        ang = np.concatenate([row[None, :] * inv[:, None], col[None, :] * inv[:, None]], 0)
        cos = np.ones((128, T), f); sin = np.zeros((128, T), f)
        cos[:, LC:] = np.cos(ang); sin[:, LC:] = np.sin(ang)
        m["rope_cos"] = cos; m["rope_sin"] = sin
    return m


def _conv_blocks(self, blocks, cw, cb, sink, silu=True):
    B = self
    m = B.mark()
    up = Pool(B, 2, [128, T], F32, "cu")
    op = Pool(B, 2, [128, T], F32, "co")
    for i, blk in enumerate(blocks):
        u = up.next(); o = op.next()
        B.dma(u.ap, self.U[blk], [], [u])
        bias = cb.ap[:, i:i + 1] if cb is not None else self.zero_t.ap
        B.I("scalar", "activation", o.ap, u.ap, AF.Identity, bias=bias, scale=cw.ap[:, i, 1:2],
            reads=[u, cw] + ([cb] if cb is not None else [self.zero_t]), writes=[o])
        for lo, hi in ((0, LC), (LC, T)):
            B.I("vector", "scalar_tensor_tensor", o.ap[:, lo + 1:hi], u.ap[:, lo:hi - 1], cw.ap[:, i, 0:1], o.ap[:, lo + 1:hi],
                ALU.mult, ALU.add, reads=[u, cw, o], writes=[o])
            B.I("vector", "scalar_tensor_tensor", o.ap[:, lo:hi - 1], u.ap[:, lo + 1:hi], cw.ap[:, i, 2:3], o.ap[:, lo:hi - 1],
                ALU.mult, ALU.add, reads=[u, cw, o], writes=[o])
        if silu:
            B.I("scalar", "activation", o.ap, o.ap, AF.Silu, reads=[o], writes=[o])
        sink(i, blk, o)
    B.release(m)


def _ssd_mixer(self, li, last):
    B = self
    _inproj_dump(self, self.d["ssd_w_in"], 81)
    m0 = B.mark()
    cw = B.alloc([128, 48, 3], F32, "cw"); cb = B.alloc([128, 48], F32, "cb")
    B.dma(cw.ap, self.d["ssd_cw"], [], [cw]); B.dma(cb.ap, self.d["ssd_cb"], [], [cb])
    obp = Pool(B, 2, [128, T], BF16, "cob")
    self.XS = self.P.dram("XS", [32, 128, T], BF16) if not hasattr(self, "XS") else self.XS

    def sink(i, blk, o):
        ob = obp.next()
        B.I("gpsimd", "tensor_copy", ob.ap, o.ap, reads=[o], writes=[ob])
        if blk < 64:
            B.dma(self.XS[blk - 32], ob.ap, [ob], [])
        elif blk < 72:
            B.dma(self.KT[blk - 64], ob.ap, [ob], [])
        else:
            B.dma(self.QT[blk - 72], ob.ap, [ob], [])
    _conv_blocks(self, list(range(32, 80)), cw, cb, sink)
    m = B.mark()
    par = B.alloc([128, 4], F32, "par")
    B.dma(par.ap[:, 0:2], self.d["ssd_par"], [], [par])
    B.I("scalar", "activation", par.ap[:, 2:3], par.ap[:, 1:2], AF.Exp, reads=[par], writes=[par])
    B.I("vector", "tensor_scalar", par.ap[:, 2:3], par.ap[:, 2:3], -1.0, None, ALU.mult, reads=[par], writes=[par])
    B.I("vector", "memset", par.ap[:, 3:4], 1.0, reads=[par], writes=[par])
    u = B.alloc([128, T], F32, "dtu"); la = B.alloc([128, T], F32, "dtla")
    B.dma(u.ap, self.U[80], [], [u])
    B.I("scalar", "activation", u.ap, u.ap, AF.Exp, bias=par.ap[:, 0:1], reads=[u, par], writes=[u])
    B.I("scalar", "activation", u.ap, u.ap, AF.Ln, bias=par.ap[:, 3:4], reads=[u, par], writes=[u])
    B.I("vector", "tensor_scalar", la.ap, u.ap, par.ap[:, 2:3], None, ALU.mult, reads=[u, par], writes=[la])
    B.I("scalar", "activation", u.ap, u.ap, AF.Ln, reads=[u], writes=[u])
    op = Pool(B, 2, [128, 4, 128], F32, "dto")
    for src, dst in ((la, self.LA), (u, self.WT)):
        for c0 in range(0, NT, 4):
            nt = min(4, NT - c0)
            ps = B.bank()
            for i in range(nt):
                B.I("tensor", "transpose", ps.ap[:, i * 128:(i + 1) * 128], src.ap[:, (c0 + i) * 128:(c0 + i + 1) * 128],
                    self.ident_f.ap, reads=[src, self.ident_f], writes=[ps])
            o = op.next()
            B.I("vector", "tensor_copy", o.ap[:, 0:nt, :], ps.ap[:, 0:nt * 128].rearrange("p (c f) -> p c f", f=128), reads=[ps], writes=[o])
            for dr in range(2):
                B.dma(dst[dr, c0 * 128:(c0 + nt) * 128, 0:64].rearrange("(c t) h -> t c h", t=128),
                      o.ap[:, 0:nt, dr * 64:(dr + 1) * 64], [o], [])
    B.release(m)
    B.release(m0)
    _transpose_blocks2(self, lambda b: self.KT[b], 8, self.KK, BF16)
    _transpose_blocks2(self, lambda b: self.XS[b], 32, self.VV, BF16)
    _scan(self, 64, 1, lambda h: [h // 8], 64, True, True)
    m = B.mark()
    dsk = B.alloc([128, 64], F32, "dsk")
    B.dma(dsk.ap, self.d["ssd_dbc"], [], [dsk])
    ng = B.alloc([128, 32], F32, "sng")
    B.dma(ng.ap, self.d["ssd_ng"], [], [ng])
    yp = Pool(B, 2, [128, 4096], F32, "ey")
    xp = Pool(B, 2, [128, 4096], BF16, "ex")
    zp = Pool(B, 2, [128, 32, 128], F32, "ez")
    gp = Pool(B, 1, [128, 32, 128], F32, "eg")
    sqp = Pool(B, 1, [128, 32, 128], F32, "esq")
    rsp = Pool(B, 2, [128, 8, 128], F32, "ers")
    op = Pool(B, 2, [128, 32, 128], BF16, "eo")
    for c in range(2 if last else 0, NT):
        t0 = c * 128
        y = yp.next(); xs = xp.next(); z = zp.next(); g = gp.next(); sq = sqp.next(); rs = rsp.next(); o = op.next()
        B.dma(y.ap, self.Y[t0:t0 + 128, :], [], [y])
        B.dma(xs.ap, self.VV[t0:t0 + 128, :], [], [xs])
        B.dma(z.ap, self.U[0:32, :, t0:t0 + 128].rearrange("b p t -> p b t"), [], [z])
        B.I("scalar", "activation", z.ap, z.ap, AF.Silu, reads=[z], writes=[z])
        B.I("gpsimd", "tensor_tensor", sq.ap.rearrange("p b t -> p (b t)").rearrange("p (h d) -> p h d", h=64),
            xs.ap.rearrange("p (h d) -> p h d", h=64), dsk.ap.unsqueeze(2).to_broadcast([128, 64, 64]), ALU.mult,
            reads=[xs, dsk, sq], writes=[sq])
        B.I("vector", "tensor_tensor", y.ap, y.ap, sq.ap.rearrange("p b t -> p (b t)"), ALU.add, reads=[y, sq], writes=[y])
        for b0 in range(0, 32, 4):
            ps = B.bank()
            for i in range(4):
                B.I("tensor", "transpose", ps.ap[:, i * 128:(i + 1) * 128], y.ap[:, (b0 + i) * 128:(b0 + i + 1) * 128],
                    self.ident_f.ap, reads=[y, self.ident_f], writes=[ps])
            B.I("vector", "tensor_tensor", g.ap[:, b0:b0 + 4, :], ps.ap.rearrange("p (b t) -> p b t", b=4),
                z.ap[:, b0:b0 + 4, :], ALU.mult, reads=[ps, z, g], writes=[g])
        B.I("scalar", "activation", sq.ap, g.ap, AF.Square, reads=[g, sq], writes=[sq])
        for half in range(2):
            ps = B.bank()
            for gi in range(4):
                G = half * 4 + gi
                for b in range(4):
                    B.mm(ps.ap[:, gi * 128:(gi + 1) * 128], self.ones_f.ap, sq.ap[:, G * 4 + b, :], b == 0, b == 3,
                         [self.ones_f, sq], [ps])
            B.I("scalar", "activation", rs.ap[:, half * 4:half * 4 + 4, :], ps.ap.rearrange("p (g t) -> p g t", g=4), AF.Ln,
                bias=self.eps_t.ap, scale=1.0 / 512, reads=[ps, self.eps_t, rs], writes=[rs])
        B.I("scalar", "activation", rs.ap, rs.ap, AF.Exp, scale=-0.5, reads=[rs], writes=[rs])
        for b in range(32):
            B.I("vector", "scalar_tensor_tensor", o.ap[:, b, :], g.ap[:, b, :], ng.ap[:, b:b + 1], rs.ap[:, b // 4, :],
                ALU.mult, ALU.mult, reads=[g, ng, rs, o], writes=[o])
        B.dma(self.YT[:, t0:t0 + 128].rearrange("(b p) t -> p b t", p=128), o.ap, [o], [])
    B.release(m)
    return self.d["ssd_w_out"], 2, self.YT


def _hgrn_mixer(self, li, last):
    B = self
    P = self.P
    _inproj_dump(self, self.d["hgrn_w_in"], 80)
    if not hasattr(self, "QT2"):
        self.QT2 = P.dram("QT2", [16, 128, T], BF16); self.KT2 = P.dram("KT2", [16, 128, T], BF16)
        self.KK2 = P.dram("KK2", [T, 2048], BF16); self.KS = P.dram("KS", [2, 16, 128, T], BF16)
    m_ee = B.mark()
    ee = B.alloc([128, 2, 16, 36], F32, "ee")
    m = B.mark()
    lg = B.alloc([128, 16, 4], F32, "lg"); lbt = B.alloc([128, 4, 16], F32, "lbt")
    B.dma(lg.ap, self.d["hgrn_lbl"], [], [lg])
    B.I("scalar", "activation", lg.ap, lg.ap, AF.Exp, reads=[lg], writes=[lg])
    ssum, lb, oml, noml = [lbt.ap[:, i, :] for i in range(4)]
    B.I("vector", "tensor_reduce", ssum, lg.ap, AX.X, ALU.add, reads=[lg], writes=[lbt])
    B.I("vector", "reciprocal", ssum, ssum, reads=[lbt], writes=[lbt])
    B.I("vector", "tensor_tensor", lb, lg.ap[:, :, 1], lg.ap[:, :, 2], ALU.add, reads=[lg, lbt], writes=[lbt])
    B.I("vector", "tensor_tensor", lb, lb, ssum, ALU.mult, reads=[lbt], writes=[lbt])
    B.I("vector", "tensor_scalar", oml, lb, -1.0, 1.0, ALU.mult, ALU.add, reads=[lbt], writes=[lbt])
    B.I("vector", "tensor_scalar", noml, oml, -1.0, None, ALU.mult, reads=[lbt], writes=[lbt])
    qp = Pool(B, 2, [128, T], F32, "hq"); fp = Pool(B, 2, [128, T], F32, "hf")
    kgp = Pool(B, 2, [128, T], F32, "hkg"); lfp = Pool(B, 2, [128, T], F32, "hlf")
    cp = Pool(B, 2, [128, T], F32, "hc"); clp = Pool(B, 2, [128, T], F32, "hcl")
    ep = Pool(B, 2, [128, T], F32, "he"); obp = Pool(B, 3, [128, T], BF16, "hob")
    totp = Pool(B, 2, [128, 36], F32, "htot")
    v3 = lambda t: t.ap.rearrange("p (c q) -> p c q", q=64)
    for b in range(16):
        q = qp.next()
        B.dma(q.ap, self.U[b], [], [q])
        for dr in range(2):
            f = fp.next(); kg = kgp.next(); lf = lfp.next(); C = cp.next(); cl = clp.next(); e = ep.next(); tot = totp.next()
            B.dma(f.ap, self.U[16 + 16 * dr + b], [], [f])
            B.I("scalar", "activation", f.ap, f.ap, AF.Sigmoid, reads=[f], writes=[f])
            B.I("vector", "tensor_scalar", lf.ap, f.ap, oml[:, b:b + 1], lb[:, b:b + 1], ALU.mult, ALU.add, reads=[f, lbt], writes=[lf])
            B.I("scalar", "activation", lf.ap, lf.ap, AF.Ln, reads=[lf], writes=[lf])
            B.I("gpsimd", "tensor_scalar", kg.ap, f.ap, noml[:, b:b + 1], oml[:, b:b + 1], ALU.mult, ALU.add, reads=[f, lbt], writes=[kg])
            B.I("vector", "tensor_tensor_scan", C.ap, lf.ap, lf.ap, 0.0, ALU.add, ALU.bypass, reads=[lf], writes=[C])
            B.I("vector", "tensor_tensor", tot.ap, v3(C)[:, :, 63], v3(C)[:, :, 0], ALU.subtract, reads=[C], writes=[tot])
            B.I("vector", "tensor_tensor", tot.ap, tot.ap, v3(lf)[:, :, 0], ALU.add, reads=[tot, lf], writes=[tot])
            if dr == 0:
                B.I("vector", "tensor_tensor", cl.ap[:, 0:36], v3(C)[:, :, 0], v3(lf)[:, :, 0], ALU.subtract, reads=[C, lf], writes=[cl])
                B.I("vector", "tensor_tensor", v3(e), v3(C), cl.ap[:, 0:36].unsqueeze(2).to_broadcast([128, 36, 64]), ALU.subtract,
                    reads=[C, cl], writes=[e])
                B.I("vector", "tensor_copy", cl.ap, e.ap, reads=[e, cl], writes=[cl])
            else:
                B.I("vector", "tensor_tensor", v3(cl), v3(C)[:, :, 63:64].to_broadcast([128, 36, 64]), v3(C), ALU.subtract,
                    reads=[C], writes=[cl])
                B.I("vector", "tensor_tensor", cl.ap, cl.ap, lf.ap, ALU.add, reads=[cl, lf], writes=[cl])
            B.I("scalar", "activation", ee.ap[:, dr, b, :], tot.ap, AF.Exp, reads=[tot, ("EE",)], writes=[("EE",)])
            B.I("scalar", "activation", e.ap, cl.ap, AF.Exp, reads=[cl, e], writes=[e])
            ob = obp.next()
            B.I("gpsimd", "tensor_tensor", ob.ap, q.ap, e.ap, ALU.mult, reads=[q, e], writes=[ob])
            B.dma((self.QT, self.QT2)[dr][b], ob.ap, [ob], [])
            B.I("scalar", "activation", e.ap, cl.ap, AF.Exp, scale=-1.0, reads=[cl, e], writes=[e])
            ob = obp.next()
            B.I("gpsimd", "tensor_tensor", ob.ap, kg.ap, e.ap, ALU.mult, reads=[kg, e], writes=[ob])
            B.dma((self.KT, self.KT2)[dr][b], ob.ap, [ob], [])
            B.I("vector", "tensor_tensor", v3(e), tot.ap.unsqueeze(2).to_broadcast([128, 36, 64]), v3(cl), ALU.subtract,
                reads=[tot, cl, e], writes=[e])
            B.I("scalar", "activation", e.ap, e.ap, AF.Exp, reads=[e], writes=[e])
            ob = obp.next()
            B.I("vector", "tensor_tensor", ob.ap, kg.ap, e.ap, ALU.mult, reads=[kg, e], writes=[ob])
            B.dma(self.KS[dr, b], ob.ap, [ob], [])
    B.release(m)
    _transpose_blocks2(self, lambda b: self.KS[0, b], 16, self.KK, BF16)
    _transpose_blocks2(self, lambda b: self.KS[1, b], 16, self.KK2, BF16)
    _transpose_blocks2(self, lambda b: self.U[48 + b], 16, self.VV, F32)
    _scan(self, 16, 1, lambda h: [h], 128, False, False, Q=64, EE=lambda qb, c, dr: ee.ap[:, dr, qb, c:c + 1],
          QTs=[self.QT, self.QT2], KTs=[self.KT, self.KT2], KKs=[self.KK, self.KK2])
    m = B.mark()
    ng = B.alloc([128, 16], F32, "hng")
    B.dma(ng.ap, self.d["hgrn_ng"], [], [ng])
    yp = Pool(B, 2, [128, 2048], F32, "ey"); sqp = Pool(B, 2, [128, 2048], F32, "esq")
    ynp = Pool(B, 2, [128, 2048], BF16, "eyn")
    gp = Pool(B, 2, [128, 16, 128], F32, "eg"); op = Pool(B, 2, [128, 16, 128], BF16, "eo")
    stp = Pool(B, 2, [128, 16], F32, "est")
    for c in range(2 if last else 0, NT):
        t0 = c * 128
        y = yp.next(); sq = sqp.next(); yn = ynp.next(); g = gp.next(); o = op.next(); st = stp.next()
        B.dma(y.ap, self.Y[t0:t0 + 128, 0:2048], [], [y])
        B.dma(g.ap, self.U[64:80, :, t0:t0 + 128].rearrange("b p t -> p b t"), [], [g])
        B.I("scalar", "activation", g.ap, g.ap, AF.Silu, reads=[g], writes=[g])
        B.I("scalar", "activation", sq.ap, y.ap, AF.Square, reads=[y], writes=[sq])
        B.I("vector", "tensor_reduce", st.ap, sq.ap.rearrange("p (h d) -> p h d", h=16), AX.X, ALU.add, reads=[sq], writes=[st])
        B.I("scalar", "activation", st.ap, st.ap, AF.Ln, bias=self.eps_t.ap, scale=1.0 / 128, reads=[st, self.eps_t], writes=[st])
        B.I("scalar", "activation", st.ap, st.ap, AF.Exp, scale=-0.5, reads=[st], writes=[st])
        B.I("vector", "tensor_tensor", yn.ap.rearrange("p (h d) -> p h d", h=16), y.ap.rearrange("p (h d) -> p h d", h=16),
            st.ap.unsqueeze(2).to_broadcast([128, 16, 128]), ALU.mult, reads=[y, st], writes=[yn])
        for b0 in range(0, 16, 8):
            ps = B.bank()
            pv = ps.ap.bitcast(BF16)
            for i in range(8):
                B.I("tensor", "transpose", pv[:, i * 128:(i + 1) * 128], yn.ap[:, (b0 + i) * 128:(b0 + i + 1) * 128],
                    self.ident_b.ap, reads=[yn, self.ident_b], writes=[ps])
            for i in range(8):
                b = b0 + i
                B.I("vector", "scalar_tensor_tensor", o.ap[:, b, :], pv[:, i * 128:(i + 1) * 128], ng.ap[:, b:b + 1], g.ap[:, b, :],
                    ALU.mult, ALU.mult, reads=[ps, ng, g, o], writes=[o])
        B.dma(self.YT[0:2048, t0:t0 + 128].rearrange("(b p) t -> p b t", p=128), o.ap, [o], [])
    B.release(m)
    B.release(m_ee)
    return self.d["hgrn_w_out"], 1, self.YT


def _epi_headnorm(self, nblk, gblk0, ng, last):
    B = self
    W = nblk * 128
    m = B.mark()
    yp = Pool(B, 2, [128, W], F32, "ey"); sqp = Pool(B, 1, [128, W], F32, "esq")
    ynp = Pool(B, 2, [128, W], BF16, "eyn")
    gp = Pool(B, 2, [128, nblk, 128], F32, "eg"); op = Pool(B, 2, [128, nblk, 128], BF16, "eo")
    stp = Pool(B, 2, [128, nblk], F32, "est")
    for c in range(2 if last else 0, NT):
        t0 = c * 128
        y = yp.next(); sq = sqp.next(); yn = ynp.next(); g = gp.next(); o = op.next(); st = stp.next()
        B.dma(y.ap, self.Y[t0:t0 + 128, 0:W], [], [y])
        B.dma(g.ap, self.U[gblk0:gblk0 + nblk, :, t0:t0 + 128].rearrange("b p t -> p b t"), [], [g])
        B.I("scalar", "activation", g.ap, g.ap, AF.Silu, reads=[g], writes=[g])
        B.I("scalar", "activation", sq.ap, y.ap, AF.Square, reads=[y], writes=[sq])
        B.I("vector", "tensor_reduce", st.ap, sq.ap.rearrange("p (h d) -> p h d", h=nblk), AX.X, ALU.add, reads=[sq], writes=[st])
        B.I("scalar", "activation", st.ap, st.ap, AF.Ln, bias=self.eps_t.ap, scale=1.0 / 128, reads=[st, self.eps_t], writes=[st])
        B.I("scalar", "activation", st.ap, st.ap, AF.Exp, scale=-0.5, reads=[st], writes=[st])
        B.I("vector", "tensor_tensor", yn.ap.rearrange("p (h d) -> p h d", h=nblk), y.ap.rearrange("p (h d) -> p h d", h=nblk),
            st.ap.unsqueeze(2).to_broadcast([128, nblk, 128]), ALU.mult, reads=[y, st], writes=[yn])
        for b0 in range(0, nblk, 8):
            ps = B.bank()
            pv = ps.ap.bitcast(BF16)
            for i in range(8):
                B.I("tensor", "transpose", pv[:, i * 128:(i + 1) * 128], yn.ap[:, (b0 + i) * 128:(b0 + i + 1) * 128],
                    self.ident_b.ap, reads=[yn, self.ident_b], writes=[ps])
            for i in range(8):
                b = b0 + i
                B.I("vector", "scalar_tensor_tensor", o.ap[:, b, :], pv[:, i * 128:(i + 1) * 128], ng.ap[:, b:b + 1], g.ap[:, b, :],
                    ALU.mult, ALU.mult, reads=[ps, ng, g, o], writes=[o])
        B.dma(self.YT[0:W, t0:t0 + 128].rearrange("(b p) t -> p b t", p=128), o.ap, [o], [])
    B.release(m)


def _gdn_mixer(self, li, last):
    B = self
    P = self.P
    _inproj_dump(self, self.d["gdn_w_in"], 97)
    if not hasattr(self, "BT"):
        self.BT = P.dram("BT", [2, T, 32], F32)
        self.VF = P.dram("VF", [32, 128, T], BF16)
    m0 = B.mark()
    cw = B.alloc([128, 64, 3], F32, "gcw")
    B.dma(cw.ap, self.d["gdn_cw"], [], [cw])
    obp = Pool(B, 2, [128, T], BF16, "gob")
    sqp = Pool(B, 6, [128, 512], F32, "gsq")
    rsp = Pool(B, 6, [128, 512], F32, "grs")

    def sink(i, blk, o):
        ob = obp.next()
        if blk < 32:
            sqs = [sqp.next() for _ in TG6]; rss = [rsp.next() for _ in TG6]; pss = []
            for gi, (lo, hi) in enumerate(TG6):
                B.I("scalar", "activation", sqs[gi].ap[:, 0:hi - lo], o.ap[:, lo:hi], AF.Square, reads=[o], writes=[sqs[gi]])
            for gi, (lo, hi) in enumerate(TG6):
                ps = B.bank()
                B.mm(ps.ap[:, 0:hi - lo], self.ones_f.ap, sqs[gi].ap[:, 0:hi - lo], True, True, [self.ones_f, sqs[gi]], [ps])
                pss.append(ps)
            for gi, (lo, hi) in enumerate(TG6):
                n = hi - lo
                B.I("scalar", "activation", rss[gi].ap[:, 0:n], pss[gi].ap[:, 0:n], AF.Ln, bias=self.eps_t.ap,
                    reads=[pss[gi], self.eps_t], writes=[rss[gi]])
            for gi, (lo, hi) in enumerate(TG6):
                n = hi - lo
                B.I("scalar", "activation", rss[gi].ap[:, 0:n], rss[gi].ap[:, 0:n], AF.Exp, scale=-0.5, reads=[rss[gi]], writes=[rss[gi]])
            for gi, (lo, hi) in enumerate(TG6):
                n = hi - lo
                B.I("vector", "scalar_tensor_tensor", ob.ap[:, lo:hi], o.ap[:, lo:hi], (128.0 ** -0.5) if blk < 16 else 1.0, rss[gi].ap[:, 0:n],
                    ALU.mult, ALU.mult, reads=[o, rss[gi], ob], writes=[ob])
            B.dma((self.QT[blk] if blk < 16 else self.KT[blk - 16]), ob.ap, [ob], [])
        else:
            B.I("gpsimd", "tensor_copy", ob.ap, o.ap, reads=[o], writes=[ob])
            B.dma(self.VF[blk - 32], ob.ap, [ob], [])
    _conv_blocks(self, list(range(0, 64)), cw, None, sink)
    m = B.mark()
    par = B.alloc([128, 4], F32, "gpar")
    B.dma(par.ap[:, 0:3], self.d["gdn_par"], [], [par])
    B.I("scalar", "activation", par.ap[:, 1:2], par.ap[:, 1:2], AF.Exp, reads=[par], writes=[par])
    B.I("vector", "tensor_tensor", par.ap[:, 2:3], par.ap[:, 2:3], par.ap[:, 1:2], ALU.mult, reads=[par], writes=[par])
    B.I("vector", "memset", par.ap[:, 3:4], 1.0, reads=[par], writes=[par])
    u = B.alloc([128, T], F32, "gu"); sg = B.alloc([128, T], F32, "gsg")
    B.dma(u.ap, self.U[96], [], [u])
    B.I("scalar", "activation", sg.ap, u.ap, AF.Sigmoid, reads=[u], writes=[sg])
    B.I("scalar", "activation", u.ap, u.ap, AF.Exp, bias=par.ap[:, 0:1], reads=[u, par], writes=[u])
    B.I("scalar", "activation", u.ap, u.ap, AF.Ln, bias=par.ap[:, 3:4], reads=[u, par], writes=[u])
    B.I("vector", "tensor_scalar", u.ap, u.ap, par.ap[:, 2:3], None, ALU.mult, reads=[u, par], writes=[u])
    op = Pool(B, 2, [128, 4, 128], F32, "gdo")
    for src, dst, coff in ((u, self.LA, 64), (sg, self.BT, 0)):
        for c0 in range(0, NT, 4):
            nt = min(4, NT - c0)
            ps = B.bank()
            for i in range(nt):
                B.I("tensor", "transpose", ps.ap[:, i * 128:(i + 1) * 128], src.ap[:, (c0 + i) * 128:(c0 + i + 1) * 128],
                    self.ident_f.ap, reads=[src, self.ident_f], writes=[ps])
            o = op.next()
            B.I("vector", "tensor_copy", o.ap[:, 0:nt, :], ps.ap[:, 0:nt * 128].rearrange("p (c f) -> p c f", f=128), reads=[ps], writes=[o])
            for dr in range(2):
                B.dma(dst[dr, c0 * 128:(c0 + nt) * 128, 0:32].rearrange("(c t) h -> t c h", t=128),
                      o.ap[:, 0:nt, coff + dr * 32:coff + (dr + 1) * 32], [o], [])
    B.release(m)
    B.release(m0)
    _transpose_blocks2(self, lambda b: self.KT[b], 16, self.KK, BF16)
    _transpose_blocks2(self, lambda b: self.VF[b], 32, self.VV, BF16)
    mh = B.mark()
    pos = [B.alloc([128, 128], F32, "pos0"), B.alloc([128, 128], F32, "pos1")]
    for i in range(2):
        B.dma(pos[i].ap, self.d["pos"][i], [], [pos[i]])
    kkp = Pool(B, 4, [128, 128], F32, "gkk")
    Lp = Pool(B, 8, [128, 128], F32, "gL"); LTp = Pool(B, 8, [128, 128], F32, "gLT")
    Xp = Pool(B, 8, [128, 128], F32, "gX"); Ep = Pool(B, 8, [128, 128], F32, "gE")
    vnp = Pool(B, 8, [128, 128], BF16, "gvn")
    GH = 4
    Mps = [Pool(B, 28, [128, 128], F32, f"gM{i}") for i in range(GH)]
    bmask = [B.alloc([128, 128], F32, f"bm{i}") for i in range(4)]
    for i in range(4):
        B.dma(bmask[i].ap, self.d["bmask"][i], [], [bmask[i]])
    state = {}

    def vhook(h, slot, dr, qbs, kc, vc, la, sm, rh, Sb, hs):
        Mp = Mps[slot]
        g = qbs[0]
        cs, ec = sm.ap[:, 0, :], sm.ap[:, 2, :]
        beta = la.ap[:, 1, :]

        def msk(src, mi):
            t = Mp.next()
            B.I("gpsimd", "tensor_tensor", t.ap, src.ap, bmask[mi].ap, ALU.mult, reads=[src, bmask[mi]], writes=[t])
            return t

        def mmf(a, b):
            ps = B.bank(True)
            B.mm(ps.ap[:, 0:128], a.ap, b.ap, True, True, [a, b], [ps])
            return ps

        def ev_copy(ps):
            t = Mp.next()
            B.I("scalar", "copy", t.ap, ps.ap[:, 0:128], reads=[ps], writes=[t])
            B.done(ps)
            return t

        def ev_add(base, ps, op):
            t = Mp.next()
            B.I("vector", "tensor_tensor", t.ap, base.ap, ps.ap[:, 0:128], op, reads=[base, ps], writes=[t])
            B.done(ps)
            return t
        pk = None
        if state.get("g") != (g, id(kc)):
            state["g"] = (g, id(kc))
            pk = B.bank(True)
            B.mm(pk.ap[:, 0:128], kc.ap[:, g, :], kc.ap[:, g, :], True, True, [kc], [pk])
            state["kk"] = kkp.next()
        kk = state["kk"]
        p2 = B.bank(True)
        B.mm(p2.ap[:, 0:128], self.ones_f.ap, rh.ap, True, False, [self.ones_f, rh], [p2])
        B.mm(p2.ap[:, 0:128], self.ident_f.ap, pos[dr].ap, False, True, [self.ident_f, pos[dr]], [p2])
        yield
        if pk is not None:
            B.I("scalar", "copy", kk.ap, pk.ap[:, 0:128], reads=[pk], writes=[kk])
            B.done(pk)
        E = Ep.next()
        B.I("scalar", "activation", E.ap, p2.ap[:, 0:128], AF.Exp, bias=cs[:, h:h + 1], scale=-1.0, reads=[p2, sm], writes=[E])
        B.done(p2)
        pk2 = B.bank(True)
        B.mm(pk2.ap[:, 0:128], kc.ap[:, g, :], Sb.ap[:, 0, hs], True, True, [kc, ("Sb", h)], [pk2])
        yield
        X = Xp.next()
        B.I("vector", "tensor_scalar", X.ap, pk2.ap[:, 0:128], ec[:, h:h + 1], -1.0, ALU.mult, ALU.mult, reads=[pk2, sm], writes=[X])
        B.done(pk2)
        B.I("vector", "tensor_tensor", X.ap, X.ap, vc.ap[:, hs], ALU.add, reads=[X, vc], writes=[X])
        B.I("vector", "tensor_scalar", X.ap, X.ap, beta[:, h:h + 1], None, ALU.mult, reads=[X, la], writes=[X])
        yield
        L = Lp.next()
        B.I("vector", "scalar_tensor_tensor", L.ap, E.ap, beta[:, h:h + 1], kk.ap, ALU.mult, ALU.mult, reads=[E, la, kk], writes=[L])
        yield
        pt = B.bank(True)
        B.I("tensor", "transpose", pt.ap[:, 0:128], L.ap, self.ident_f.ap, reads=[L, self.ident_f], writes=[pt])
        Dm = msk(L, 0)
        Cs = [None, msk(L, 1), msk(L, 2), msk(L, 3)]
        yield
        LT = LTp.next()
        B.I("scalar", "copy", LT.ap, pt.ap[:, 0:128], reads=[pt], writes=[LT])
        B.done(pt)
        N = Mp.next()
        B.I("vector", "tensor_tensor", N.ap, self.ident_f.ap, Dm.ap, ALU.subtract, reads=[self.ident_f, Dm], writes=[N])
        yield
        DTm = msk(LT, 0)
        CTs = [None, msk(LT, 1), msk(LT, 2), msk(LT, 3)]
        yield
        NT = Mp.next()
        B.I("vector", "tensor_tensor", NT.ap, self.ident_f.ap, DTm.ap, ALU.subtract, reads=[self.ident_f, DTm], writes=[NT])
        Pm, PT = Dm, DTm
        for lev in range(3):
            q1 = mmf(PT, Pm)
            q2 = mmf(Pm, PT) if lev < 2 else None
            yield
            P2 = ev_copy(q1)
            P2T = ev_copy(q2) if lev < 2 else None
            yield
            a = mmf(NT, P2); b = mmf(P2, NT)
            yield
            N2 = ev_add(N, a, ALU.add); NT2 = ev_add(NT, b, ALU.add)
            N, NT = N2, NT2
            Pm, PT = P2, P2T
            yield
        Mm, MT = N, NT
        for lev in range(1, 4):
            C, CT = Cs[lev], CTs[lev]
            qa = mmf(CT, Mm) if lev < 3 else None
            qb_ = mmf(C, MT)
            yield
            Q1 = ev_copy(qa) if lev < 3 else None
            R1 = ev_copy(qb_)
            yield
            qc_ = mmf(MT, Q1) if lev < 3 else None
            qd = mmf(Mm, R1)
            yield
            if lev < 3:
                Mn = ev_add(Mm, qc_, ALU.subtract)
            MTn = ev_add(MT, qd, ALU.subtract)
            if lev < 3:
                Mm = Mn
            MT = MTn
            yield
        px = B.bank(True)
        B.mm(px.ap[:, 0:128], MT.ap, X.ap, True, True, [MT, X], [px])
        yield
        vn = vnp.next()
        B.I("scalar", "copy", vn.ap, px.ap[:, 0:128], reads=[px], writes=[vn])
        B.done(px)
        yield
        return vn.ap, [vn]
    _scan(self, 32, 1, lambda h: [h // 2], 128, True, False, aux=self.BT, vhook=(vhook if getattr(self, "use_hook", True) else None), G=GH)
    B.release(mh)
    m = B.mark()
    ng = B.alloc([128, 32], F32, "gng")
    B.dma(ng.ap, self.d["gdn_ng"], [], [ng])
    _epi_headnorm(self, 32, 64, ng, last)
    B.release(m)
    return self.d["gdn_w_out"], 2, self.YT


def build_program(nlayers=4, stub=False, debug_xt=False, layers=None):
    B = Builder()
    B.dbg_mlp = debug_xt
    _tok_init(B, nlayers)
    layers = list(range(nlayers)) if layers is None else layers
    if not stub:
        _mix_init(B, layers)
    for li in layers:
        last = (li == 3)
        _mod(B, li)
        B.barrier()
        if not stub:
            _set_gain(B, 0)
            B.barrier()
            W, nkq, YT = _mixer(B, li, last)
            B.barrier()
            _outproj(B, li, W, nkq, YT, last)
        _mlp(B, li, last)
        if debug_xt:
            dl = B.P.dram(f"dbgL{li}", [D, T], F32, kind="ExternalOutput")
            B.barrier()
            for k in range(4):
                B.dma(dl[k * 512:(k + 1) * 512, :], B.XT[k * 512:(k + 1) * 512, :], [], [("dbgL", li, k)])
            B.dbgl_keys = getattr(B, "dbgl_keys", []) + [("dbgL", li, k) for k in range(4)]
            B.barrier()
    if debug_xt:
        dm = B.P.dram("dbg_mod", [128, 96, 2], F32, kind="ExternalOutput")
        B.dma(dm, B.modT.ap, [B.modT], [("dbgm",)])
        B.dbg = B.P.dram("dbgXT", [D, T], F32, kind="ExternalOutput")
        for k in range(4):
            B.dma(B.dbg[k * 512:(k + 1) * 512, :], B.XT[k * 512:(k + 1) * 512, :], [], [("dbg", k)])
        B.out_keys = [("dbg", k) for k in range(4)] + [("dbgm",), ("dbgh2",), ("dbghid",)] + getattr(B, "dbgl_keys", [])
    else:
        _final(B)
    nc = B.P.emit(final_keys=B.out_keys)
    return B, nc


def host_inputs(inputs, b, nl=4):
    f = np.float32
    x = np.asarray(inputs["x"], f); ctx = np.asarray(inputs["ctx"], f)
    m = {}
    m["xT0"] = np.ascontiguousarray(np.concatenate([ctx[b], x[b]], axis=0).T)
    cv = np.stack([np.asarray(inputs["c"], f)[b], np.asarray(inputs["c_ctx"], f)], axis=-1)
    m["cvec"] = np.ascontiguousarray(cv.reshape(16, 128, 2).transpose(1, 0, 2))
    m["ada_w"] = np.asarray(inputs["ada_w"], f)[:nl]
    m["ada_bT"] = np.ascontiguousarray(np.asarray(inputs["ada_b"], f).reshape(4, 96, 128).transpose(0, 2, 1))[:nl]
    m["norm_gT"] = np.ascontiguousarray(np.asarray(inputs["norm_g"], f).reshape(4, 2, 16, 128).transpose(0, 1, 3, 2))[:nl]
    m["mlp_w1"] = np.asarray(inputs["mlp_w1"], f)[:nl]
    m["mlp_w2"] = np.asarray(inputs["mlp_w2"], f)[:nl]
    m["final_gT"] = np.ascontiguousarray(np.asarray(inputs["final_g"], f).reshape(16, 128).T)
    m["ident_f"] = np.eye(128, dtype=f)
    m["ones_f"] = np.ones((128, 128), f)
    return m


_CACHE = {}


def kernel(**inputs):
    if "prog" not in _CACHE:
        _CACHE["prog"] = build_program()
    B, nc = _CACHE["prog"]
    in_maps = []
    for b in range(4):
        m = host_inputs(inputs, b)
        m.update(host_mixer_inputs(inputs, b))
        in_maps.append(m)
    res = run_bass_kernel_spmd(nc, in_maps, core_ids=list(range(4)))
    out = np.stack([np.ascontiguousarray(res.results[b]["outT"].T) for b in range(4)], axis=0)
    return out.astype(np.float32)
```
